# Optimizing a Trainium2 kernel written in Bass

```python
import jax, jax.numpy as jnp
from jax import lax
import numpy as np

D_MODEL = 1024
BATCH = 8
SEQ = 2048
DEPTH = 2

A_DILATION_PAIRS = ((128, 1), (512, 4), (2048, 16))
A_GROUPS = len(A_DILATION_PAIRS)
A_HEADS = 4
A_HEAD_DIM = 128
A_QK_WIDTH = A_GROUPS * A_HEADS * A_HEAD_DIM
A_OUT_WIDTH = A_HEADS * A_HEAD_DIM
A_BLOCK = 128
ROPE_THETA = 10000.0
NEG_INF = -1e30
B_WIDTH = 512
B_KERNEL = 31
C_WIDTH = D_MODEL
C_KERNEL = 3
FFN_HIDDEN = -(-8 * D_MODEL // (3 * 256)) * 256
PLE_DIM = 256
N_EVEN = (DEPTH + 1) // 2
N_ODD = DEPTH // 2
DN_ALPHA = float((2 * DEPTH) ** 0.25)
DN_BETA = float((8 * DEPTH) ** -0.25)
LN_EPS = 1e-5
EVEN_IN_WIDTH = 3 * A_QK_WIDTH + 2 * B_WIDTH
EVEN_OUT_WIDTH = A_OUT_WIDTH + B_WIDTH

kernel_name = "hybrid_dilated_attn_conformer_shortconv_deepnorm"


def layer_norm(x, g, b):
    xf = x.astype(jnp.float32)
    mu = jnp.mean(xf, axis=-1, keepdims=True)
    var = jnp.mean(jnp.square(xf - mu), axis=-1, keepdims=True)
    return ((xf - mu) * lax.rsqrt(var + LN_EPS) * g.astype(jnp.float32) + b.astype(jnp.float32)).astype(x.dtype)


def rope(t, pos):
    half = t.shape[-1] // 2
    inv = ROPE_THETA ** (-jnp.arange(half, dtype=jnp.float32) / half)
    ang = pos.astype(jnp.float32)[:, None] * inv[None, :]
    bshape = (1, t.shape[1]) + (1,) * (t.ndim - 3) + (half,)
    cos = jnp.cos(ang).reshape(bshape)
    sin = jnp.sin(ang).reshape(bshape)
    tf = t.astype(jnp.float32)
    t1, t2 = tf[..., :half], tf[..., half:]
    return jnp.concatenate([t1 * cos - t2 * sin, t1 * sin + t2 * cos], axis=-1).astype(t.dtype)


def causal_depthwise_conv(x, w):
    k, c = w.shape
    return lax.conv_general_dilated(
        x, w[:, None, :].astype(x.dtype), window_strides=(1,), padding=[(k - 1, 0)],
        dimension_numbers=("NWC", "WIO", "NWC"), feature_group_count=c)


def to_strided_blocks(t, dil, n_blocks):
    b, s, h, e = t.shape
    length = s // dil
    t = t.reshape(b, length, dil, h, e).transpose(0, 2, 1, 3, 4)
    t = jnp.pad(t, ((0, 0), (0, 0), (0, n_blocks * A_BLOCK - length), (0, 0), (0, 0)))
    return t.reshape(b, dil, n_blocks, A_BLOCK, h, e)


def from_strided_blocks(t, s):
    b, dil, nb, qb = t.shape[:4]
    rest = t.shape[4:]
    t = t.reshape((b, dil, nb * qb) + rest)[:, :, : s // dil]
    t = jnp.moveaxis(t, 1, 2)
    return t.reshape((b, s) + rest)


def dilated_window_branch(q, k, v, window, dil):
    b, s, h, e = q.shape
    sub_w = window // dil
    nb = -(-(s // dil) // A_BLOCK)
    qb = to_strided_blocks(q, dil, nb).astype(jnp.float32)
    kb = to_strided_blocks(k, dil, nb).astype(jnp.float32)
    vb = to_strided_blocks(v, dil, nb).astype(jnp.float32)

    def with_prev(t):
        prev = jnp.pad(t[:, :, :-1], ((0, 0), (0, 0), (1, 0), (0, 0), (0, 0), (0, 0)))
        return jnp.concatenate([prev, t], axis=3)

    kk, vv = with_prev(kb), with_prev(vb)
    sc = jnp.einsum("brnqhe,brnkhe->brnhqk", qb, kk) * (e ** -0.5)
    qi = jnp.arange(A_BLOCK)[:, None]
    kj = jnp.arange(2 * A_BLOCK)[None, :]
    dist = qi + A_BLOCK - kj
    band = (dist >= 0) & (dist <= sub_w)
    valid_prev = (jnp.arange(nb) > 0)[:, None, None] | (kj >= A_BLOCK)[None]
    mask = band[None] & valid_prev
    sc = jnp.where(mask[None, None, :, None], sc, NEG_INF)
    m = jnp.max(sc, axis=-1, keepdims=True)
    ex = jnp.exp(sc - m)
    den = jnp.sum(ex, axis=-1)
    o = jnp.einsum("brnhqk,brnkhe->brnqhe", ex, vv) / jnp.moveaxis(den, -1, -2)[..., None]
    lse = m[..., 0] + jnp.log(den)
    o = from_strided_blocks(o, s)
    lse = from_strided_blocks(jnp.moveaxis(lse, -1, -2), s)
    return o, lse


def dilated_attention_mixture(q, k, v):
    outs, lses = [], []
    for g, (window, dil) in enumerate(A_DILATION_PAIRS):
        o, l = dilated_window_branch(q[:, :, g], k[:, :, g], v[:, :, g], window, dil)
        outs.append(o)
        lses.append(l)
    wts = jax.nn.softmax(jnp.stack(lses), axis=0)
    return jnp.sum(wts[..., None] * jnp.stack(outs), axis=0)


def conformer_conv(u, conv_w, conv_b, ln_g, ln_b):
    a, gate = jnp.split(u, 2, axis=-1)
    h = a * jax.nn.sigmoid(gate)
    h = causal_depthwise_conv(h, conv_w) + conv_b.astype(h.dtype)
    return jax.nn.silu(layer_norm(h, ln_g, ln_b))


def even_mixer(x, w_in, w_out, conv_w, conv_b, ln_g, ln_b, pos):
    b, s, _ = x.shape
    z = x @ w_in
    q, k, v, u = jnp.split(z, [A_QK_WIDTH, 2 * A_QK_WIDTH, 3 * A_QK_WIDTH], axis=-1)
    hshape = (b, s, A_GROUPS, A_HEADS, A_HEAD_DIM)
    q = rope(q.reshape(hshape), pos)
    k = rope(k.reshape(hshape), pos)
    attn = dilated_attention_mixture(q, k, v.reshape(hshape)).reshape(b, s, A_OUT_WIDTH).astype(x.dtype)
    conv = conformer_conv(u, conv_w, conv_b, ln_g, ln_b)
    return jnp.concatenate([attn, conv], axis=-1) @ w_out


def short_conv_mixer(x, w_in, conv_w, w_out):
    bg, cg, h = jnp.split(x @ w_in, 3, axis=-1)
    return (bg * causal_depthwise_conv(cg * h, conv_w)) @ w_out


def swiglu(x, w_in, w_out):
    gate, up = jnp.split(x @ w_in, 2, axis=-1)
    return (jax.nn.silu(gate) * up) @ w_out


def setup_inputs(seed: int = 0) -> dict:
    key = jax.random.key(seed)
    ks = jax.random.split(key, 19)
    d = D_MODEL

    def nrm(k, shape, scale):
        return jax.random.normal(k, shape, jnp.float32) * scale

    return {
        "x": nrm(ks[0], (BATCH, SEQ, d), 1.0),
        "p": nrm(ks[1], (DEPTH, BATCH, SEQ, PLE_DIM), 1.0),
        "even_w_in": nrm(ks[2], (N_EVEN, d, EVEN_IN_WIDTH), d ** -0.5),
        "even_w_out": nrm(ks[3], (N_EVEN, EVEN_OUT_WIDTH, d), EVEN_OUT_WIDTH ** -0.5 * DN_BETA),
        "conf_conv_w": nrm(ks[4], (N_EVEN, B_KERNEL, B_WIDTH), B_KERNEL ** -0.5),
        "conf_conv_b": nrm(ks[5], (N_EVEN, B_WIDTH), 0.02),
        "conf_ln_g": 1.0 + nrm(ks[6], (N_EVEN, B_WIDTH), 0.02),
        "conf_ln_b": nrm(ks[7], (N_EVEN, B_WIDTH), 0.02),
        "odd_w_in": nrm(ks[8], (N_ODD, d, 3 * C_WIDTH), d ** -0.5),
        "odd_conv_w": nrm(ks[9], (N_ODD, C_KERNEL, C_WIDTH), C_KERNEL ** -0.5),
        "odd_w_out": nrm(ks[10], (N_ODD, C_WIDTH, d), C_WIDTH ** -0.5 * DN_BETA),
        "ln_mix_g": 1.0 + nrm(ks[11], (DEPTH, d), 0.02),
        "ln_mix_b": nrm(ks[12], (DEPTH, d), 0.02),
        "ln_ffn_g": 1.0 + nrm(ks[13], (DEPTH, d), 0.02),
        "ln_ffn_b": nrm(ks[14], (DEPTH, d), 0.02),
        "ffn_w_in": nrm(ks[15], (DEPTH, d, 2 * FFN_HIDDEN), d ** -0.5),
        "ffn_w_out": nrm(ks[16], (DEPTH, FFN_HIDDEN, d), FFN_HIDDEN ** -0.5 * DN_BETA),
        "ple_w_proj": nrm(ks[17], (DEPTH, PLE_DIM, d), PLE_DIM ** -0.5),
        "ple_w_gate": nrm(ks[18], (DEPTH, d, d), d ** -0.5),
    }


def reference(x, p, even_w_in, even_w_out, conf_conv_w, conf_conv_b, conf_ln_g, conf_ln_b,
              odd_w_in, odd_conv_w, odd_w_out, ln_mix_g, ln_mix_b, ln_ffn_g, ln_ffn_b,
              ffn_w_in, ffn_w_out, ple_w_proj, ple_w_gate):
    pos = jnp.arange(x.shape[1], dtype=jnp.int32)
    for i in range(DEPTH):
        j = i // 2
        if i % 2 == 0:
            mix = even_mixer(x, even_w_in[j], even_w_out[j], conf_conv_w[j], conf_conv_b[j],
                             conf_ln_g[j], conf_ln_b[j], pos)
        else:
            mix = short_conv_mixer(x, odd_w_in[j], odd_conv_w[j], odd_w_out[j])
        x = layer_norm(DN_ALPHA * x + mix, ln_mix_g[i], ln_mix_b[i])
        x = layer_norm(DN_ALPHA * x + swiglu(x, ffn_w_in[i], ffn_w_out[i]), ln_ffn_g[i], ln_ffn_b[i])
        x = x + (p[i] @ ple_w_proj[i]) * jax.nn.sigmoid(x @ ple_w_gate[i])
    return x
```

```python
import os
import numpy as np
import concourse.bass as bass
import concourse.mybir as mybir
from concourse.bass_utils import run_bass_kernel_spmd

F32 = mybir.dt.float32
BF16 = mybir.dt.bfloat16
AF = mybir.ActivationFunctionType
ALU = mybir.AluOpType

T = 2048
D = 1024
NT = 16
FF = 2816
NF = 22
ALPHA = float(4 ** 0.25)
EPS = 1e-5
SCALE = float(128 ** -0.5)
DILS = (1, 4, 16)
K = 1024

BASE = 16512
R_A = BASE
R_B = R_A + 64 * K
R_C = R_B + 32 * K
R_C_SZ = 50 * K
R_W1 = R_C + R_C_SZ
RING = R_W1 + 16 * K
MISC = RING + 16 * K


class Res:
    ALL = []

    def __init__(self, name="", ranges=None):
        self.ws = {}
        self.rs = {}
        self.name = name
        self.ranges = ranges
        self.ov = [self]
        if ranges is not None:
            for o in Res.ALL:
                if o.ranges is None:
                    continue
                if any(a0 < b1 and b0 < a1 for (a0, a1) in ranges for (b0, b1) in o.ranges):
                    self.ov.append(o)
                    o.ov.append(self)
            Res.ALL.append(self)


def _merge(d, tok):
    if tok is None:
        return
    sem, val, key = tok
    if key not in d or d[key][1] < val:
        d[key] = tok


class DmaSem:
    def __init__(self, sem):
        self.sem = sem
        self.n = 0


class Sched:
    ENG = ("pe", "act", "dve", "pool", "sp")

    def __init__(self, nc):
        self.nc = nc
        self.q = {e: [] for e in self.ENG}
        self.sem = {e: nc.alloc_semaphore("sem_" + e) for e in self.ENG}
        self.cnt = {e: 0 for e in self.ENG}
        self.waited = {e: {} for e in self.ENG}
        self.nsem = 0
        self.ninst = {e: 0 for e in self.ENG}

    def dsem(self, name=None):
        self.nsem += 1
        return DmaSem(self.nc.alloc_semaphore(name or ("ds%d" % self.nsem)))

    def _wait(self, eng, toks):
        w = self.waited[eng]
        for d in toks:
            if d is None:
                continue
            sem, val, key = d
            if key == eng:
                pass
            if w.get(key, 0) >= val:
                continue
            w[key] = val
            self.q[eng].append(lambda e, sem=sem, val=val: e.wait_ge(sem, val))
            self.ninst[eng] += 1

    def _deps(self, reads, writes, rw, deps):
        toks = list(deps)
        for r in reads:
            for o in r.ov:
                toks.extend(o.ws.values())
        for r in writes:
            for o in r.ov:
                toks.extend(o.rs.values())
                if o is not r:
                    toks.extend(o.ws.values())
        for r in rw:
            for o in r.ov:
                toks.extend(o.ws.values())
                toks.extend(o.rs.values())
        return toks

    def _post(self, tok, reads, writes, rw):
        for r in reads:
            _merge(r.rs, tok)
        for r in writes:
            if r.rs:
                r.ws = {}
                r.rs = {}
            _merge(r.ws, tok)
        for r in rw:
            r.ws = {}
            r.rs = {}
            _merge(r.ws, tok)

    def op(self, eng, fn, reads=(), writes=(), rw=(), deps=()):
        self._wait(eng, self._deps(reads, writes, rw, deps))
        self.cnt[eng] += 1
        sem = self.sem[eng]
        self.q[eng].append(lambda e, fn=fn, sem=sem: fn(e).then_inc(sem, 1))
        self.ninst[eng] += 1
        tok = (sem, self.cnt[eng], eng)
        self._post(tok, reads, writes, rw)
        return tok

    def mm_nosig(self, fns, reads=(), writes=(), deps=()):
        eng = "pe"
        self._wait(eng, self._deps(reads, writes, (), deps))
        for fn in fns:
            self.q[eng].append(lambda e, fn=fn: fn(e))
        self.ninst[eng] += len(fns)

    def mm(self, fns, reads=(), writes=(), deps=()):
        eng = "pe"
        self._wait(eng, self._deps(reads, writes, (), deps))
        for fn in fns[:-1]:
            self.q[eng].append(lambda e, fn=fn: fn(e))
        self.ninst[eng] += len(fns)
        self.cnt[eng] += 1
        sem = self.sem[eng]
        fn = fns[-1]
        self.q[eng].append(lambda e, fn=fn, sem=sem: fn(e).then_inc(sem, 1))
        tok = (sem, self.cnt[eng], eng)
        self._post(tok, reads, writes, ())
        return tok

    def dma(self, eng, fns, ds, reads=(), writes=(), deps=()):
        toks = self._deps(reads, writes, (), deps)
        if ds.n > 0:
            toks.append((ds.sem, 16 * ds.n, id(ds)))
        self._wait(eng, toks)
        for fn in fns:
            self.q[eng].append(lambda e, fn=fn, sem=ds.sem: fn(e).then_inc(sem, 16))
        self.ninst[eng] += len(fns)
        ds.n += len(fns)
        tok = (ds.sem, 16 * ds.n, id(ds))
        self._post(tok, reads, writes, ())
        return tok

    def wait(self, eng, toks):
        self._wait(eng, toks)

    def emit(self):
        nc = self.nc
        q = self.q
        with nc.Block() as block:
            @block.tensor
            def _(e):
                for f in q["pe"]:
                    f(e)

            @block.scalar
            def _(e):
                for f in q["act"]:
                    f(e)

            @block.vector
            def _(e):
                for f in q["dve"]:
                    f(e)

            @block.gpsimd
            def _(e):
                for f in q["pool"]:
                    f(e)

            @block.sync
            def _(e):
                for f in q["sp"]:
                    f(e)


def build(stop_after=None, dbg=False):
    nc = bass.Bass("TRN2", target_bir_lowering=False)
    S = Sched(nc)
    Res.ALL = []

    def din(name, shape):
        return nc.dram_tensor(name, list(shape), F32, kind="ExternalInput").ap()

    x_d = din("x", [T, D])
    p_d = din("p", [2, T, 256])
    ewin_d = din("even_w_in", [D, 5632])
    ewout_d = din("even_w_out", [D, D])
    ccw_d = din("conf_conv_w", [128, 4, 31])
    ccb_d = din("conf_conv_b", [128, 4])
    clg_d = din("conf_ln_g", [128, 4])
    clb_d = din("conf_ln_b", [128, 4])
    owin_d = din("odd_w_in", [D, 3072])
    ocw_d = din("odd_conv_w", [128, 8, 3])
    owout_d = din("odd_w_out", [D, D])
    lmg_d = din("ln_mix_g", [2, D])
    lmb_d = din("ln_mix_b", [2, D])
    lfg_d = din("ln_ffn_g", [2, D])
    lfb_d = din("ln_ffn_b", [2, D])
    fwin_d = din("ffn_w_in", [2, D, 2 * FF])
    fwout_d = din("ffn_w_out", [2, FF, D])
    pwp_d = din("ple_w_proj", [2, 256, D])
    pwg_d = din("ple_w_gate", [2, D, D])
    ropec_d = din("rope_c", [128, T])
    ropes_d = din("rope_s", [128, T])
    out_d = nc.dram_tensor("out", [T, D], F32, kind="ExternalOutput").ap()
    dbg_d = {}

    def sb(name, shape, dt, off):
        n = int(np.prod(shape[1:])) * (4 if dt == F32 else 2)
        t_ = nc.alloc_sbuf_tensor_at(name, list(shape), dt, offset=off)
        return t_, Res(name, [(off, off + n)])

    x_tok = nc.alloc_sbuf_tensor_at("x_tok", [128, NT, D], F32, offset=R_A)
    r_xtok = [Res("xtok%d" % t, [(R_A + t * 4096, R_A + (t + 1) * 4096)]) for t in range(NT)]
    UD, r_UD = sb("UD", [128, 2, 2, T], F32, R_A)
    qT, r_qT = sb("qT", [128, 2, T], BF16, R_A + 32 * K)
    kT, r_kT = sb("kT", [128, 2, T], BF16, R_A + 40 * K)
    Vg, r_Vg = sb("Vg", [128, 16, 256], BF16, R_A + 48 * K)
    rtmp1, r_rtmp1, rtmp2, r_rtmp2 = [], [], [], []
    for i in range(2):
        a_, b_ = sb("rtmp1_%d" % i, [128, 512], F32, R_A + 56 * K + i * 2 * K)
        rtmp1.append(a_); r_rtmp1.append(b_)
        a_, b_ = sb("rtmp2_%d" % i, [128, 512], F32, R_A + 60 * K + i * 2 * K)
        rtmp2.append(a_); r_rtmp2.append(b_)
    xT = nc.alloc_sbuf_tensor_at("xT", [128, 8, T], BF16, offset=R_B)
    r_xT = [Res("xT%d" % c, [(R_B + d * 4096 + c * 1024, R_B + d * 4096 + (c + 1) * 1024) for d in range(8)])
            for c in range(4)]
    HP = 2080
    hTc, r_hTc = sb("hTc", [128, 4, HP], BF16, R_C)
    attnT, r_attnT = sb("attnT", [128, 4, T], BF16, R_C + 16640)
    convT, r_convT = sb("convT", [128, 4, T], BF16, R_C + 16640 + 16 * K)
    ropeC, r_ropeC = sb("ropeC", [128, T], F32, R_W1)
    ropeS, r_ropeS = sb("ropeS", [128, T], F32, R_W1 + 8 * K)
    r_convT4 = [Res("convT_tc%d" % c, [(R_C + 16640 + 16 * K + j * 4096 + c * 1024,
                                        R_C + 16640 + 16 * K + j * 4096 + (c + 1) * 1024) for j in range(4)])
                for c in range(4)]
    mt, r_mt = sb("mt", [128, 512], F32, R_C + 8 * K)
    vt, r_vt = sb("vt", [128, 512], F32, R_C + 10 * K)
    zt, r_zt = [], []
    for i in range(2):
        a_, b_ = sb("zt%d" % i, [128, 512], F32, R_C + 12 * K + i * 2 * K)
        zt.append(a_); r_zt.append(b_)
    hTf, r_hTf = sb("hTf", [128, 4, T], BF16, R_C)
    Wpg, r_Wpg = sb("Wpg", [128, 8, D], BF16, R_C + 16 * K)
    Wpp, r_Wpp = sb("Wpp", [128, 2, D], BF16, R_C + 32 * K)
    lnp2, r_lnp2 = sb("lnp2", [128, 2, D], F32, R_C + 36 * K)
    lnp1_L0, r_lnp1_L0 = sb("lnp1a", [128, 2, D], F32, R_C)
    sigt1, r_sigt1 = sb("sigt1", [128, D], F32, R_C + 44 * K)
    yT, r_yT = sb("yT", [128, 8, T], BF16, R_C)
    ub, r_ub = [], []
    for i in range(2):
        a_, b_ = sb("ub%d" % i, [128, 2 + T], F32, R_C + 32 * K + i * 8224)
        ub.append(a_); r_ub.append(b_)
    lnp1_L1, r_lnp1_L1 = sb("lnp1b", [128, 2, D], F32, R_C + 32 * K)
    assert 32 * K + 2 * 8224 <= R_C_SZ
    Wmix, r_Wmix = sb("Wmix", [128, 8, D], BF16, R_W1)
    Wfo, r_Wfo = [], []
    for i in range(2):
        a_, b_ = sb("Wfo%d" % i, [128, 4, D], BF16, R_W1 + i * 8 * K)
        Wfo.append(a_); r_Wfo.append(b_)
    castd, r_castd = [], []
    for i in range(2):
        a_, b_ = sb("castd%d" % i, [128, D], BF16, R_W1 + i * 2 * K)
        castd.append(a_); r_castd.append(b_)
    ring, r_ring = [], []
    for i in range(4):
        a_, b_ = sb("ring%d" % i, [128, 8, 256], BF16, RING + i * 4 * K)
        ring.append(a_); r_ring.append(b_)
    mo = [MISC]

    def misc(name, shape, dt):
        n = int(np.prod(shape[1:])) * (4 if dt == F32 else 2)
        n = (n + 31) // 32 * 32
        t_, r_ = sb(name, shape, dt, mo[0])
        mo[0] += n
        return t_, r_

    def misc2(name, shape, dt):
        ts, rs = [], []
        for i in range(2):
            a_, b_ = misc("%s%d" % (name, i), shape, dt)
            ts.append(a_); rs.append(b_)
        return ts, rs

    ident, r_ident = misc("ident", [128, 128], BF16)
    identf, r_identf = misc("identf", [128, 128], F32)
    ones, r_ones = misc("ones", [128, 128], BF16)
    masks, r_masks = misc("masks", [128, 2, 128], BF16)
    maskf, r_maskf = misc("maskf", [128, 2, 128], F32)
    ccw, r_ccw = misc("ccw", [128, 4, 31], F32)
    ccb, r_ccb = misc("ccb", [128, 4], F32)
    clg, r_clg = misc("clg", [128, 4], F32)
    clb, r_clb = misc("clb", [128, 4], F32)
    ocw, r_ocw = misc("ocw", [128, 8, 3], F32)
    r_prm = [r_ccw, r_ccb, r_clg, r_clb, r_ocw]
    off_dg0 = mo[0]
    sgt, r_sgt = misc2("sgt", [128, 512], F32)
    ctmp, r_ctmp = misc2("ctmp", [128, 512], F32)
    off_dg1 = mo[0]
    castt, r_castt = misc2("castt", [128, D], BF16)
    x2T, r_x2T = misc2("x2T", [128, 8, 128], BF16)
    assert off_dg1 - off_dg0 == 8 * K and mo[0] - off_dg1 == 8 * K
    stg_a, r_stg_a = [], []
    for i in range(2):
        a_, b_ = sb("stg%d" % i, [128, D], BF16, off_dg1 + 4 * K + i * 2 * K)
        stg_a.append(a_); r_stg_a.append(b_)
    ysq, r_ysq = sb("ysq", [128, 4, 512], BF16, off_dg0)
    dgb, r_dgb = [], []
    for i, o_ in enumerate((off_dg0, off_dg1)):
        a_, b_ = sb("dgb%d" % i, [128, 31, 128], BF16, o_)
        dgb.append(a_); r_dgb.append(b_)
    pbt, r_pbt = misc2("pbt", [128, 256], BF16)
    pTt, r_pTt = misc2("pTt", [128, 2, 128], BF16)
    sigt0, r_sigt0 = misc("sigt", [128, D], F32)
    PT, r_PT = misc2("PT", [128, 512], BF16)
    sigt = [sigt0, sigt1]
    r_sigt = [r_sigt0, r_sigt1]
    stt, r_st = misc2("stt", [128, 2, 6], F32)
    mvall, _r = misc("mvall", [128, NT, 4], F32)
    r_mvall = [Res("mvall%d" % t) for t in range(NT)]
    assert mo[0] <= 229344, mo[0]

    PD = [nc.alloc_psum_tensor("pd%d" % i, [128, 1024], F32) for i in range(3)]
    PTB = [nc.alloc_psum_tensor("ptb%d" % i, [128, 1024], BF16) for i in range(2)]
    pd_res = [[Res("pd%d_%d" % (i, h)) for h in range(2)] for i in range(3)]
    ptb_res = [Res("ptb%d" % i) for i in range(2)]
    psc = [0, 0]
    psn = [6]

    def ps_half():
        i = psc[0] % psn[0]
        psc[0] += 1
        return PD[i // 2][:, (i % 2) * 512:(i % 2) * 512 + 512], [pd_res[i // 2][i % 2]]

    def ps_full():
        if psc[0] % 2 == 1:
            psc[0] += 1
        i = psc[0] % psn[0]
        psc[0] += 2
        return PD[i // 2], pd_res[i // 2]

    def ps_t():
        i = psc[1] % 2
        psc[1] += 1
        return PTB[i], [ptb_res[i]]

    ds_ring = [S.dsem("ds_ring%d" % i) for i in range(4)]
    ds_w1 = [S.dsem("ds_w1_%d" % i) for i in range(2)]
    ds_x = [S.dsem("ds_x%d" % i) for i in range(4)]
    ds_p = [S.dsem("ds_p%d" % i) for i in range(2)]
    ds_misc = S.dsem("ds_misc")
    ds_ln = S.dsem("ds_ln")
    ds_ple = S.dsem("ds_ple")
    ds_out = [S.dsem("ds_out%d" % i) for i in range(4)]
    ds_dbg = S.dsem("ds_dbg")

    pieces = []
    for j in range(4):
        pieces.append([(ewin_d[:, 4608 + j * 128:4608 + (j + 1) * 128], 0),
                       (ewin_d[:, 5120 + j * 128:5120 + (j + 1) * 128], 128)])
    for hp in range(2):
        for g in range(3):
            c0 = g * 512 + hp * 256
            for base in (0, 1536, 3072):
                pieces.append([(ewin_d[:, base + c0:base + c0 + 256], 0)])
    for l in range(2):
        if l == 1:
            for jj in range(4):
                for base in (0, 1024, 2048):
                    pieces.append([(owin_d[:, base + jj * 256:base + (jj + 1) * 256], 0)])
        for f in range(NF):
            pieces.append([(fwin_d[l][:, f * 128:(f + 1) * 128], 0),
                           (fwin_d[l][:, FF + f * 128:FF + (f + 1) * 128], 128)])
    wq = {"k": 0, "issued": 0, "done": 0}

    def _wissue():
        while wq["issued"] < len(pieces) and wq["issued"] - 4 < wq["done"]:
            m = wq["issued"]
            i = m % 4
            fns = []
            for (src, co) in pieces[m]:
                n = src.shape[1]
                fns.append(lambda e, src=src, co=co, n=n, i=i: e.dma_start(
                    out=ring[i][:, :, co:co + n], in_=src.rearrange("(kc p) n -> p kc n", p=128)))
            S.dma("pool", fns, ds_ring[i], writes=[r_ring[i]])
            wq["issued"] += 1

    def wnext():
        k = wq["k"]
        wq["k"] += 1
        _wissue()
        assert wq["issued"] > k
        return ring[k % 4], r_ring[k % 4]

    def wdone():
        wq["done"] += 1
        _wissue()

    S.op("pool", lambda e: e.memset(identf[:], 1.0), writes=[r_identf])
    S.op("pool", lambda e: e.affine_select(out=identf[:], in_=identf[:], pattern=[[-1, 128]],
                                           compare_op=ALU.is_equal, fill=0.0, base=0, channel_multiplier=1),
         rw=[r_identf])
    S.op("pool", lambda e: e.memset(maskf[:], 1.0), writes=[r_maskf])
    S.op("pool", lambda e: e.affine_select(out=maskf[:, 0, :], in_=maskf[:, 0, :], pattern=[[1, 128]],
                                           compare_op=ALU.is_ge, fill=0.0, base=0, channel_multiplier=-1),
         rw=[r_maskf])
    S.op("pool", lambda e: e.affine_select(out=maskf[:, 1, :], in_=maskf[:, 1, :], pattern=[[-1, 128]],
                                           compare_op=ALU.is_ge, fill=0.0, base=0, channel_multiplier=1),
         rw=[r_maskf])
    S.op("dve", lambda e: e.tensor_copy(out=ident[:], in_=identf[:]), reads=[r_identf], writes=[r_ident])
    S.op("dve", lambda e: e.tensor_copy(out=masks[:], in_=maskf[:]), reads=[r_maskf], writes=[r_masks])
    S.op("dve", lambda e: e.memset(ones[:], 1.0), writes=[r_ones])
    S.op("dve", lambda e: e.memset(hTc[:, :, 0:32], 0.0), writes=[r_hTc])
    S.dma("sp", [
        lambda e: e.dma_start(out=ccw[:], in_=ccw_d),
        lambda e: e.dma_start(out=ccb[:], in_=ccb_d),
        lambda e: e.dma_start(out=clg[:], in_=clg_d),
        lambda e: e.dma_start(out=clb[:], in_=clb_d),
        lambda e: e.dma_start(out=ocw[:], in_=ocw_d),
    ], ds_misc, writes=r_prm)
    S.dma("sp", [lambda e: e.dma_start(out=ropeC[:], in_=ropec_d),
                 lambda e: e.dma_start(out=ropeS[:], in_=ropes_d)], ds_misc, writes=[r_ropeC, r_ropeS])

    def transpose_to(src_bf, r_src, dst_ap_fn, r_dst, nchunk, evac_eng="dve"):
        pt, rpt = ps_t()
        fns = [lambda e, c=c, pt=pt: e.transpose(out=pt[:, c * 128:(c + 1) * 128], in_=src_bf[:, c * 128:(c + 1) * 128],
                                                 identity=ident[:]) for c in range(nchunk)]
        S.mm(fns, reads=[r_src, r_ident], writes=rpt)
        src = pt[:, 0:nchunk * 128].rearrange("p (c n) -> p c n", c=nchunk)
        if evac_eng == "dve":
            return S.op("dve", lambda e: e.tensor_copy(out=dst_ap_fn(), in_=src), reads=rpt, writes=[r_dst])
        else:
            return S.op("act", lambda e: e.activation(out=dst_ap_fn(), in_=src, func=AF.Copy), reads=rpt, writes=[r_dst])

    def dbg_out(name, tensor_ap, shape, reads):
        d = nc.dram_tensor("dbg_" + name, list(shape), tensor_ap.dtype, kind="ExternalOutput").ap()
        dbg_d[name] = d
        S.dma("sp", [lambda e: e.dma_start(out=d, in_=tensor_ap)], ds_dbg, reads=reads)

    def finish():
        toks = []
        for ds in ds_out + [ds_dbg]:
            if ds.n:
                toks.append((ds.sem, 16 * ds.n, id(ds)))
        S.wait("sp", toks)
        S.emit()
        return nc

    def ln_stats(t, k2):
        st = stt[k2]
        S.op("dve", lambda e: e.bn_stats(out=st[:, 0, :], in_=x_tok[:, t, 0:512]), reads=[r_xtok[t]], writes=[r_st[k2]])
        S.op("dve", lambda e: e.bn_stats(out=st[:, 1, :], in_=x_tok[:, t, 512:1024]), reads=[r_xtok[t]], writes=[r_st[k2]])
        S.op("dve", lambda e: e.bn_aggr(out=mvall[:, t, 0:2], in_=st[:]), reads=[r_st[k2]], writes=[r_mvall[t]])
        S.op("dve", lambda e: e.tensor_scalar_mul(out=mvall[:, t, 3:4], in0=mvall[:, t, 0:1], scalar1=-1.0), rw=[r_mvall[t]])

    def ln_rstd_tile(t):
        S.op("act", lambda e: e.activation(out=mvall[:, t, 2:3], in_=mvall[:, t, 1:2], func=AF.Ln, bias=EPS), rw=[r_mvall[t]])
        S.op("act", lambda e: e.activation(out=mvall[:, t, 2:3], in_=mvall[:, t, 2:3], func=AF.Exp, scale=-0.5), rw=[r_mvall[t]])
        S.op("act", lambda e: e.activation(out=mvall[:, t, 3:4], in_=mvall[:, t, 3:4], func=AF.Copy, scale=mvall[:, t, 2:3]),
             rw=[r_mvall[t]])

    def ln_rstd_all():
        S.op("act", lambda e: e.activation(out=mvall[:, :, 2], in_=mvall[:, :, 1], func=AF.Sqrt, bias=EPS), rw=r_mvall)
        S.op("dve", lambda e: e.reciprocal(out=mvall[:, :, 2], in_=mvall[:, :, 2]), rw=r_mvall)
        S.op("dve", lambda e: e.tensor_tensor(out=mvall[:, :, 3], in0=mvall[:, :, 3], in1=mvall[:, :, 2], op=ALU.mult), rw=r_mvall)

    def ln_apply(t):
        xt = x_tok[:, t, :]
        S.op("act", lambda e: e.activation(out=xt, in_=xt, func=AF.Identity, scale=mvall[:, t, 2:3], bias=mvall[:, t, 3:4]),
             reads=[r_mvall[t]], rw=[r_xtok[t]])

    def ln_affine(t, lnp, r_lnp, eg="dve", eb="pool"):
        xt = x_tok[:, t, :]
        S.op(eg, lambda e: e.tensor_tensor(out=xt, in0=xt, in1=lnp[:, 0, :], op=ALU.mult),
             reads=[r_lnp], rw=[r_xtok[t]])
        S.op(eb, lambda e: e.tensor_tensor(out=xt, in0=xt, in1=lnp[:, 1, :], op=ALU.add),
             reads=[r_lnp], rw=[r_xtok[t]])

    def pipeline(stages, n, pre_step=None):
        ns = len(stages)
        for s_ in range(n + ns - 1):
            if pre_step is not None:
                pre_step(s_)
            for k_, f_ in enumerate(stages):
                t_ = s_ - k_
                if 0 <= t_ < n:
                    f_(t_)

    def cast_and_transpose(t, dst_fn, r_dst, k2, evac_eng="dve"):
        cb = castt[k2]
        S.op("act", lambda e: e.activation(out=cb[:], in_=x_tok[:, t, :], func=AF.Copy),
             reads=[r_xtok[t]], writes=[r_castt[k2]])
        transpose_to(cb, r_castt[k2], dst_fn, r_dst, 8, evac_eng)

    def load_lnp(lnp, r_lnp, g_d, b_d, l):
        S.dma("sp", [lambda e: e.dma_start(out=lnp[:, 0, :], in_=g_d[l:l + 1, :].broadcast_to([128, D])),
                     lambda e: e.dma_start(out=lnp[:, 1, :], in_=b_d[l:l + 1, :].broadcast_to([128, D]))],
              ds_ln, writes=[r_lnp])

    for t in range(NT):
        S.dma("sp", [lambda e, t=t: e.dma_start(out=x_tok[:, t, :], in_=x_d[t * 128:(t + 1) * 128, :])],
              ds_x[2 + (t % 2)], writes=[r_xtok[t]])
    stg = [(castt[0], r_castt[0]), (castt[1], r_castt[1]), (stg_a[0], r_stg_a[0]), (stg_a[1], r_stg_a[1])]

    def p0_cast(t):
        cb, r_cb = stg[t % 4]
        if t % 2 == 0:
            S.op("act", lambda e: e.activation(out=cb[:], in_=x_tok[:, t, :], func=AF.Copy),
                 reads=[r_xtok[t]], writes=[r_cb])
        else:
            S.op("dve", lambda e: e.tensor_copy(out=cb[:], in_=x_tok[:, t, :]), reads=[r_xtok[t]], writes=[r_cb])

    p0_cast(0)
    p0_cast(1)
    for t in range(NT):
        if t + 2 < NT:
            p0_cast(t + 2)
        cb, r_cb = stg[t % 4]
        transpose_to(cb, r_cb, lambda t=t: xT[:, :, t * 128:(t + 1) * 128], r_xT[t // 4], 8,
                     "dve" if t % 2 == 0 else "act")

    for j in range(4):
        slot, rs = wnext()
        for tc in range(4):
            k2 = (j * 4 + tc) % 2
            ba, ra = ps_half()
            bg, rg = ps_half()
            fa = [lambda e, d=d, ba=ba, tc=tc, slot=slot: e.matmul(ba, lhsT=slot[:, d, 0:128], rhs=xT[:, d, tc * 512:(tc + 1) * 512],
                                                                   start=(d == 0), stop=(d == 7)) for d in range(8)]
            S.mm(fa, reads=[rs, r_xT[tc]], writes=ra)
            fg = [lambda e, d=d, bg=bg, tc=tc, slot=slot: e.matmul(bg, lhsT=slot[:, d, 128:256], rhs=xT[:, d, tc * 512:(tc + 1) * 512],
                                                                   start=(d == 0), stop=(d == 7)) for d in range(8)]
            S.mm(fg, reads=[rs, r_xT[tc]], writes=rg)
            S.op("act", lambda e, bg=bg, k2=k2: e.activation(out=sgt[k2][:], in_=bg, func=AF.Sigmoid),
                 reads=rg, writes=[r_sgt[k2]])
            S.op("dve", lambda e, ba=ba, k2=k2, j=j, tc=tc: e.tensor_tensor(
                out=hTc[:, j, 32 + tc * 512:32 + (tc + 1) * 512], in0=ba, in1=sgt[k2][:], op=ALU.mult),
                reads=ra + [r_sgt[k2]], writes=[r_hTc])
        wdone()

    if stop_after == "glu":
        dbg_out("hTc", hTc[:], [128, 4, HP], [r_hTc])
        dbg_out("xT", xT[:], [128, 8, T], r_xT)
        return finish(), dbg_d

    conv_groups = [(j, tc) for j in range(4) for tc in range(4)]
    cg = {"i": 0, "tap": 0, "n": 0}
    conv_banks = [(PD[2][:, 0:512], [pd_res[2][0]]), (PD[2][:, 512:1024], [pd_res[2][1]])]

    def build_dg(j):
        S.op("pool", lambda e, j=j: e.tensor_tensor(
            out=dgb[j % 2][:], in0=ident[:].unsqueeze(1).broadcast_to([128, 31, 128]),
            in1=ccw[:, j, :].unsqueeze(2).broadcast_to([128, 31, 128]), op=ALU.mult),
            reads=[r_ident, r_ccw], writes=[r_dgb[j % 2]])

    def conv_emit(n):
        while n > 0 and cg["i"] < len(conv_groups):
            j, tc = conv_groups[cg["i"]]
            bk, rb = conv_banks[cg["i"] % 2]
            tap = cg["tap"]
            fn = lambda e, tap=tap, bk=bk, j=j, tc=tc: e.matmul(
                bk, lhsT=dgb[j % 2][:, tap, :], rhs=hTc[:, j, tc * 512 + 2 + tap:tc * 512 + 2 + tap + 512],
                start=(tap == 0), stop=(tap == 30))
            if tap < 30:
                S.mm_nosig([fn], reads=[r_dgb[j % 2], r_hTc], writes=(rb if tap == 0 else ()))
                cg["tap"] += 1
            else:
                S.mm([fn], reads=[r_dgb[j % 2], r_hTc], writes=rb)
                S.op("act", lambda e, bk=bk, j=j, tc=tc: e.activation(out=convT[:, j, tc * 512:(tc + 1) * 512], in_=bk,
                                                                    func=AF.Identity, bias=ccb[:, j:j + 1]),
                     reads=rb + [r_ccb], writes=[r_convT4[tc]])
                cg["tap"] = 0
                cg["i"] += 1
                if tc == 3 and j + 2 < 4:
                    build_dg(j + 2)
            n -= 1

    build_dg(0)
    build_dg(1)
    psn[0] = 4
    ri = [0]
    for hp in range(2):
        for g in range(3):
            dil = DILS[g]
            L = T // dil
            nb = L // 128
            wq_, rwq = wnext()
            wk_, rwk = wnext()
            wv, rwv = wnext()
            J = 512 // dil
            for (w_, rw_, dstT, r_dst) in ((wq_, rwq, qT, r_qT), (wk_, rwk, kT, r_kT)):
                for h in range(2):
                    for tc in range(4):
                        k2 = ri[0] % 2
                        ri[0] += 1
                        bk, rb = ps_half()
                        fns = [lambda e, d=d, bk=bk, w_=w_, h=h, tc=tc: e.matmul(
                            bk, lhsT=w_[:, d, h * 128:(h + 1) * 128], rhs=xT[:, d, tc * 512:(tc + 1) * 512],
                            start=(d == 0), stop=(d == 7)) for d in range(8)]
                        S.mm(fns, reads=[rw_, r_xT[tc]], writes=rb)
                        t1, t2 = rtmp1[k2], rtmp2[k2]
                        S.op("dve", lambda e, bk=bk, t1=t1, tc=tc: e.tensor_tensor(
                            out=t1[:], in0=bk, in1=ropeC[:, tc * 512:(tc + 1) * 512], op=ALU.mult),
                            reads=rb + [r_ropeC], writes=[r_rtmp1[k2]])
                        S.op("dve", lambda e, bk=bk, t2=t2, tc=tc: e.tensor_tensor(
                            out=t2[0:64, :], in0=bk[64:128, :], in1=ropeS[64:128, tc * 512:(tc + 1) * 512], op=ALU.mult),
                            reads=rb + [r_ropeS], writes=[r_rtmp2[k2]])
                        S.op("dve", lambda e, bk=bk, t2=t2, tc=tc: e.tensor_tensor(
                            out=t2[64:128, :], in0=bk[0:64, :], in1=ropeS[0:64, tc * 512:(tc + 1) * 512], op=ALU.mult),
                            reads=rb + [r_ropeS], writes=[r_rtmp2[k2]])
                        j0 = tc * 512 // dil
                        S.op("pool", lambda e, t1=t1, t2=t2, dstT=dstT, h=h, dil=dil, j0=j0, J=J: e.tensor_tensor(
                            out=dstT[:, h, :].rearrange("p (r j) -> p r j", r=dil)[:, :, j0:j0 + J],
                            in0=t1[:].rearrange("p (j r) -> p r j", r=dil),
                            in1=t2[:].rearrange("p (j r) -> p r j", r=dil), op=ALU.add),
                            reads=[r_rtmp1[k2], r_rtmp2[k2]], writes=[r_dst])
                wdone()
            for blk in range(16):
                r_, n_ = divmod(blk, nb)
                st0 = n_ * 128 * dil + r_
                bk, rb = ps_half()
                fns = [lambda e, d=d, bk=bk, st0=st0, dil=dil, wv=wv: e.matmul(
                    bk[:, 0:256], lhsT=xT[:, d, st0:st0 + 127 * dil + 1:dil], rhs=wv[:, d, :],
                    start=(d == 0), stop=(d == 7)) for d in range(8)]
                S.mm(fns, reads=[rwv] + r_xT, writes=rb)
                S.op("act", lambda e, bk=bk, blk=blk: e.activation(out=Vg[:, blk, :], in_=bk[:, 0:256], func=AF.Copy),
                     reads=rb, writes=[r_Vg])
            wdone()
            for blk in range(16):
                r_, n_ = divmod(blk, nb)
                hasp = n_ > 0
                cc = blk * 128
                cp = (blk - 1) * 128
                k2 = blk % 2
                bS, rS = ps_half()
                fns = []
                for h in range(2):
                    fns.append(lambda e, h=h, bS=bS, cc=cc: e.matmul(
                        bS[:, h * 128:(h + 1) * 128], lhsT=kT[:, h, cc:cc + 128], rhs=qT[:, h, cc:cc + 128],
                        start=True, stop=True))
                if hasp:
                    for h in range(2):
                        fns.append(lambda e, h=h, bS=bS, cc=cc, cp=cp: e.matmul(
                            bS[:, 256 + h * 128:256 + (h + 1) * 128], lhsT=kT[:, h, cp:cp + 128], rhs=qT[:, h, cc:cc + 128],
                            start=True, stop=True))
                S.mm(fns, reads=[r_qT, r_kT], writes=rS)
                conv_emit(6)
                ncp = 2 if hasp else 1
                ncol = 256 * ncp
                S.op("act", lambda e, bS=bS, k2=k2, ncol=ncol: e.activation(
                    out=PT[k2][:, 0:ncol], in_=bS[:, 0:ncol], func=AF.Exp, scale=SCALE),
                    reads=rS, writes=[r_PT[k2]])
                S.op("dve", lambda e, k2=k2, ncol=ncol, ncp=ncp: e.tensor_tensor(
                    out=PT[k2][:, 0:ncol].rearrange("p (c h q) -> p c h q", c=ncp, h=2),
                    in0=PT[k2][:, 0:ncol].rearrange("p (c h q) -> p c h q", c=ncp, h=2),
                    in1=masks[:, 0:ncp, :].unsqueeze(2).broadcast_to([128, ncp, 2, 128]), op=ALU.mult),
                    reads=[r_masks], rw=[r_PT[k2]])
                bU, rU = ps_half()
                fns = []
                for h in range(2):
                    fns.append(lambda e, h=h, bU=bU, blk=blk, k2=k2, hasp=hasp: e.matmul(
                        bU[:, h * 128:(h + 1) * 128], lhsT=Vg[:, blk, h * 128:(h + 1) * 128],
                        rhs=PT[k2][:, h * 128:(h + 1) * 128], start=True, stop=(not hasp)))
                    if hasp:
                        fns.append(lambda e, h=h, bU=bU, blk=blk, k2=k2: e.matmul(
                            bU[:, h * 128:(h + 1) * 128], lhsT=Vg[:, blk - 1, h * 128:(h + 1) * 128],
                            rhs=PT[k2][:, 256 + h * 128:256 + (h + 1) * 128], start=False, stop=True))
                fns.append(lambda e, bU=bU, k2=k2, hasp=hasp: e.matmul(
                    bU[:, 256:512], lhsT=ones[:], rhs=PT[k2][:, 0:256], start=True, stop=(not hasp)))
                if hasp:
                    fns.append(lambda e, bU=bU, k2=k2: e.matmul(
                        bU[:, 256:512], lhsT=ones[:], rhs=PT[k2][:, 256:512], start=False, stop=True))
                S.mm(fns, reads=[r_Vg, r_PT[k2], r_ones], writes=rU)
                st0 = n_ * 128 * dil + r_
                dst = UD[:, :, :, st0:st0 + 127 * dil + 1:dil]
                src = bU.rearrange("p (u h q) -> p u h q", u=2, h=2)
                if g == 0:
                    S.op("act", lambda e, dst=dst, src=src: e.activation(out=dst, in_=src, func=AF.Copy), reads=rU, writes=[r_UD])
                else:
                    S.op("dve", lambda e, dst=dst, src=src: e.tensor_tensor(out=dst, in0=src, in1=dst, op=ALU.add),
                         reads=rU, rw=[r_UD])
        S.op("act", lambda e: e.activation(out=UD[:, 1, :, :], in_=UD[:, 1, :, :], func=AF.Ln), rw=[r_UD])
        S.op("act", lambda e: e.activation(out=UD[:, 1, :, :], in_=UD[:, 1, :, :], func=AF.Exp, scale=-1.0), rw=[r_UD])
        S.op("dve", lambda e, hp=hp: e.tensor_tensor(out=attnT[:, 2 * hp:2 * hp + 2, :], in0=UD[:, 0, :, :],
                                                     in1=UD[:, 1, :, :], op=ALU.mult),
             reads=[r_UD], writes=[r_attnT])

    if stop_after == "attn":
        dbg_out("attnT", attnT[:], [128, 4, T], [r_attnT])
        dbg_out("hTc", hTc[:], [128, 4, HP], [r_hTc])
        return finish(), dbg_d

    conv_emit(10 ** 6)
    psn[0] = 6

    def conv_ln(tc):
        cs = slice(tc * 512, (tc + 1) * 512)
        for j in range(4):
            S.op("dve", lambda e, j=j, cs=cs: e.tensor_tensor(out=ysq[:, j, :], in0=convT[:, j, cs], in1=convT[:, j, cs], op=ALU.mult),
                 reads=[r_convT4[tc]], writes=[r_ysq])
        bm, rm = ps_half()
        S.mm([lambda e, j=j, bm=bm, cs=cs: e.matmul(bm, lhsT=ones[:], rhs=convT[:, j, cs], start=(j == 0), stop=(j == 3))
              for j in range(4)], reads=[r_convT4[tc], r_ones], writes=rm)
        bq, rq = ps_half()
        S.mm([lambda e, j=j, bq=bq: e.matmul(bq, lhsT=ones[:], rhs=ysq[:, j, :], start=(j == 0), stop=(j == 3))
              for j in range(4)], reads=[r_ysq, r_ones], writes=rq)
        S.op("dve", lambda e, bm=bm: e.tensor_scalar_mul(out=mt[:], in0=bm, scalar1=1.0 / 512), reads=rm, writes=[r_mt])
        S.op("dve", lambda e: e.tensor_tensor(out=vt[:], in0=mt[:], in1=mt[:], op=ALU.mult), reads=[r_mt], writes=[r_vt])
        S.op("dve", lambda e, bq=bq: e.scalar_tensor_tensor(out=vt[:], in0=bq, scalar=1.0 / 512, in1=vt[:],
                                                            op0=ALU.mult, op1=ALU.subtract), reads=rq, rw=[r_vt])
        S.op("act", lambda e: e.activation(out=vt[:], in_=vt[:], func=AF.Ln, bias=EPS), rw=[r_vt])
        S.op("act", lambda e: e.activation(out=vt[:], in_=vt[:], func=AF.Exp, scale=-0.5), rw=[r_vt])
        for j in range(4):
            k2 = j % 2
            S.op("dve", lambda e, j=j, k2=k2, cs=cs: e.tensor_tensor(out=zt[k2][:], in0=convT[:, j, cs], in1=mt[:], op=ALU.subtract),
                 reads=[r_convT4[tc], r_mt], writes=[r_zt[k2]])
            S.op("dve", lambda e, k2=k2: e.tensor_tensor(out=zt[k2][:], in0=zt[k2][:], in1=vt[:], op=ALU.mult),
                 reads=[r_vt], rw=[r_zt[k2]])
            S.op("act", lambda e, j=j, k2=k2, cs=cs: e.activation(
                out=convT[:, j, cs], in_=zt[k2][:], func=AF.Silu, scale=clg[:, j:j + 1], bias=clb[:, j:j + 1]),
                reads=[r_zt[k2], r_clg, r_clb], rw=[r_convT4[tc]])

    if stop_after == "conv":
        for tc in range(4):
            conv_ln(tc)
        dbg_out("convT", convT[:], [128, 4, T], r_convT4)
        dbg_out("attnT", attnT[:], [128, 4, T], [r_attnT])
        return finish(), dbg_d

    def mixer_out_and_ln1(l, cat_fn, r_cat_fn, w_d, lnp1, r_lnp1, x_from_hbm, pre_step=None):
        def xload(t):
            S.dma("sp", [lambda e, t=t: e.dma_start(out=x_tok[:, t, :], in_=x_d[t * 128:(t + 1) * 128, :])],
                  ds_x[2 + (t % 2)], writes=[r_xtok[t]])
        if x_from_hbm:
            for t in range(4):
                xload(t)
        S.dma("pool", [lambda e: e.dma_start(out=Wmix[:], in_=w_d.rearrange("(kc p) n -> p kc n", p=128))],
              ds_w1[0], writes=[r_Wmix])
        load_lnp(lnp1, r_lnp1, lmg_d, lmb_d, l)
        if x_from_hbm:
            for t in range(4, NT):
                xload(t)
        def A1(t):
            k2 = t % 2
            bD, rD = ps_full()
            fns = []
            for half in range(2):
                for f in range(8):
                    fns.append(lambda e, half=half, f=f, bD=bD, t=t: e.matmul(
                        bD[:, half * 512:(half + 1) * 512], lhsT=cat_fn(f, t), rhs=Wmix[:, f, half * 512:(half + 1) * 512],
                        start=(f == 0), stop=(f == 7)))
            S.mm(fns, reads=r_cat_fn(t) + [r_Wmix], writes=rD)
            S.op("dve", lambda e, bD=bD, t=t: e.scalar_tensor_tensor(
                out=x_tok[:, t, :], in0=x_tok[:, t, :], scalar=ALPHA, in1=bD[:], op0=ALU.mult, op1=ALU.add),
                reads=rD, rw=[r_xtok[t]])
            ln_stats(t, k2)

        def A2(t):
            ln_rstd_tile(t)
            ln_apply(t)

        pipeline([A1, A2,
                  lambda t: ln_affine(t, lnp1, r_lnp1, "dve", "pool"),
                  lambda t: cast_and_transpose(t, lambda t=t: xT[:, :, t * 128:(t + 1) * 128], r_xT[t // 4], t % 2, "dve")],
                 NT, pre_step)

    for tc_ in range(4):
        conv_ln(tc_)

    def l0_pre(s_):
        pass

    mixer_out_and_ln1(0, lambda f, t: (attnT[:, f, t * 128:(t + 1) * 128] if f < 4 else convT[:, f - 4, t * 128:(t + 1) * 128]),
                      lambda t: [r_attnT, r_convT4[t // 4]], ewout_d, lnp1_L0, r_lnp1_L0, True, l0_pre)

    if stop_after == "ln1":
        for t in range(NT):
            dbg_out("x1_%d" % t, x_tok[:, t, :], [128, D], [r_xtok[t]])
        return finish(), dbg_d

    FG = [(0, 4), (4, 8), (8, 12), (12, 16), (16, 19), (19, 22)]

    def ffn_and_ple(l, last_layer):
        fw_out = fwout_d[l]
        for pi, (f0, f1) in enumerate(FG):
            nf = f1 - f0
            wb = pi % 2
            S.dma("pool", [lambda e, wb=wb, f0=f0, f1=f1, nf=nf: e.dma_start(
                out=Wfo[wb][:, 0:nf, :], in_=fw_out[f0 * 128:f1 * 128, :].rearrange("(kc p) n -> p kc n", p=128))],
                ds_w1[wb], writes=[r_Wfo[wb]])
            for f in range(f0, f1):
                slot, rs = wnext()
                for tc in range(4):
                    k2 = tc % 2
                    bg, rg = ps_half()
                    bu, ru = ps_half()
                    S.mm([lambda e, d=d, bg=bg, tc=tc, slot=slot: e.matmul(
                        bg, lhsT=slot[:, d, 0:128], rhs=xT[:, d, tc * 512:(tc + 1) * 512], start=(d == 0), stop=(d == 7))
                        for d in range(8)], reads=[rs, r_xT[tc]], writes=rg)
                    S.mm([lambda e, d=d, bu=bu, tc=tc, slot=slot: e.matmul(
                        bu, lhsT=slot[:, d, 128:256], rhs=xT[:, d, tc * 512:(tc + 1) * 512], start=(d == 0), stop=(d == 7))
                        for d in range(8)], reads=[rs, r_xT[tc]], writes=ru)
                    S.op("act", lambda e, bg=bg, k2=k2: e.activation(out=sgt[k2][:], in_=bg, func=AF.Silu),
                         reads=rg, writes=[r_sgt[k2]])
                    S.op("dve", lambda e, bu=bu, k2=k2, f=f, f0=f0, tc=tc: e.tensor_tensor(
                        out=hTf[:, f - f0, tc * 512:(tc + 1) * 512], in0=bu, in1=sgt[k2][:], op=ALU.mult),
                        reads=ru + [r_sgt[k2]], writes=[r_hTf])
                wdone()
            if pi == 0:
                load_lnp(lnp2, r_lnp2, lfg_d, lfb_d, l)
                S.dma("pool", [lambda e: e.dma_start(out=Wpg[:], in_=pwg_d[l].rearrange("(kc p) n -> p kc n", p=128)),
                               lambda e: e.dma_start(out=Wpp[:], in_=pwp_d[l].rearrange("(kc p) n -> p kc n", p=128))],
                      ds_ple, writes=[r_Wpg, r_Wpp])
            last = (pi == len(FG) - 1)

            def A1(t, nf=nf, wb=wb, pi=pi, last=last):
                k2 = t % 2
                bD, rD = ps_full()
                fns = []
                for half in range(2):
                    for ff in range(nf):
                        fns.append(lambda e, half=half, ff=ff, bD=bD, t=t, wb=wb, nf=nf: e.matmul(
                            bD[:, half * 512:(half + 1) * 512], lhsT=hTf[:, ff, t * 128:(t + 1) * 128],
                            rhs=Wfo[wb][:, ff, half * 512:(half + 1) * 512], start=(ff == 0), stop=(ff == nf - 1)))
                S.mm(fns, reads=[r_hTf, r_Wfo[wb]], writes=rD)
                if pi == 0:
                    S.op("dve", lambda e, bD=bD, t=t: e.scalar_tensor_tensor(
                        out=x_tok[:, t, :], in0=x_tok[:, t, :], scalar=ALPHA, in1=bD[:], op0=ALU.mult, op1=ALU.add),
                        reads=rD, rw=[r_xtok[t]])
                else:
                    S.op("dve", lambda e, bD=bD, t=t: e.tensor_tensor(
                        out=x_tok[:, t, :], in0=x_tok[:, t, :], in1=bD[:], op=ALU.add), reads=rD, rw=[r_xtok[t]])
                if last:
                    ln_stats(t, k2)

            for t in range(NT):
                A1(t)
            if not last:
                continue
            ln_rstd_all()

            def cast_only(t, cb, r_cb):
                S.op("act", lambda e: e.activation(out=cb[:], in_=x_tok[:, t, :], func=AF.Copy),
                     reads=[r_xtok[t]], writes=[r_cb])

            def ok(t):
                return 0 <= t < NT

            for s_ in range(NT + 6):
                t = s_ - 2
                if ok(t):
                    k2 = t % 2
                    transpose_to(castt[k2], r_castt[k2], lambda k2=k2: x2T[k2][:], r_x2T[k2], 8, "dve")
                t = s_ - 3
                if ok(t):
                    k2 = t % 2
                    transpose_to(pbt[k2], r_pbt[k2], lambda k2=k2: pTt[k2][:], r_pTt[k2], 2, "act")
                t = s_ - 6
                if ok(t) and not last_layer:
                    k2 = t % 2
                    transpose_to(castd[k2], r_castd[k2], lambda t=t: xT[:, :, t * 128:(t + 1) * 128], r_xT[t // 4], 8, "dve")
                t = s_
                if ok(t):
                    ln_apply(t)
                    ln_affine(t, lnp2, r_lnp2, "dve", "dve")
                t = s_ - 1
                if ok(t):
                    k2 = t % 2
                    cast_only(t, castt[k2], r_castt[k2])
                    S.dma("pool", [lambda e, t=t, k2=k2: e.dma_start(out=pbt[k2][:], in_=p_d[l, t * 128:(t + 1) * 128, :])],
                          ds_p[k2], writes=[r_pbt[k2]])
                t = s_ - 5
                if ok(t):
                    if last_layer:
                        S.dma("sp", [lambda e, t=t: e.dma_start(out=out_d[t * 128:(t + 1) * 128, :], in_=x_tok[:, t, :])],
                              ds_out[t % 4], reads=[r_xtok[t]])
                    else:
                        cast_only(t, castd[t % 2], r_castd[t % 2])
                tg = s_ - 3
                bG = rG = None
                if ok(tg):
                    k2 = tg % 2
                    bG, rG = ps_full()
                    fns = []
                    for half in range(2):
                        for c in range(8):
                            fns.append(lambda e, half=half, c=c, bG=bG, k2=k2: e.matmul(
                                bG[:, half * 512:(half + 1) * 512], lhsT=x2T[k2][:, c, :], rhs=Wpg[:, c, half * 512:(half + 1) * 512],
                                start=(c == 0), stop=(c == 7)))
                    S.mm(fns, reads=[r_x2T[k2], r_Wpg], writes=rG)
                t = s_ - 4
                if ok(t):
                    k2 = t % 2
                    sg_ = sigt[k2]
                    bP, rP = ps_full()
                    fns = []
                    for half in range(2):
                        for c in range(2):
                            fns.append(lambda e, half=half, c=c, bP=bP, k2=k2: e.matmul(
                                bP[:, half * 512:(half + 1) * 512], lhsT=pTt[k2][:, c, :], rhs=Wpp[:, c, half * 512:(half + 1) * 512],
                                start=(c == 0), stop=(c == 1)))
                    S.mm(fns, reads=[r_pTt[k2], r_Wpp], writes=rP)
                    S.op("dve", lambda e, bP=bP, sg_=sg_: e.tensor_tensor(out=sg_[:], in0=bP[:], in1=sg_[:], op=ALU.mult),
                         reads=rP, rw=[r_sigt[k2]])
                    S.op("pool", lambda e, t=t, sg_=sg_: e.tensor_tensor(out=x_tok[:, t, :], in0=x_tok[:, t, :], in1=sg_[:], op=ALU.add),
                         reads=[r_sigt[k2]], rw=[r_xtok[t]])
                if ok(tg):
                    k2 = tg % 2
                    sg_ = sigt[k2]
                    S.op("act", lambda e, bG=bG, sg_=sg_: e.activation(out=sg_[:], in_=bG[:], func=AF.Sigmoid),
                         reads=rG, writes=[r_sigt[k2]])

    ffn_and_ple(0, False)

    if stop_after == "l0":
        for t in range(NT):
            dbg_out("x3_%d" % t, x_tok[:, t, :], [128, D], [r_xtok[t]])
        return finish(), dbg_d

    for i in range(2):
        S.op("dve", lambda e, i=i: e.memset(ub[i][:, 0:2], 0.0), writes=[r_ub[i]])
    ui = [0]
    for jj in range(4):
        wb_, rwb = wnext()
        wc_, rwc = wnext()
        wh_, rwh = wnext()
        for j2 in range(2):
            j = 2 * jj + j2
            ubi = ui[0] % 2
            ui[0] += 1
            u = ub[ubi]
            for tc in range(4):
                k2 = tc % 2
                bb, rbb = ps_half()
                bc, rbc = ps_half()
                bh, rbh = ps_half()
                for (bk, rb, w_, rw_) in ((bb, rbb, wb_, rwb), (bc, rbc, wc_, rwc), (bh, rbh, wh_, rwh)):
                    S.mm([lambda e, d=d, bk=bk, w_=w_, j2=j2, tc=tc: e.matmul(
                        bk, lhsT=w_[:, d, j2 * 128:(j2 + 1) * 128], rhs=xT[:, d, tc * 512:(tc + 1) * 512],
                        start=(d == 0), stop=(d == 7)) for d in range(8)], reads=[rw_, r_xT[tc]], writes=rb)
                S.op("act", lambda e, bc=bc, k2=k2: e.activation(out=sgt[k2][:], in_=bc, func=AF.Copy),
                     reads=rbc, writes=[r_sgt[k2]])
                S.op("dve", lambda e, bh=bh, k2=k2, u=u, tc=tc: e.tensor_tensor(
                    out=u[:, 2 + tc * 512:2 + (tc + 1) * 512], in0=bh, in1=sgt[k2][:], op=ALU.mult),
                    reads=rbh + [r_sgt[k2]], rw=[r_ub[ubi]])
                ct = ctmp[k2]
                S.op("dve", lambda e, u=u, ct=ct, tc=tc, j=j: e.tensor_scalar_mul(
                    out=ct[:], in0=u[:, 2 + tc * 512:2 + (tc + 1) * 512], scalar1=ocw[:, j, 2:3]),
                    reads=[r_ub[ubi], r_ocw], writes=[r_ctmp[k2]])
                S.op("dve", lambda e, u=u, ct=ct, tc=tc, j=j: e.scalar_tensor_tensor(
                    out=ct[:], in0=u[:, 1 + tc * 512:1 + (tc + 1) * 512], scalar=ocw[:, j, 1:2], in1=ct[:],
                    op0=ALU.mult, op1=ALU.add), reads=[r_ub[ubi]], rw=[r_ctmp[k2]])
                S.op("dve", lambda e, u=u, ct=ct, tc=tc, j=j: e.scalar_tensor_tensor(
                    out=ct[:], in0=u[:, tc * 512:(tc + 1) * 512], scalar=ocw[:, j, 0:1], in1=ct[:],
                    op0=ALU.mult, op1=ALU.add), reads=[r_ub[ubi]], rw=[r_ctmp[k2]])
                S.op("dve", lambda e, bb=bb, ct=ct, j=j, tc=tc: e.tensor_tensor(
                    out=yT[:, j, tc * 512:(tc + 1) * 512], in0=bb, in1=ct[:], op=ALU.mult),
                    reads=rbb + [r_ctmp[k2]], writes=[r_yT])
        wdone(); wdone(); wdone()

    mixer_out_and_ln1(1, lambda f, t: yT[:, f, t * 128:(t + 1) * 128], lambda t: [r_yT], owout_d, lnp1_L1, r_lnp1_L1, False)
    ffn_and_ple(1, True)
    return finish(), dbg_d


_CACHE = {}


def _rope_tables():
    half = 64
    inv = 10000.0 ** (-np.arange(half, dtype=np.float64) / half)
    pos = np.arange(T, dtype=np.float64)
    ang = pos[:, None] * inv[None, :]
    c = np.cos(ang).T.astype(np.float32)
    s = np.sin(ang).T.astype(np.float32)
    C = np.concatenate([c, c], axis=0)
    SW = np.concatenate([s, -s], axis=0)
    return np.ascontiguousarray(C), np.ascontiguousarray(SW)


def make_in_maps(inputs):
    f = lambda a: np.ascontiguousarray(np.asarray(a, dtype=np.float32))
    C, SW = _rope_tables()
    shared = {
        "even_w_in": f(inputs["even_w_in"])[0],
        "even_w_out": f(inputs["even_w_out"])[0],
        "conf_conv_w": np.ascontiguousarray(f(inputs["conf_conv_w"])[0].reshape(31, 4, 128).transpose(2, 1, 0)),
        "conf_conv_b": np.ascontiguousarray(f(inputs["conf_conv_b"]).reshape(4, 128).T),
        "conf_ln_g": np.ascontiguousarray(f(inputs["conf_ln_g"]).reshape(4, 128).T),
        "conf_ln_b": np.ascontiguousarray(f(inputs["conf_ln_b"]).reshape(4, 128).T),
        "odd_w_in": f(inputs["odd_w_in"])[0],
        "odd_conv_w": np.ascontiguousarray(f(inputs["odd_conv_w"])[0].reshape(3, 8, 128).transpose(2, 1, 0)),
        "odd_w_out": f(inputs["odd_w_out"])[0],
        "ln_mix_g": f(inputs["ln_mix_g"]),
        "ln_mix_b": f(inputs["ln_mix_b"]),
        "ln_ffn_g": f(inputs["ln_ffn_g"]),
        "ln_ffn_b": f(inputs["ln_ffn_b"]),
        "ffn_w_in": f(inputs["ffn_w_in"]),
        "ffn_w_out": f(inputs["ffn_w_out"]),
        "ple_w_proj": f(inputs["ple_w_proj"]),
        "ple_w_gate": f(inputs["ple_w_gate"]),
        "rope_c": C,
        "rope_s": SW,
    }
    x = f(inputs["x"])
    p = f(inputs["p"])
    maps = []
    for b in range(8):
        m = dict(shared)
        m["x"] = x[b]
        m["p"] = np.ascontiguousarray(p[:, b])
        maps.append(m)
    return maps


def kernel(**inputs):
    if "nc" not in _CACHE:
        _CACHE["nc"] = build()[0]
    nc = _CACHE["nc"]
    in_maps = make_in_maps(inputs)
    res = run_bass_kernel_spmd(nc, in_maps, core_ids=list(range(8)))
    out = np.stack([np.asarray(res.results[b]["out"], dtype=np.float32) for b in range(8)], axis=0)
    return out
```

```python
import os
import numpy as np
import concourse.bass as bass
import concourse.mybir as mybir
from concourse.bass_utils import run_bass_kernel_spmd

F32 = mybir.dt.float32
BF16 = mybir.dt.bfloat16
AF = mybir.ActivationFunctionType
ALU = mybir.AluOpType

T = 2048
D = 1024
NT = 16
FF = 2816
NF = 22
ALPHA = float(4 ** 0.25)
EPS = 1e-5
SCALE = float(128 ** -0.5)
DILS = (1, 4, 16)
K = 1024

BASE = 16512
R_A = BASE
R_B = R_A + 64 * K
R_C = R_B + 32 * K
R_C_SZ = 50 * K
R_W1 = R_C + R_C_SZ
RING = R_W1 + 16 * K
MISC = RING + 16 * K


class Res:
    ALL = []

    def __init__(self, name="", ranges=None):
        self.ws = {}
        self.rs = {}
        self.name = name
        self.ranges = ranges
        self.ov = [self]
        if ranges is not None:
            for o in Res.ALL:
                if o.ranges is None:
                    continue
                if any(a0 < b1 and b0 < a1 for (a0, a1) in ranges for (b0, b1) in o.ranges):
                    self.ov.append(o)
                    o.ov.append(self)
            Res.ALL.append(self)


def _merge(d, tok):
    if tok is None:
        return
    sem, val, key = tok
    if key not in d or d[key][1] < val:
        d[key] = tok


class DmaSem:
    def __init__(self, sem):
        self.sem = sem
        self.n = 0


class Sched:
    ENG = ("pe", "act", "dve", "pool", "sp")

    def __init__(self, nc):
        self.nc = nc
        self.q = {e: [] for e in self.ENG}
        self.sem = {e: nc.alloc_semaphore("sem_" + e) for e in self.ENG}
        self.cnt = {e: 0 for e in self.ENG}
        self.waited = {e: {} for e in self.ENG}
        self.nsem = 0
        self.ninst = {e: 0 for e in self.ENG}

    def dsem(self, name=None):
        self.nsem += 1
        return DmaSem(self.nc.alloc_semaphore(name or ("ds%d" % self.nsem)))

    def _wait(self, eng, toks):
        w = self.waited[eng]
        for d in toks:
            if d is None:
                continue
            sem, val, key = d
            if key == eng:
                pass
            if w.get(key, 0) >= val:
                continue
            w[key] = val
            self.q[eng].append(lambda e, sem=sem, val=val: e.wait_ge(sem, val))
            self.ninst[eng] += 1

    def _deps(self, reads, writes, rw, deps):
        toks = list(deps)
        for r in reads:
            for o in r.ov:
                toks.extend(o.ws.values())
        for r in writes:
            for o in r.ov:
                toks.extend(o.rs.values())
                if o is not r:
                    toks.extend(o.ws.values())
        for r in rw:
            for o in r.ov:
                toks.extend(o.ws.values())
                toks.extend(o.rs.values())
        return toks

    def _post(self, tok, reads, writes, rw):
        for r in reads:
            _merge(r.rs, tok)
        for r in writes:
            if r.rs:
                r.ws = {}
                r.rs = {}
            _merge(r.ws, tok)
        for r in rw:
            r.ws = {}
            r.rs = {}
            _merge(r.ws, tok)

    def op(self, eng, fn, reads=(), writes=(), rw=(), deps=()):
        self._wait(eng, self._deps(reads, writes, rw, deps))
        self.cnt[eng] += 1
        sem = self.sem[eng]
        self.q[eng].append(lambda e, fn=fn, sem=sem: fn(e).then_inc(sem, 1))
        self.ninst[eng] += 1
        tok = (sem, self.cnt[eng], eng)
        self._post(tok, reads, writes, rw)
        return tok

    def mm_nosig(self, fns, reads=(), writes=(), deps=()):
        eng = "pe"
        self._wait(eng, self._deps(reads, writes, (), deps))
        for fn in fns:
            self.q[eng].append(lambda e, fn=fn: fn(e))
        self.ninst[eng] += len(fns)

    def mm(self, fns, reads=(), writes=(), deps=()):
        eng = "pe"
        self._wait(eng, self._deps(reads, writes, (), deps))
        for fn in fns[:-1]:
            self.q[eng].append(lambda e, fn=fn: fn(e))
        self.ninst[eng] += len(fns)
        self.cnt[eng] += 1
        sem = self.sem[eng]
        fn = fns[-1]
        self.q[eng].append(lambda e, fn=fn, sem=sem: fn(e).then_inc(sem, 1))
        tok = (sem, self.cnt[eng], eng)
        self._post(tok, reads, writes, ())
        return tok

    def dma(self, eng, fns, ds, reads=(), writes=(), deps=()):
        toks = self._deps(reads, writes, (), deps)
        if ds.n > 0:
            toks.append((ds.sem, 16 * ds.n, id(ds)))
        self._wait(eng, toks)
        for fn in fns:
            self.q[eng].append(lambda e, fn=fn, sem=ds.sem: fn(e).then_inc(sem, 16))
        self.ninst[eng] += len(fns)
        ds.n += len(fns)
        tok = (ds.sem, 16 * ds.n, id(ds))
        self._post(tok, reads, writes, ())
        return tok

    def wait(self, eng, toks):
        self._wait(eng, toks)

    def emit(self):
        nc = self.nc
        q = self.q
        with nc.Block() as block:
            @block.tensor
            def _(e):
                for f in q["pe"]:
                    f(e)

            @block.scalar
            def _(e):
                for f in q["act"]:
                    f(e)

            @block.vector
            def _(e):
                for f in q["dve"]:
                    f(e)

            @block.gpsimd
            def _(e):
                for f in q["pool"]:
                    f(e)

            @block.sync
            def _(e):
                for f in q["sp"]:
                    f(e)


def build(stop_after=None, dbg=False):
    nc = bass.Bass("TRN2", target_bir_lowering=False)
    S = Sched(nc)
    Res.ALL = []

    def din(name, shape):
        return nc.dram_tensor(name, list(shape), F32, kind="ExternalInput").ap()

    x_d = din("x", [T, D])
    p_d = din("p", [2, T, 256])
    ewin_d = din("even_w_in", [D, 5632])
    ewout_d = din("even_w_out", [D, D])
    ccw_d = din("conf_conv_w", [128, 4, 31])
    ccb_d = din("conf_conv_b", [128, 4])
    clg_d = din("conf_ln_g", [128, 4])
    clb_d = din("conf_ln_b", [128, 4])
    owin_d = din("odd_w_in", [D, 3072])
    ocw_d = din("odd_conv_w", [128, 8, 3])
    owout_d = din("odd_w_out", [D, D])
    lmg_d = din("ln_mix_g", [2, D])
    lmb_d = din("ln_mix_b", [2, D])
    lfg_d = din("ln_ffn_g", [2, D])
    lfb_d = din("ln_ffn_b", [2, D])
    fwin_d = din("ffn_w_in", [2, D, 2 * FF])
    fwout_d = din("ffn_w_out", [2, FF, D])
    pwp_d = din("ple_w_proj", [2, 256, D])
    pwg_d = din("ple_w_gate", [2, D, D])
    ropec_d = din("rope_c", [128, T])
    ropes_d = din("rope_s", [128, T])
    out_d = nc.dram_tensor("out", [T, D], F32, kind="ExternalOutput").ap()
    dbg_d = {}

    def sb(name, shape, dt, off):
        n = int(np.prod(shape[1:])) * (4 if dt == F32 else 2)
        t_ = nc.alloc_sbuf_tensor_at(name, list(shape), dt, offset=off)
        return t_, Res(name, [(off, off + n)])

    x_tok = nc.alloc_sbuf_tensor_at("x_tok", [128, NT, D], F32, offset=R_A)
    r_xtok = [Res("xtok%d" % t, [(R_A + t * 4096, R_A + (t + 1) * 4096)]) for t in range(NT)]
    UD, r_UD = sb("UD", [128, 2, 2, T], F32, R_A)
    qT, r_qT = sb("qT", [128, 2, T], BF16, R_A + 32 * K)
    kT, r_kT = sb("kT", [128, 2, T], BF16, R_A + 40 * K)
    Vg, r_Vg = sb("Vg", [128, 16, 256], BF16, R_A + 48 * K)
    rtmp1, r_rtmp1, rtmp2, r_rtmp2 = [], [], [], []
    for i in range(2):
        a_, b_ = sb("rtmp1_%d" % i, [128, 512], F32, R_A + 56 * K + i * 2 * K)
        rtmp1.append(a_); r_rtmp1.append(b_)
        a_, b_ = sb("rtmp2_%d" % i, [128, 512], F32, R_A + 60 * K + i * 2 * K)
        rtmp2.append(a_); r_rtmp2.append(b_)
    xT = nc.alloc_sbuf_tensor_at("xT", [128, 8, T], BF16, offset=R_B)
    r_xT = [Res("xT%d" % c, [(R_B + d * 4096 + c * 1024, R_B + d * 4096 + (c + 1) * 1024) for d in range(8)])
            for c in range(4)]
    HP = 2080
    hTc, r_hTc = sb("hTc", [128, 4, HP], BF16, R_C)
    attnT, r_attnT = sb("attnT", [128, 4, T], BF16, R_C + 16640)
    convT, r_convT = sb("convT", [128, 4, T], BF16, R_C + 16640 + 16 * K)
    ropeC, r_ropeC = sb("ropeC", [128, T], F32, R_W1)
    ropeS, r_ropeS = sb("ropeS", [128, T], F32, R_W1 + 8 * K)
    r_convT4 = [Res("convT_tc%d" % c, [(R_C + 16640 + 16 * K + j * 4096 + c * 1024,
                                        R_C + 16640 + 16 * K + j * 4096 + (c + 1) * 1024) for j in range(4)])
                for c in range(4)]
    mt, r_mt = sb("mt", [128, 512], F32, R_C + 8 * K)
    vt, r_vt = sb("vt", [128, 512], F32, R_C + 10 * K)
    zt, r_zt = [], []
    for i in range(2):
        a_, b_ = sb("zt%d" % i, [128, 512], F32, R_C + 12 * K + i * 2 * K)
        zt.append(a_); r_zt.append(b_)
    hTf, r_hTf = sb("hTf", [128, 4, T], BF16, R_C)
    Wpg, r_Wpg = sb("Wpg", [128, 8, D], BF16, R_C + 16 * K)
    Wpp, r_Wpp = sb("Wpp", [128, 2, D], BF16, R_C + 32 * K)
    lnp2, r_lnp2 = sb("lnp2", [128, 2, D], F32, R_C + 36 * K)
    lnp1_L0, r_lnp1_L0 = sb("lnp1a", [128, 2, D], F32, R_C)
    sigt1, r_sigt1 = sb("sigt1", [128, D], F32, R_C + 44 * K)
    yT, r_yT = sb("yT", [128, 8, T], BF16, R_C)
    ub, r_ub = [], []
    for i in range(2):
        a_, b_ = sb("ub%d" % i, [128, 2 + T], F32, R_C + 32 * K + i * 8224)
        ub.append(a_); r_ub.append(b_)
    lnp1_L1, r_lnp1_L1 = sb("lnp1b", [128, 2, D], F32, R_C + 32 * K)
    assert 32 * K + 2 * 8224 <= R_C_SZ
    Wmix, r_Wmix = sb("Wmix", [128, 8, D], BF16, R_W1)
    Wfo, r_Wfo = [], []
    for i in range(2):
        a_, b_ = sb("Wfo%d" % i, [128, 4, D], BF16, R_W1 + i * 8 * K)
        Wfo.append(a_); r_Wfo.append(b_)
    castd, r_castd = [], []
    for i in range(2):
        a_, b_ = sb("castd%d" % i, [128, D], BF16, R_W1 + i * 2 * K)
        castd.append(a_); r_castd.append(b_)
    ring, r_ring = [], []
    for i in range(4):
        a_, b_ = sb("ring%d" % i, [128, 8, 256], BF16, RING + i * 4 * K)
        ring.append(a_); r_ring.append(b_)
    mo = [MISC]

    def misc(name, shape, dt):
        n = int(np.prod(shape[1:])) * (4 if dt == F32 else 2)
        n = (n + 31) // 32 * 32
        t_, r_ = sb(name, shape, dt, mo[0])
        mo[0] += n
        return t_, r_

    def misc2(name, shape, dt):
        ts, rs = [], []
        for i in range(2):
            a_, b_ = misc("%s%d" % (name, i), shape, dt)
            ts.append(a_); rs.append(b_)
        return ts, rs

    ident, r_ident = misc("ident", [128, 128], BF16)
    identf, r_identf = misc("identf", [128, 128], F32)
    ones, r_ones = misc("ones", [128, 128], BF16)
    masks, r_masks = misc("masks", [128, 2, 128], BF16)
    maskf, r_maskf = misc("maskf", [128, 2, 128], F32)
    ccw, r_ccw = misc("ccw", [128, 4, 31], F32)
    ccb, r_ccb = misc("ccb", [128, 4], F32)
    clg, r_clg = misc("clg", [128, 4], F32)
    clb, r_clb = misc("clb", [128, 4], F32)
    ocw, r_ocw = misc("ocw", [128, 8, 3], F32)
    r_prm = [r_ccw, r_ccb, r_clg, r_clb, r_ocw]
    off_dg0 = mo[0]
    sgt, r_sgt = misc2("sgt", [128, 512], F32)
    ctmp, r_ctmp = misc2("ctmp", [128, 512], F32)
    off_dg1 = mo[0]
    castt, r_castt = misc2("castt", [128, D], BF16)
    x2T, r_x2T = misc2("x2T", [128, 8, 128], BF16)
    assert off_dg1 - off_dg0 == 8 * K and mo[0] - off_dg1 == 8 * K
    stg_a, r_stg_a = [], []
    for i in range(2):
        a_, b_ = sb("stg%d" % i, [128, D], BF16, off_dg1 + 4 * K + i * 2 * K)
        stg_a.append(a_); r_stg_a.append(b_)
    ysq, r_ysq = sb("ysq", [128, 4, 512], BF16, off_dg0)
    dgb, r_dgb = [], []
    for i, o_ in enumerate((off_dg0, off_dg1)):
        a_, b_ = sb("dgb%d" % i, [128, 31, 128], BF16, o_)
        dgb.append(a_); r_dgb.append(b_)
    pbt, r_pbt = misc2("pbt", [128, 256], BF16)
    pTt, r_pTt = misc2("pTt", [128, 2, 128], BF16)
    sigt0, r_sigt0 = misc("sigt", [128, D], F32)
    PT, r_PT = misc2("PT", [128, 512], BF16)
    sigt = [sigt0, sigt1]
    r_sigt = [r_sigt0, r_sigt1]
    stt, r_st = misc2("stt", [128, 2, 6], F32)
    mvall, _r = misc("mvall", [128, NT, 4], F32)
    r_mvall = [Res("mvall%d" % t) for t in range(NT)]
    assert mo[0] <= 229344, mo[0]

    PD = [nc.alloc_psum_tensor("pd%d" % i, [128, 1024], F32) for i in range(3)]
    PTB = [nc.alloc_psum_tensor("ptb%d" % i, [128, 1024], BF16) for i in range(2)]
    pd_res = [[Res("pd%d_%d" % (i, h)) for h in range(2)] for i in range(3)]
    ptb_res = [Res("ptb%d" % i) for i in range(2)]
    psc = [0, 0]
    psn = [6]

    def ps_half():
        i = psc[0] % psn[0]
        psc[0] += 1
        return PD[i // 2][:, (i % 2) * 512:(i % 2) * 512 + 512], [pd_res[i // 2][i % 2]]

    def ps_full():
        if psc[0] % 2 == 1:
            psc[0] += 1
        i = psc[0] % psn[0]
        psc[0] += 2
        return PD[i // 2], pd_res[i // 2]

    def ps_t():
        i = psc[1] % 2
        psc[1] += 1
        return PTB[i], [ptb_res[i]]

    ds_ring = [S.dsem("ds_ring%d" % i) for i in range(4)]
    ds_w1 = [S.dsem("ds_w1_%d" % i) for i in range(2)]
    ds_x = [S.dsem("ds_x%d" % i) for i in range(4)]
    ds_p = [S.dsem("ds_p%d" % i) for i in range(2)]
    ds_misc = S.dsem("ds_misc")
    ds_ln = S.dsem("ds_ln")
    ds_ple = S.dsem("ds_ple")
    ds_out = [S.dsem("ds_out%d" % i) for i in range(4)]
    ds_dbg = S.dsem("ds_dbg")

    pieces = []
    for j in range(4):
        pieces.append([(ewin_d[:, 4608 + j * 128:4608 + (j + 1) * 128], 0),
                       (ewin_d[:, 5120 + j * 128:5120 + (j + 1) * 128], 128)])
    for hp in range(2):
        for g in range(3):
            c0 = g * 512 + hp * 256
            for base in (0, 1536, 3072):
                pieces.append([(ewin_d[:, base + c0:base + c0 + 256], 0)])
    for l in range(2):
        if l == 1:
            for jj in range(4):
                for base in (0, 1024, 2048):
                    pieces.append([(owin_d[:, base + jj * 256:base + (jj + 1) * 256], 0)])
        for f in range(NF):
            pieces.append([(fwin_d[l][:, f * 128:(f + 1) * 128], 0),
                           (fwin_d[l][:, FF + f * 128:FF + (f + 1) * 128], 128)])
    wq = {"k": 0, "issued": 0, "done": 0}

    def _wissue():
        while wq["issued"] < len(pieces) and wq["issued"] - 4 < wq["done"]:
            m = wq["issued"]
            i = m % 4
            fns = []
            for (src, co) in pieces[m]:
                n = src.shape[1]
                fns.append(lambda e, src=src, co=co, n=n, i=i: e.dma_start(
                    out=ring[i][:, :, co:co + n], in_=src.rearrange("(kc p) n -> p kc n", p=128)))
            S.dma("pool", fns, ds_ring[i], writes=[r_ring[i]])
            wq["issued"] += 1

    def wnext():
        k = wq["k"]
        wq["k"] += 1
        _wissue()
        assert wq["issued"] > k
        return ring[k % 4], r_ring[k % 4]

    def wdone():
        wq["done"] += 1
        _wissue()

    S.op("pool", lambda e: e.memset(identf[:], 1.0), writes=[r_identf])
    S.op("pool", lambda e: e.affine_select(out=identf[:], in_=identf[:], pattern=[[-1, 128]],
                                           compare_op=ALU.is_equal, fill=0.0, base=0, channel_multiplier=1),
         rw=[r_identf])
    S.op("pool", lambda e: e.memset(maskf[:], 1.0), writes=[r_maskf])
    S.op("pool", lambda e: e.affine_select(out=maskf[:, 0, :], in_=maskf[:, 0, :], pattern=[[1, 128]],
                                           compare_op=ALU.is_ge, fill=0.0, base=0, channel_multiplier=-1),
         rw=[r_maskf])
    S.op("pool", lambda e: e.affine_select(out=maskf[:, 1, :], in_=maskf[:, 1, :], pattern=[[-1, 128]],
                                           compare_op=ALU.is_ge, fill=0.0, base=0, channel_multiplier=1),
         rw=[r_maskf])
    S.op("dve", lambda e: e.tensor_copy(out=ident[:], in_=identf[:]), reads=[r_identf], writes=[r_ident])
    S.op("dve", lambda e: e.tensor_copy(out=masks[:], in_=maskf[:]), reads=[r_maskf], writes=[r_masks])
    S.op("dve", lambda e: e.memset(ones[:], 1.0), writes=[r_ones])
    S.op("dve", lambda e: e.memset(hTc[:, :, 0:32], 0.0), writes=[r_hTc])
    S.dma("sp", [
        lambda e: e.dma_start(out=ccw[:], in_=ccw_d),
        lambda e: e.dma_start(out=ccb[:], in_=ccb_d),
        lambda e: e.dma_start(out=clg[:], in_=clg_d),
        lambda e: e.dma_start(out=clb[:], in_=clb_d),
        lambda e: e.dma_start(out=ocw[:], in_=ocw_d),
    ], ds_misc, writes=r_prm)
    S.dma("sp", [lambda e: e.dma_start(out=ropeC[:], in_=ropec_d),
                 lambda e: e.dma_start(out=ropeS[:], in_=ropes_d)], ds_misc, writes=[r_ropeC, r_ropeS])

    def transpose_to(src_bf, r_src, dst_ap_fn, r_dst, nchunk, evac_eng="dve"):
        pt, rpt = ps_t()
        fns = [lambda e, c=c, pt=pt: e.transpose(out=pt[:, c * 128:(c + 1) * 128], in_=src_bf[:, c * 128:(c + 1) * 128],
                                                 identity=ident[:]) for c in range(nchunk)]
        S.mm(fns, reads=[r_src, r_ident], writes=rpt)
        src = pt[:, 0:nchunk * 128].rearrange("p (c n) -> p c n", c=nchunk)
        if evac_eng == "dve":
            return S.op("dve", lambda e: e.tensor_copy(out=dst_ap_fn(), in_=src), reads=rpt, writes=[r_dst])
        else:
            return S.op("act", lambda e: e.activation(out=dst_ap_fn(), in_=src, func=AF.Copy), reads=rpt, writes=[r_dst])

    def dbg_out(name, tensor_ap, shape, reads):
        d = nc.dram_tensor("dbg_" + name, list(shape), tensor_ap.dtype, kind="ExternalOutput").ap()
        dbg_d[name] = d
        S.dma("sp", [lambda e: e.dma_start(out=d, in_=tensor_ap)], ds_dbg, reads=reads)

    def finish():
        toks = []
        for ds in ds_out + [ds_dbg]:
            if ds.n:
                toks.append((ds.sem, 16 * ds.n, id(ds)))
        S.wait("sp", toks)
        S.emit()
        return nc

    def ln_stats(t, k2):
        st = stt[k2]
        S.op("dve", lambda e: e.bn_stats(out=st[:, 0, :], in_=x_tok[:, t, 0:512]), reads=[r_xtok[t]], writes=[r_st[k2]])
        S.op("dve", lambda e: e.bn_stats(out=st[:, 1, :], in_=x_tok[:, t, 512:1024]), reads=[r_xtok[t]], writes=[r_st[k2]])
        S.op("dve", lambda e: e.bn_aggr(out=mvall[:, t, 0:2], in_=st[:]), reads=[r_st[k2]], writes=[r_mvall[t]])
        S.op("dve", lambda e: e.tensor_scalar_mul(out=mvall[:, t, 3:4], in0=mvall[:, t, 0:1], scalar1=-1.0), rw=[r_mvall[t]])

    def ln_rstd_tile(t):
        S.op("act", lambda e: e.activation(out=mvall[:, t, 2:3], in_=mvall[:, t, 1:2], func=AF.Ln, bias=EPS), rw=[r_mvall[t]])
        S.op("act", lambda e: e.activation(out=mvall[:, t, 2:3], in_=mvall[:, t, 2:3], func=AF.Exp, scale=-0.5), rw=[r_mvall[t]])
        S.op("act", lambda e: e.activation(out=mvall[:, t, 3:4], in_=mvall[:, t, 3:4], func=AF.Copy, scale=mvall[:, t, 2:3]),
             rw=[r_mvall[t]])

    def ln_rstd_all():
        S.op("act", lambda e: e.activation(out=mvall[:, :, 2], in_=mvall[:, :, 1], func=AF.Sqrt, bias=EPS), rw=r_mvall)
        S.op("dve", lambda e: e.reciprocal(out=mvall[:, :, 2], in_=mvall[:, :, 2]), rw=r_mvall)
        S.op("dve", lambda e: e.tensor_tensor(out=mvall[:, :, 3], in0=mvall[:, :, 3], in1=mvall[:, :, 2], op=ALU.mult), rw=r_mvall)

    def ln_apply(t):
        xt = x_tok[:, t, :]
        S.op("act", lambda e: e.activation(out=xt, in_=xt, func=AF.Identity, scale=mvall[:, t, 2:3], bias=mvall[:, t, 3:4]),
             reads=[r_mvall[t]], rw=[r_xtok[t]])

    def ln_affine(t, lnp, r_lnp, eg="dve", eb="pool"):
        xt = x_tok[:, t, :]
        S.op(eg, lambda e: e.tensor_tensor(out=xt, in0=xt, in1=lnp[:, 0, :], op=ALU.mult),
             reads=[r_lnp], rw=[r_xtok[t]])
        S.op(eb, lambda e: e.tensor_tensor(out=xt, in0=xt, in1=lnp[:, 1, :], op=ALU.add),
             reads=[r_lnp], rw=[r_xtok[t]])

    def pipeline(stages, n, pre_step=None):
        ns = len(stages)
        for s_ in range(n + ns - 1):
            if pre_step is not None:
                pre_step(s_)
            for k_, f_ in enumerate(stages):
                t_ = s_ - k_
                if 0 <= t_ < n:
                    f_(t_)

    def cast_and_transpose(t, dst_fn, r_dst, k2, evac_eng="dve"):
        cb = castt[k2]
        S.op("act", lambda e: e.activation(out=cb[:], in_=x_tok[:, t, :], func=AF.Copy),
             reads=[r_xtok[t]], writes=[r_castt[k2]])
        transpose_to(cb, r_castt[k2], dst_fn, r_dst, 8, evac_eng)

    def load_lnp(lnp, r_lnp, g_d, b_d, l):
        S.dma("sp", [lambda e: e.dma_start(out=lnp[:, 0, :], in_=g_d[l:l + 1, :].broadcast_to([128, D])),
                     lambda e: e.dma_start(out=lnp[:, 1, :], in_=b_d[l:l + 1, :].broadcast_to([128, D]))],
              ds_ln, writes=[r_lnp])

    for t in range(NT):
        S.dma("sp", [lambda e, t=t: e.dma_start(out=x_tok[:, t, :], in_=x_d[t * 128:(t + 1) * 128, :])],
              ds_x[2 + (t % 2)], writes=[r_xtok[t]])
    stg = [(castt[0], r_castt[0]), (castt[1], r_castt[1]), (stg_a[0], r_stg_a[0]), (stg_a[1], r_stg_a[1])]

    def p0_cast(t):
        cb, r_cb = stg[t % 4]
        if t % 2 == 0:
            S.op("act", lambda e: e.activation(out=cb[:], in_=x_tok[:, t, :], func=AF.Copy),
                 reads=[r_xtok[t]], writes=[r_cb])
        else:
            S.op("dve", lambda e: e.tensor_copy(out=cb[:], in_=x_tok[:, t, :]), reads=[r_xtok[t]], writes=[r_cb])

    p0_cast(0)
    p0_cast(1)
    for t in range(NT):
        if t + 2 < NT:
            p0_cast(t + 2)
        cb, r_cb = stg[t % 4]
        transpose_to(cb, r_cb, lambda t=t: xT[:, :, t * 128:(t + 1) * 128], r_xT[t // 4], 8,
                     "dve" if t % 2 == 0 else "act")

    for j in range(4):
        slot, rs = wnext()
        for tc in range(4):
            k2 = (j * 4 + tc) % 2
            ba, ra = ps_half()
            bg, rg = ps_half()
            fa = [lambda e, d=d, ba=ba, tc=tc, slot=slot: e.matmul(ba, lhsT=slot[:, d, 0:128], rhs=xT[:, d, tc * 512:(tc + 1) * 512],
                                                                   start=(d == 0), stop=(d == 7)) for d in range(8)]
            S.mm(fa, reads=[rs, r_xT[tc]], writes=ra)
            fg = [lambda e, d=d, bg=bg, tc=tc, slot=slot: e.matmul(bg, lhsT=slot[:, d, 128:256], rhs=xT[:, d, tc * 512:(tc + 1) * 512],
                                                                   start=(d == 0), stop=(d == 7)) for d in range(8)]
            S.mm(fg, reads=[rs, r_xT[tc]], writes=rg)
            S.op("act", lambda e, bg=bg, k2=k2: e.activation(out=sgt[k2][:], in_=bg, func=AF.Sigmoid),
                 reads=rg, writes=[r_sgt[k2]])
            S.op("dve", lambda e, ba=ba, k2=k2, j=j, tc=tc: e.tensor_tensor(
                out=hTc[:, j, 32 + tc * 512:32 + (tc + 1) * 512], in0=ba, in1=sgt[k2][:], op=ALU.mult),
                reads=ra + [r_sgt[k2]], writes=[r_hTc])
        wdone()

    if stop_after == "glu":
        dbg_out("hTc", hTc[:], [128, 4, HP], [r_hTc])
        dbg_out("xT", xT[:], [128, 8, T], r_xT)
        return finish(), dbg_d

    conv_groups = [(j, tc) for j in range(4) for tc in range(4)]
    cg = {"i": 0, "tap": 0, "n": 0}
    conv_banks = [(PD[2][:, 0:512], [pd_res[2][0]]), (PD[2][:, 512:1024], [pd_res[2][1]])]

    def build_dg(j):
        S.op("pool", lambda e, j=j: e.tensor_tensor(
            out=dgb[j % 2][:], in0=ident[:].unsqueeze(1).broadcast_to([128, 31, 128]),
            in1=ccw[:, j, :].unsqueeze(2).broadcast_to([128, 31, 128]), op=ALU.mult),
            reads=[r_ident, r_ccw], writes=[r_dgb[j % 2]])

    def conv_emit(n):
        while n > 0 and cg["i"] < len(conv_groups):
            j, tc = conv_groups[cg["i"]]
            bk, rb = conv_banks[cg["i"] % 2]
            tap = cg["tap"]
            fn = lambda e, tap=tap, bk=bk, j=j, tc=tc: e.matmul(
                bk, lhsT=dgb[j % 2][:, tap, :], rhs=hTc[:, j, tc * 512 + 2 + tap:tc * 512 + 2 + tap + 512],
                start=(tap == 0), stop=(tap == 30))
            if tap < 30:
                S.mm_nosig([fn], reads=[r_dgb[j % 2], r_hTc], writes=(rb if tap == 0 else ()))
                cg["tap"] += 1
            else:
                S.mm([fn], reads=[r_dgb[j % 2], r_hTc], writes=rb)
                S.op("act", lambda e, bk=bk, j=j, tc=tc: e.activation(out=convT[:, j, tc * 512:(tc + 1) * 512], in_=bk,
                                                                    func=AF.Identity, bias=ccb[:, j:j + 1]),
                     reads=rb + [r_ccb], writes=[r_convT4[tc]])
                cg["tap"] = 0
                cg["i"] += 1
                if tc == 3 and j + 2 < 4:
                    build_dg(j + 2)
            n -= 1

    build_dg(0)
    build_dg(1)
    psn[0] = 4
    ri = [0]
    for hp in range(2):
        for g in range(3):
            dil = DILS[g]
            L = T // dil
            nb = L // 128
            wq_, rwq = wnext()
            wk_, rwk = wnext()
            wv, rwv = wnext()
            J = 512 // dil
            for (w_, rw_, dstT, r_dst) in ((wq_, rwq, qT, r_qT), (wk_, rwk, kT, r_kT)):
                for h in range(2):
                    for tc in range(4):
                        k2 = ri[0] % 2
                        ri[0] += 1
                        bk, rb = ps_half()
                        fns = [lambda e, d=d, bk=bk, w_=w_, h=h, tc=tc: e.matmul(
                            bk, lhsT=w_[:, d, h * 128:(h + 1) * 128], rhs=xT[:, d, tc * 512:(tc + 1) * 512],
                            start=(d == 0), stop=(d == 7)) for d in range(8)]
                        S.mm(fns, reads=[rw_, r_xT[tc]], writes=rb)
                        t1, t2 = rtmp1[k2], rtmp2[k2]
                        S.op("dve", lambda e, bk=bk, t1=t1, tc=tc: e.tensor_tensor(
                            out=t1[:], in0=bk, in1=ropeC[:, tc * 512:(tc + 1) * 512], op=ALU.mult),
                            reads=rb + [r_ropeC], writes=[r_rtmp1[k2]])
                        S.op("dve", lambda e, bk=bk, t2=t2, tc=tc: e.tensor_tensor(
                            out=t2[0:64, :], in0=bk[64:128, :], in1=ropeS[64:128, tc * 512:(tc + 1) * 512], op=ALU.mult),
                            reads=rb + [r_ropeS], writes=[r_rtmp2[k2]])
                        S.op("dve", lambda e, bk=bk, t2=t2, tc=tc: e.tensor_tensor(
                            out=t2[64:128, :], in0=bk[0:64, :], in1=ropeS[0:64, tc * 512:(tc + 1) * 512], op=ALU.mult),
                            reads=rb + [r_ropeS], writes=[r_rtmp2[k2]])
                        j0 = tc * 512 // dil
                        S.op("pool", lambda e, t1=t1, t2=t2, dstT=dstT, h=h, dil=dil, j0=j0, J=J: e.tensor_tensor(
                            out=dstT[:, h, :].rearrange("p (r j) -> p r j", r=dil)[:, :, j0:j0 + J],
                            in0=t1[:].rearrange("p (j r) -> p r j", r=dil),
                            in1=t2[:].rearrange("p (j r) -> p r j", r=dil), op=ALU.add),
                            reads=[r_rtmp1[k2], r_rtmp2[k2]], writes=[r_dst])
                wdone()
            for blk in range(16):
                r_, n_ = divmod(blk, nb)
                st0 = n_ * 128 * dil + r_
                bk, rb = ps_half()
                fns = [lambda e, d=d, bk=bk, st0=st0, dil=dil, wv=wv: e.matmul(
                    bk[:, 0:256], lhsT=xT[:, d, st0:st0 + 127 * dil + 1:dil], rhs=wv[:, d, :],
                    start=(d == 0), stop=(d == 7)) for d in range(8)]
                S.mm(fns, reads=[rwv] + r_xT, writes=rb)
                S.op("act", lambda e, bk=bk, blk=blk: e.activation(out=Vg[:, blk, :], in_=bk[:, 0:256], func=AF.Copy),
                     reads=rb, writes=[r_Vg])
            wdone()
            for blk in range(16):
                r_, n_ = divmod(blk, nb)
                hasp = n_ > 0
                cc = blk * 128
                cp = (blk - 1) * 128
                k2 = blk % 2
                bS, rS = ps_half()
                fns = []
                for h in range(2):
                    fns.append(lambda e, h=h, bS=bS, cc=cc: e.matmul(
                        bS[:, h * 128:(h + 1) * 128], lhsT=kT[:, h, cc:cc + 128], rhs=qT[:, h, cc:cc + 128],
                        start=True, stop=True))
                if hasp:
                    for h in range(2):
                        fns.append(lambda e, h=h, bS=bS, cc=cc, cp=cp: e.matmul(
                            bS[:, 256 + h * 128:256 + (h + 1) * 128], lhsT=kT[:, h, cp:cp + 128], rhs=qT[:, h, cc:cc + 128],
                            start=True, stop=True))
                S.mm(fns, reads=[r_qT, r_kT], writes=rS)
                conv_emit(6)
                ncp = 2 if hasp else 1
                ncol = 256 * ncp
                S.op("act", lambda e, bS=bS, k2=k2, ncol=ncol: e.activation(
                    out=PT[k2][:, 0:ncol], in_=bS[:, 0:ncol], func=AF.Exp, scale=SCALE),
                    reads=rS, writes=[r_PT[k2]])
                S.op("dve", lambda e, k2=k2, ncol=ncol, ncp=ncp: e.tensor_tensor(
                    out=PT[k2][:, 0:ncol].rearrange("p (c h q) -> p c h q", c=ncp, h=2),
                    in0=PT[k2][:, 0:ncol].rearrange("p (c h q) -> p c h q", c=ncp, h=2),
                    in1=masks[:, 0:ncp, :].unsqueeze(2).broadcast_to([128, ncp, 2, 128]), op=ALU.mult),
                    reads=[r_masks], rw=[r_PT[k2]])
                bU, rU = ps_half()
                fns = []
                for h in range(2):
                    fns.append(lambda e, h=h, bU=bU, blk=blk, k2=k2, hasp=hasp: e.matmul(
                        bU[:, h * 128:(h + 1) * 128], lhsT=Vg[:, blk, h * 128:(h + 1) * 128],
                        rhs=PT[k2][:, h * 128:(h + 1) * 128], start=True, stop=(not hasp)))
                    if hasp:
                        fns.append(lambda e, h=h, bU=bU, blk=blk, k2=k2: e.matmul(
                            bU[:, h * 128:(h + 1) * 128], lhsT=Vg[:, blk - 1, h * 128:(h + 1) * 128],
                            rhs=PT[k2][:, 256 + h * 128:256 + (h + 1) * 128], start=False, stop=True))
                fns.append(lambda e, bU=bU, k2=k2, hasp=hasp: e.matmul(
                    bU[:, 256:512], lhsT=ones[:], rhs=PT[k2][:, 0:256], start=True, stop=(not hasp)))
                if hasp:
                    fns.append(lambda e, bU=bU, k2=k2: e.matmul(
                        bU[:, 256:512], lhsT=ones[:], rhs=PT[k2][:, 256:512], start=False, stop=True))
                S.mm(fns, reads=[r_Vg, r_PT[k2], r_ones], writes=rU)
                st0 = n_ * 128 * dil + r_
                dst = UD[:, :, :, st0:st0 + 127 * dil + 1:dil]
                src = bU.rearrange("p (u h q) -> p u h q", u=2, h=2)
                if g == 0:
                    S.op("act", lambda e, dst=dst, src=src: e.activation(out=dst, in_=src, func=AF.Copy), reads=rU, writes=[r_UD])
                else:
                    S.op("dve", lambda e, dst=dst, src=src: e.tensor_tensor(out=dst, in0=src, in1=dst, op=ALU.add),
                         reads=rU, rw=[r_UD])
        S.op("act", lambda e: e.activation(out=UD[:, 1, :, :], in_=UD[:, 1, :, :], func=AF.Ln), rw=[r_UD])
        S.op("act", lambda e: e.activation(out=UD[:, 1, :, :], in_=UD[:, 1, :, :], func=AF.Exp, scale=-1.0), rw=[r_UD])
        S.op("dve", lambda e, hp=hp: e.tensor_tensor(out=attnT[:, 2 * hp:2 * hp + 2, :], in0=UD[:, 0, :, :],
                                                     in1=UD[:, 1, :, :], op=ALU.mult),
             reads=[r_UD], writes=[r_attnT])

    if stop_after == "attn":
        dbg_out("attnT", attnT[:], [128, 4, T], [r_attnT])
        dbg_out("hTc", hTc[:], [128, 4, HP], [r_hTc])
        return finish(), dbg_d

    conv_emit(10 ** 6)
    psn[0] = 6

    def conv_ln(tc):
        cs = slice(tc * 512, (tc + 1) * 512)
        for j in range(4):
            S.op("dve", lambda e, j=j, cs=cs: e.tensor_tensor(out=ysq[:, j, :], in0=convT[:, j, cs], in1=convT[:, j, cs], op=ALU.mult),
                 reads=[r_convT4[tc]], writes=[r_ysq])
        bm, rm = ps_half()
        S.mm([lambda e, j=j, bm=bm, cs=cs: e.matmul(bm, lhsT=ones[:], rhs=convT[:, j, cs], start=(j == 0), stop=(j == 3))
              for j in range(4)], reads=[r_convT4[tc], r_ones], writes=rm)
        bq, rq = ps_half()
        S.mm([lambda e, j=j, bq=bq: e.matmul(bq, lhsT=ones[:], rhs=ysq[:, j, :], start=(j == 0), stop=(j == 3))
              for j in range(4)], reads=[r_ysq, r_ones], writes=rq)
        S.op("dve", lambda e, bm=bm: e.tensor_scalar_mul(out=mt[:], in0=bm, scalar1=1.0 / 512), reads=rm, writes=[r_mt])
        S.op("dve", lambda e: e.tensor_tensor(out=vt[:], in0=mt[:], in1=mt[:], op=ALU.mult), reads=[r_mt], writes=[r_vt])
        S.op("dve", lambda e, bq=bq: e.scalar_tensor_tensor(out=vt[:], in0=bq, scalar=1.0 / 512, in1=vt[:],
                                                            op0=ALU.mult, op1=ALU.subtract), reads=rq, rw=[r_vt])
        S.op("act", lambda e: e.activation(out=vt[:], in_=vt[:], func=AF.Ln, bias=EPS), rw=[r_vt])
        S.op("act", lambda e: e.activation(out=vt[:], in_=vt[:], func=AF.Exp, scale=-0.5), rw=[r_vt])
        for j in range(4):
            k2 = j % 2
            S.op("dve", lambda e, j=j, k2=k2, cs=cs: e.tensor_tensor(out=zt[k2][:], in0=convT[:, j, cs], in1=mt[:], op=ALU.subtract),
                 reads=[r_convT4[tc], r_mt], writes=[r_zt[k2]])
            S.op("dve", lambda e, k2=k2: e.tensor_tensor(out=zt[k2][:], in0=zt[k2][:], in1=vt[:], op=ALU.mult),
                 reads=[r_vt], rw=[r_zt[k2]])
            S.op("act", lambda e, j=j, k2=k2, cs=cs: e.activation(
                out=convT[:, j, cs], in_=zt[k2][:], func=AF.Silu, scale=clg[:, j:j + 1], bias=clb[:, j:j + 1]),
                reads=[r_zt[k2], r_clg, r_clb], rw=[r_convT4[tc]])

    if stop_after == "conv":
        for tc in range(4):
            conv_ln(tc)
        dbg_out("convT", convT[:], [128, 4, T], r_convT4)
        dbg_out("attnT", attnT[:], [128, 4, T], [r_attnT])
        return finish(), dbg_d

    def mixer_out_and_ln1(l, cat_fn, r_cat_fn, w_d, lnp1, r_lnp1, x_from_hbm, pre_step=None):
        def xload(t):
            S.dma("sp", [lambda e, t=t: e.dma_start(out=x_tok[:, t, :], in_=x_d[t * 128:(t + 1) * 128, :])],
                  ds_x[2 + (t % 2)], writes=[r_xtok[t]])
        if x_from_hbm:
            for t in range(4):
                xload(t)
        S.dma("pool", [lambda e: e.dma_start(out=Wmix[:], in_=w_d.rearrange("(kc p) n -> p kc n", p=128))],
              ds_w1[0], writes=[r_Wmix])
        load_lnp(lnp1, r_lnp1, lmg_d, lmb_d, l)
        if x_from_hbm:
            for t in range(4, NT):
                xload(t)
        def A1(t):
            k2 = t % 2
            bD, rD = ps_full()
            fns = []
            for half in range(2):
                for f in range(8):
                    fns.append(lambda e, half=half, f=f, bD=bD, t=t: e.matmul(
                        bD[:, half * 512:(half + 1) * 512], lhsT=cat_fn(f, t), rhs=Wmix[:, f, half * 512:(half + 1) * 512],
                        start=(f == 0), stop=(f == 7)))
            S.mm(fns, reads=r_cat_fn(t) + [r_Wmix], writes=rD)
            S.op("dve", lambda e, bD=bD, t=t: e.scalar_tensor_tensor(
                out=x_tok[:, t, :], in0=x_tok[:, t, :], scalar=ALPHA, in1=bD[:], op0=ALU.mult, op1=ALU.add),
                reads=rD, rw=[r_xtok[t]])
            ln_stats(t, k2)

        def A2(t):
            ln_rstd_tile(t)
            ln_apply(t)

        pipeline([A1, A2,
                  lambda t: ln_affine(t, lnp1, r_lnp1, "pool", "pool"),
                  lambda t: cast_and_transpose(t, lambda t=t: xT[:, :, t * 128:(t + 1) * 128], r_xT[t // 4], t % 2, "act")],
                 NT, pre_step)

    for tc_ in range(4):
        conv_ln(tc_)

    def l0_pre(s_):
        pass

    mixer_out_and_ln1(0, lambda f, t: (attnT[:, f, t * 128:(t + 1) * 128] if f < 4 else convT[:, f - 4, t * 128:(t + 1) * 128]),
                      lambda t: [r_attnT, r_convT4[t // 4]], ewout_d, lnp1_L0, r_lnp1_L0, True, l0_pre)

    if stop_after == "ln1":
        for t in range(NT):
            dbg_out("x1_%d" % t, x_tok[:, t, :], [128, D], [r_xtok[t]])
        return finish(), dbg_d

    FG = [(0, 4), (4, 8), (8, 12), (12, 16), (16, 19), (19, 22)]

    def ffn_and_ple(l, last_layer):
        fw_out = fwout_d[l]
        for pi, (f0, f1) in enumerate(FG):
            nf = f1 - f0
            wb = pi % 2
            S.dma("pool", [lambda e, wb=wb, f0=f0, f1=f1, nf=nf: e.dma_start(
                out=Wfo[wb][:, 0:nf, :], in_=fw_out[f0 * 128:f1 * 128, :].rearrange("(kc p) n -> p kc n", p=128))],
                ds_w1[wb], writes=[r_Wfo[wb]])
            for f in range(f0, f1):
                slot, rs = wnext()
                for tc in range(4):
                    k2 = tc % 2
                    bg, rg = ps_half()
                    bu, ru = ps_half()
                    S.mm([lambda e, d=d, bg=bg, tc=tc, slot=slot: e.matmul(
                        bg, lhsT=slot[:, d, 0:128], rhs=xT[:, d, tc * 512:(tc + 1) * 512], start=(d == 0), stop=(d == 7))
                        for d in range(8)], reads=[rs, r_xT[tc]], writes=rg)
                    S.mm([lambda e, d=d, bu=bu, tc=tc, slot=slot: e.matmul(
                        bu, lhsT=slot[:, d, 128:256], rhs=xT[:, d, tc * 512:(tc + 1) * 512], start=(d == 0), stop=(d == 7))
                        for d in range(8)], reads=[rs, r_xT[tc]], writes=ru)
                    S.op("act", lambda e, bg=bg, k2=k2: e.activation(out=sgt[k2][:], in_=bg, func=AF.Silu),
                         reads=rg, writes=[r_sgt[k2]])
                    S.op("dve", lambda e, bu=bu, k2=k2, f=f, f0=f0, tc=tc: e.tensor_tensor(
                        out=hTf[:, f - f0, tc * 512:(tc + 1) * 512], in0=bu, in1=sgt[k2][:], op=ALU.mult),
                        reads=ru + [r_sgt[k2]], writes=[r_hTf])
                wdone()
            if pi == 0:
                load_lnp(lnp2, r_lnp2, lfg_d, lfb_d, l)
                S.dma("pool", [lambda e: e.dma_start(out=Wpg[:], in_=pwg_d[l].rearrange("(kc p) n -> p kc n", p=128)),
                               lambda e: e.dma_start(out=Wpp[:], in_=pwp_d[l].rearrange("(kc p) n -> p kc n", p=128))],
                      ds_ple, writes=[r_Wpg, r_Wpp])
            last = (pi == len(FG) - 1)

            def A1(t, nf=nf, wb=wb, pi=pi, last=last):
                k2 = t % 2
                bD, rD = ps_full()
                fns = []
                for half in range(2):
                    for ff in range(nf):
                        fns.append(lambda e, half=half, ff=ff, bD=bD, t=t, wb=wb, nf=nf: e.matmul(
                            bD[:, half * 512:(half + 1) * 512], lhsT=hTf[:, ff, t * 128:(t + 1) * 128],
                            rhs=Wfo[wb][:, ff, half * 512:(half + 1) * 512], start=(ff == 0), stop=(ff == nf - 1)))
                S.mm(fns, reads=[r_hTf, r_Wfo[wb]], writes=rD)
                if pi == 0:
                    S.op("dve", lambda e, bD=bD, t=t: e.scalar_tensor_tensor(
                        out=x_tok[:, t, :], in0=x_tok[:, t, :], scalar=ALPHA, in1=bD[:], op0=ALU.mult, op1=ALU.add),
                        reads=rD, rw=[r_xtok[t]])
                else:
                    S.op("dve", lambda e, bD=bD, t=t: e.tensor_tensor(
                        out=x_tok[:, t, :], in0=x_tok[:, t, :], in1=bD[:], op=ALU.add), reads=rD, rw=[r_xtok[t]])
                if last:
                    ln_stats(t, k2)

            for t in range(NT):
                A1(t)
            if not last:
                continue
            ln_rstd_all()

            def cast_only(t, cb, r_cb):
                S.op("act", lambda e: e.activation(out=cb[:], in_=x_tok[:, t, :], func=AF.Copy),
                     reads=[r_xtok[t]], writes=[r_cb])

            def ok(t):
                return 0 <= t < NT

            for s_ in range(NT + 6):
                t = s_ - 2
                if ok(t):
                    k2 = t % 2
                    transpose_to(castt[k2], r_castt[k2], lambda k2=k2: x2T[k2][:], r_x2T[k2], 8, "dve")
                t = s_ - 3
                if ok(t):
                    k2 = t % 2
                    transpose_to(pbt[k2], r_pbt[k2], lambda k2=k2: pTt[k2][:], r_pTt[k2], 2, "act")
                t = s_ - 6
                if ok(t) and not last_layer:
                    k2 = t % 2
                    transpose_to(castd[k2], r_castd[k2], lambda t=t: xT[:, :, t * 128:(t + 1) * 128], r_xT[t // 4], 8, "dve")
                t = s_
                if ok(t):
                    ln_apply(t)
                    ln_affine(t, lnp2, r_lnp2, "dve", "dve")
                t = s_ - 1
                if ok(t):
                    k2 = t % 2
                    cast_only(t, castt[k2], r_castt[k2])
                    S.dma("pool", [lambda e, t=t, k2=k2: e.dma_start(out=pbt[k2][:], in_=p_d[l, t * 128:(t + 1) * 128, :])],
                          ds_p[k2], writes=[r_pbt[k2]])
                t = s_ - 5
                if ok(t):
                    if last_layer:
                        S.dma("sp", [lambda e, t=t: e.dma_start(out=out_d[t * 128:(t + 1) * 128, :], in_=x_tok[:, t, :])],
                              ds_out[t % 4], reads=[r_xtok[t]])
                    else:
                        cast_only(t, castd[t % 2], r_castd[t % 2])
                tg = s_ - 3
                bG = rG = None
                if ok(tg):
                    k2 = tg % 2
                    bG, rG = ps_full()
                    fns = []
                    for half in range(2):
                        for c in range(8):
                            fns.append(lambda e, half=half, c=c, bG=bG, k2=k2: e.matmul(
                                bG[:, half * 512:(half + 1) * 512], lhsT=x2T[k2][:, c, :], rhs=Wpg[:, c, half * 512:(half + 1) * 512],
                                start=(c == 0), stop=(c == 7)))
                    S.mm(fns, reads=[r_x2T[k2], r_Wpg], writes=rG)
                t = s_ - 4
                if ok(t):
                    k2 = t % 2
                    sg_ = sigt[k2]
                    bP, rP = ps_full()
                    fns = []
                    for half in range(2):
                        for c in range(2):
                            fns.append(lambda e, half=half, c=c, bP=bP, k2=k2: e.matmul(
                                bP[:, half * 512:(half + 1) * 512], lhsT=pTt[k2][:, c, :], rhs=Wpp[:, c, half * 512:(half + 1) * 512],
                                start=(c == 0), stop=(c == 1)))
                    S.mm(fns, reads=[r_pTt[k2], r_Wpp], writes=rP)
                    S.op("dve", lambda e, bP=bP, sg_=sg_: e.tensor_tensor(out=sg_[:], in0=bP[:], in1=sg_[:], op=ALU.mult),
                         reads=rP, rw=[r_sigt[k2]])
                    S.op("pool", lambda e, t=t, sg_=sg_: e.tensor_tensor(out=x_tok[:, t, :], in0=x_tok[:, t, :], in1=sg_[:], op=ALU.add),
                         reads=[r_sigt[k2]], rw=[r_xtok[t]])
                if ok(tg):
                    k2 = tg % 2
                    sg_ = sigt[k2]
                    S.op("act", lambda e, bG=bG, sg_=sg_: e.activation(out=sg_[:], in_=bG[:], func=AF.Sigmoid),
                         reads=rG, writes=[r_sigt[k2]])

    ffn_and_ple(0, False)

    if stop_after == "l0":
        for t in range(NT):
            dbg_out("x3_%d" % t, x_tok[:, t, :], [128, D], [r_xtok[t]])
        return finish(), dbg_d

    for i in range(2):
        S.op("dve", lambda e, i=i: e.memset(ub[i][:, 0:2], 0.0), writes=[r_ub[i]])
    ui = [0]
    for jj in range(4):
        wb_, rwb = wnext()
        wc_, rwc = wnext()
        wh_, rwh = wnext()
        for j2 in range(2):
            j = 2 * jj + j2
            ubi = ui[0] % 2
            ui[0] += 1
            u = ub[ubi]
            for tc in range(4):
                k2 = tc % 2
                bb, rbb = ps_half()
                bc, rbc = ps_half()
                bh, rbh = ps_half()
                for (bk, rb, w_, rw_) in ((bb, rbb, wb_, rwb), (bc, rbc, wc_, rwc), (bh, rbh, wh_, rwh)):
                    S.mm([lambda e, d=d, bk=bk, w_=w_, j2=j2, tc=tc: e.matmul(
                        bk, lhsT=w_[:, d, j2 * 128:(j2 + 1) * 128], rhs=xT[:, d, tc * 512:(tc + 1) * 512],
                        start=(d == 0), stop=(d == 7)) for d in range(8)], reads=[rw_, r_xT[tc]], writes=rb)
                S.op("act", lambda e, bc=bc, k2=k2: e.activation(out=sgt[k2][:], in_=bc, func=AF.Copy),
                     reads=rbc, writes=[r_sgt[k2]])
                S.op("dve", lambda e, bh=bh, k2=k2, u=u, tc=tc: e.tensor_tensor(
                    out=u[:, 2 + tc * 512:2 + (tc + 1) * 512], in0=bh, in1=sgt[k2][:], op=ALU.mult),
                    reads=rbh + [r_sgt[k2]], rw=[r_ub[ubi]])
                ct = ctmp[k2]
                S.op("dve", lambda e, u=u, ct=ct, tc=tc, j=j: e.tensor_scalar_mul(
                    out=ct[:], in0=u[:, 2 + tc * 512:2 + (tc + 1) * 512], scalar1=ocw[:, j, 2:3]),
                    reads=[r_ub[ubi], r_ocw], writes=[r_ctmp[k2]])
                S.op("dve", lambda e, u=u, ct=ct, tc=tc, j=j: e.scalar_tensor_tensor(
                    out=ct[:], in0=u[:, 1 + tc * 512:1 + (tc + 1) * 512], scalar=ocw[:, j, 1:2], in1=ct[:],
                    op0=ALU.mult, op1=ALU.add), reads=[r_ub[ubi]], rw=[r_ctmp[k2]])
                S.op("dve", lambda e, u=u, ct=ct, tc=tc, j=j: e.scalar_tensor_tensor(
                    out=ct[:], in0=u[:, tc * 512:(tc + 1) * 512], scalar=ocw[:, j, 0:1], in1=ct[:],
                    op0=ALU.mult, op1=ALU.add), reads=[r_ub[ubi]], rw=[r_ctmp[k2]])
                S.op("dve", lambda e, bb=bb, ct=ct, j=j, tc=tc: e.tensor_tensor(
                    out=yT[:, j, tc * 512:(tc + 1) * 512], in0=bb, in1=ct[:], op=ALU.mult),
                    reads=rbb + [r_ctmp[k2]], writes=[r_yT])
        wdone(); wdone(); wdone()

    mixer_out_and_ln1(1, lambda f, t: yT[:, f, t * 128:(t + 1) * 128], lambda t: [r_yT], owout_d, lnp1_L1, r_lnp1_L1, False)
    ffn_and_ple(1, True)
    return finish(), dbg_d


_CACHE = {}


def _rope_tables():
    half = 64
    inv = 10000.0 ** (-np.arange(half, dtype=np.float64) / half)
    pos = np.arange(T, dtype=np.float64)
    ang = pos[:, None] * inv[None, :]
    c = np.cos(ang).T.astype(np.float32)
    s = np.sin(ang).T.astype(np.float32)
    C = np.concatenate([c, c], axis=0)
    SW = np.concatenate([s, -s], axis=0)
    return np.ascontiguousarray(C), np.ascontiguousarray(SW)


def make_in_maps(inputs):
    f = lambda a: np.ascontiguousarray(np.asarray(a, dtype=np.float32))
    C, SW = _rope_tables()
    shared = {
        "even_w_in": f(inputs["even_w_in"])[0],
        "even_w_out": f(inputs["even_w_out"])[0],
        "conf_conv_w": np.ascontiguousarray(f(inputs["conf_conv_w"])[0].reshape(31, 4, 128).transpose(2, 1, 0)),
        "conf_conv_b": np.ascontiguousarray(f(inputs["conf_conv_b"]).reshape(4, 128).T),
        "conf_ln_g": np.ascontiguousarray(f(inputs["conf_ln_g"]).reshape(4, 128).T),
        "conf_ln_b": np.ascontiguousarray(f(inputs["conf_ln_b"]).reshape(4, 128).T),
        "odd_w_in": f(inputs["odd_w_in"])[0],
        "odd_conv_w": np.ascontiguousarray(f(inputs["odd_conv_w"])[0].reshape(3, 8, 128).transpose(2, 1, 0)),
        "odd_w_out": f(inputs["odd_w_out"])[0],
        "ln_mix_g": f(inputs["ln_mix_g"]),
        "ln_mix_b": f(inputs["ln_mix_b"]),
        "ln_ffn_g": f(inputs["ln_ffn_g"]),
        "ln_ffn_b": f(inputs["ln_ffn_b"]),
        "ffn_w_in": f(inputs["ffn_w_in"]),
        "ffn_w_out": f(inputs["ffn_w_out"]),
        "ple_w_proj": f(inputs["ple_w_proj"]),
        "ple_w_gate": f(inputs["ple_w_gate"]),
        "rope_c": C,
        "rope_s": SW,
    }
    x = f(inputs["x"])
    p = f(inputs["p"])
    maps = []
    for b in range(8):
        m = dict(shared)
        m["x"] = x[b]
        m["p"] = np.ascontiguousarray(p[:, b])
        maps.append(m)
    return maps


def kernel(**inputs):
    if "nc" not in _CACHE:
        _CACHE["nc"] = build()[0]
    nc = _CACHE["nc"]
    in_maps = make_in_maps(inputs)
    res = run_bass_kernel_spmd(nc, in_maps, core_ids=list(range(8)))
    out = np.stack([np.asarray(res.results[b]["out"], dtype=np.float32) for b in range(8)], axis=0)
    return out
```

```python
import os
import numpy as np
import concourse.bass as bass
import concourse.mybir as mybir
from concourse.bass_utils import run_bass_kernel_spmd

F32 = mybir.dt.float32
BF16 = mybir.dt.bfloat16
AF = mybir.ActivationFunctionType
ALU = mybir.AluOpType

T = 2048
D = 1024
NT = 16
FF = 2816
NF = 22
ALPHA = float(4 ** 0.25)
EPS = 1e-5
SCALE = float(128 ** -0.5)
DILS = (1, 4, 16)
K = 1024

BASE = 16512
R_A = BASE
R_B = R_A + 64 * K
R_C = R_B + 32 * K
R_C_SZ = 50 * K
R_W1 = R_C + R_C_SZ
RING = R_W1 + 16 * K
MISC = RING + 16 * K


class Res:
    ALL = []

    def __init__(self, name="", ranges=None):
        self.ws = {}
        self.rs = {}
        self.name = name
        self.ranges = ranges
        self.ov = [self]
        if ranges is not None:
            for o in Res.ALL:
                if o.ranges is None:
                    continue
                if any(a0 < b1 and b0 < a1 for (a0, a1) in ranges for (b0, b1) in o.ranges):
                    self.ov.append(o)
                    o.ov.append(self)
            Res.ALL.append(self)


def _merge(d, tok):
    if tok is None:
        return
    sem, val, key = tok
    if key not in d or d[key][1] < val:
        d[key] = tok


class DmaSem:
    def __init__(self, sem):
        self.sem = sem
        self.n = 0


class Sched:
    ENG = ("pe", "act", "dve", "pool", "sp")

    def __init__(self, nc):
        self.nc = nc
        self.q = {e: [] for e in self.ENG}
        self.sem = {e: nc.alloc_semaphore("sem_" + e) for e in self.ENG}
        self.cnt = {e: 0 for e in self.ENG}
        self.waited = {e: {} for e in self.ENG}
        self.nsem = 0
        self.ninst = {e: 0 for e in self.ENG}

    def dsem(self, name=None):
        self.nsem += 1
        return DmaSem(self.nc.alloc_semaphore(name or ("ds%d" % self.nsem)))

    def _wait(self, eng, toks):
        w = self.waited[eng]
        for d in toks:
            if d is None:
                continue
            sem, val, key = d
            if key == eng:
                pass
            if w.get(key, 0) >= val:
                continue
            w[key] = val
            self.q[eng].append(lambda e, sem=sem, val=val: e.wait_ge(sem, val))
            self.ninst[eng] += 1

    def _deps(self, reads, writes, rw, deps):
        toks = list(deps)
        for r in reads:
            for o in r.ov:
                toks.extend(o.ws.values())
        for r in writes:
            for o in r.ov:
                toks.extend(o.rs.values())
                if o is not r:
                    toks.extend(o.ws.values())
        for r in rw:
            for o in r.ov:
                toks.extend(o.ws.values())
                toks.extend(o.rs.values())
        return toks

    def _post(self, tok, reads, writes, rw):
        for r in reads:
            _merge(r.rs, tok)
        for r in writes:
            if r.rs:
                r.ws = {}
                r.rs = {}
            _merge(r.ws, tok)
        for r in rw:
            r.ws = {}
            r.rs = {}
            _merge(r.ws, tok)

    def op(self, eng, fn, reads=(), writes=(), rw=(), deps=()):
        self._wait(eng, self._deps(reads, writes, rw, deps))
        self.cnt[eng] += 1
        sem = self.sem[eng]
        self.q[eng].append(lambda e, fn=fn, sem=sem: fn(e).then_inc(sem, 1))
        self.ninst[eng] += 1
        tok = (sem, self.cnt[eng], eng)
        self._post(tok, reads, writes, rw)
        return tok

    def mm_nosig(self, fns, reads=(), writes=(), deps=()):
        eng = "pe"
        self._wait(eng, self._deps(reads, writes, (), deps))
        for fn in fns:
            self.q[eng].append(lambda e, fn=fn: fn(e))
        self.ninst[eng] += len(fns)

    def mm(self, fns, reads=(), writes=(), deps=()):
        eng = "pe"
        self._wait(eng, self._deps(reads, writes, (), deps))
        for fn in fns[:-1]:
            self.q[eng].append(lambda e, fn=fn: fn(e))
        self.ninst[eng] += len(fns)
        self.cnt[eng] += 1
        sem = self.sem[eng]
        fn = fns[-1]
        self.q[eng].append(lambda e, fn=fn, sem=sem: fn(e).then_inc(sem, 1))
        tok = (sem, self.cnt[eng], eng)
        self._post(tok, reads, writes, ())
        return tok

    def dma(self, eng, fns, ds, reads=(), writes=(), deps=()):
        toks = self._deps(reads, writes, (), deps)
        if ds.n > 0:
            toks.append((ds.sem, 16 * ds.n, id(ds)))
        self._wait(eng, toks)
        for fn in fns:
            self.q[eng].append(lambda e, fn=fn, sem=ds.sem: fn(e).then_inc(sem, 16))
        self.ninst[eng] += len(fns)
        ds.n += len(fns)
        tok = (ds.sem, 16 * ds.n, id(ds))
        self._post(tok, reads, writes, ())
        return tok

    def wait(self, eng, toks):
        self._wait(eng, toks)

    def emit(self):
        nc = self.nc
        q = self.q
        with nc.Block() as block:
            @block.tensor
            def _(e):
                for f in q["pe"]:
                    f(e)

            @block.scalar
            def _(e):
                for f in q["act"]:
                    f(e)

            @block.vector
            def _(e):
                for f in q["dve"]:
                    f(e)

            @block.gpsimd
            def _(e):
                for f in q["pool"]:
                    f(e)

            @block.sync
            def _(e):
                for f in q["sp"]:
                    f(e)


def build(stop_after=None, dbg=False):
    nc = bass.Bass("TRN2", target_bir_lowering=False)
    S = Sched(nc)
    Res.ALL = []

    def din(name, shape):
        return nc.dram_tensor(name, list(shape), F32, kind="ExternalInput").ap()

    x_d = din("x", [T, D])
    p_d = din("p", [2, T, 256])
    ewin_d = din("even_w_in", [D, 5632])
    ewout_d = din("even_w_out", [D, D])
    ccw_d = din("conf_conv_w", [128, 4, 31])
    ccb_d = din("conf_conv_b", [128, 4])
    clg_d = din("conf_ln_g", [128, 4])
    clb_d = din("conf_ln_b", [128, 4])
    owin_d = din("odd_w_in", [D, 3072])
    ocw_d = din("odd_conv_w", [128, 8, 3])
    owout_d = din("odd_w_out", [D, D])
    lmg_d = din("ln_mix_g", [2, D])
    lmb_d = din("ln_mix_b", [2, D])
    lfg_d = din("ln_ffn_g", [2, D])
    lfb_d = din("ln_ffn_b", [2, D])
    fwin_d = din("ffn_w_in", [2, D, 2 * FF])
    fwout_d = din("ffn_w_out", [2, FF, D])
    pwp_d = din("ple_w_proj", [2, 256, D])
    pwg_d = din("ple_w_gate", [2, D, D])
    ropec_d = din("rope_c", [128, T])
    ropes_d = din("rope_s", [128, T])
    out_d = nc.dram_tensor("out", [T, D], F32, kind="ExternalOutput").ap()
    dbg_d = {}

    def sb(name, shape, dt, off):
        n = int(np.prod(shape[1:])) * (4 if dt == F32 else 2)
        t_ = nc.alloc_sbuf_tensor_at(name, list(shape), dt, offset=off)
        return t_, Res(name, [(off, off + n)])

    x_tok = nc.alloc_sbuf_tensor_at("x_tok", [128, NT, D], F32, offset=R_A)
    r_xtok = [Res("xtok%d" % t, [(R_A + t * 4096, R_A + (t + 1) * 4096)]) for t in range(NT)]
    UD, r_UD = sb("UD", [128, 2, 2, T], F32, R_A)
    qT, r_qT = sb("qT", [128, 2, T], BF16, R_A + 32 * K)
    kT, r_kT = sb("kT", [128, 2, T], BF16, R_A + 40 * K)
    Vg, r_Vg = sb("Vg", [128, 16, 256], BF16, R_A + 48 * K)
    rtmp1, r_rtmp1, rtmp2, r_rtmp2 = [], [], [], []
    for i in range(2):
        a_, b_ = sb("rtmp1_%d" % i, [128, 512], F32, R_A + 56 * K + i * 2 * K)
        rtmp1.append(a_); r_rtmp1.append(b_)
        a_, b_ = sb("rtmp2_%d" % i, [128, 512], F32, R_A + 60 * K + i * 2 * K)
        rtmp2.append(a_); r_rtmp2.append(b_)
    xT = nc.alloc_sbuf_tensor_at("xT", [128, 8, T], BF16, offset=R_B)
    r_xT = [Res("xT%d" % c, [(R_B + d * 4096 + c * 1024, R_B + d * 4096 + (c + 1) * 1024) for d in range(8)])
            for c in range(4)]
    HP = 2080
    hTc, r_hTc = sb("hTc", [128, 4, HP], BF16, R_C)
    attnT, r_attnT = sb("attnT", [128, 4, T], BF16, R_C + 16640)
    convT, r_convT = sb("convT", [128, 4, T], BF16, R_C + 16640 + 16 * K)
    ropeC, r_ropeC = sb("ropeC", [128, T], F32, R_W1)
    ropeS, r_ropeS = sb("ropeS", [128, T], F32, R_W1 + 8 * K)
    r_convT4 = [Res("convT_tc%d" % c, [(R_C + 16640 + 16 * K + j * 4096 + c * 1024,
                                        R_C + 16640 + 16 * K + j * 4096 + (c + 1) * 1024) for j in range(4)])
                for c in range(4)]
    mt, r_mt = sb("mt", [128, 512], F32, R_C + 8 * K)
    vt, r_vt = sb("vt", [128, 512], F32, R_C + 10 * K)
    zt, r_zt = [], []
    for i in range(2):
        a_, b_ = sb("zt%d" % i, [128, 512], F32, R_C + 12 * K + i * 2 * K)
        zt.append(a_); r_zt.append(b_)
    hTf, r_hTf = sb("hTf", [128, 4, T], BF16, R_C)
    r_hTf4 = [Res("hTf_tc%d" % c, [(R_C + ff * 4096 + c * 1024, R_C + ff * 4096 + (c + 1) * 1024) for ff in range(4)])
              for c in range(4)]
    Wpg, r_Wpg = sb("Wpg", [128, 8, D], BF16, R_C + 16 * K)
    Wpp, r_Wpp = sb("Wpp", [128, 2, D], BF16, R_C + 32 * K)
    lnp2, r_lnp2 = sb("lnp2", [128, 2, D], F32, R_C + 36 * K)
    lnp1_L0, r_lnp1_L0 = sb("lnp1a", [128, 2, D], F32, R_C)
    sigt1, r_sigt1 = sb("sigt1", [128, D], F32, R_C + 44 * K)
    yT, r_yT = sb("yT", [128, 8, T], BF16, R_C)
    ub, r_ub = [], []
    for i in range(2):
        a_, b_ = sb("ub%d" % i, [128, 2 + T], F32, R_C + 32 * K + i * 8224)
        ub.append(a_); r_ub.append(b_)
    lnp1_L1, r_lnp1_L1 = sb("lnp1b", [128, 2, D], F32, R_C + 32 * K)
    assert 32 * K + 2 * 8224 <= R_C_SZ
    Wmix, r_Wmix = sb("Wmix", [128, 8, D], BF16, R_W1)
    Wfo, r_Wfo = [], []
    for i in range(2):
        a_, b_ = sb("Wfo%d" % i, [128, 4, D], BF16, R_W1 + i * 8 * K)
        Wfo.append(a_); r_Wfo.append(b_)
    castd, r_castd = [], []
    for i in range(2):
        a_, b_ = sb("castd%d" % i, [128, D], BF16, R_W1 + i * 2 * K)
        castd.append(a_); r_castd.append(b_)
    ring, r_ring = [], []
    for i in range(4):
        a_, b_ = sb("ring%d" % i, [128, 8, 256], BF16, RING + i * 4 * K)
        ring.append(a_); r_ring.append(b_)
    mo = [MISC]

    def misc(name, shape, dt):
        n = int(np.prod(shape[1:])) * (4 if dt == F32 else 2)
        n = (n + 31) // 32 * 32
        t_, r_ = sb(name, shape, dt, mo[0])
        mo[0] += n
        return t_, r_

    def misc2(name, shape, dt):
        ts, rs = [], []
        for i in range(2):
            a_, b_ = misc("%s%d" % (name, i), shape, dt)
            ts.append(a_); rs.append(b_)
        return ts, rs

    ident, r_ident = misc("ident", [128, 128], BF16)
    identf, r_identf = misc("identf", [128, 128], F32)
    ones, r_ones = misc("ones", [128, 128], BF16)
    masks, r_masks = misc("masks", [128, 2, 128], BF16)
    maskf, r_maskf = misc("maskf", [128, 2, 128], F32)
    ccw, r_ccw = misc("ccw", [128, 4, 31], F32)
    ccb, r_ccb = misc("ccb", [128, 4], F32)
    clg, r_clg = misc("clg", [128, 4], F32)
    clb, r_clb = misc("clb", [128, 4], F32)
    ocw, r_ocw = misc("ocw", [128, 8, 3], F32)
    r_prm = [r_ccw, r_ccb, r_clg, r_clb, r_ocw]
    off_dg0 = mo[0]
    sgt, r_sgt = misc2("sgt", [128, 512], F32)
    ctmp, r_ctmp = misc2("ctmp", [128, 512], F32)
    off_dg1 = mo[0]
    castt, r_castt = misc2("castt", [128, D], BF16)
    x2T, r_x2T = misc2("x2T", [128, 8, 128], BF16)
    assert off_dg1 - off_dg0 == 8 * K and mo[0] - off_dg1 == 8 * K
    stg_a, r_stg_a = [], []
    for i in range(2):
        a_, b_ = sb("stg%d" % i, [128, D], BF16, off_dg1 + 4 * K + i * 2 * K)
        stg_a.append(a_); r_stg_a.append(b_)
    ysq, r_ysq = sb("ysq", [128, 4, 512], BF16, off_dg0)
    dgb, r_dgb = [], []
    for i, o_ in enumerate((off_dg0, off_dg1)):
        a_, b_ = sb("dgb%d" % i, [128, 31, 128], BF16, o_)
        dgb.append(a_); r_dgb.append(b_)
    pbt, r_pbt = misc2("pbt", [128, 256], BF16)
    pTt, r_pTt = misc2("pTt", [128, 2, 128], BF16)
    sigt0, r_sigt0 = misc("sigt", [128, D], F32)
    PT, r_PT = misc2("PT", [128, 512], BF16)
    sigt = [sigt0, sigt1]
    r_sigt = [r_sigt0, r_sigt1]
    stt, r_st = misc2("stt", [128, 2, 6], F32)
    mvall, _r = misc("mvall", [128, NT, 4], F32)
    r_mvall = [Res("mvall%d" % t) for t in range(NT)]
    assert mo[0] <= 229344, mo[0]

    PD = [nc.alloc_psum_tensor("pd%d" % i, [128, 1024], F32) for i in range(3)]
    PTB = [nc.alloc_psum_tensor("ptb%d" % i, [128, 1024], BF16) for i in range(2)]
    pd_res = [[Res("pd%d_%d" % (i, h)) for h in range(2)] for i in range(3)]
    ptb_res = [Res("ptb%d" % i) for i in range(2)]
    psc = [0, 0]
    psn = [6]

    def ps_half():
        i = psc[0] % psn[0]
        psc[0] += 1
        return PD[i // 2][:, (i % 2) * 512:(i % 2) * 512 + 512], [pd_res[i // 2][i % 2]]

    def ps_full():
        if psc[0] % 2 == 1:
            psc[0] += 1
        i = psc[0] % psn[0]
        psc[0] += 2
        return PD[i // 2], pd_res[i // 2]

    def ps_t():
        i = psc[1] % 2
        psc[1] += 1
        return PTB[i], [ptb_res[i]]

    ds_ring = [S.dsem("ds_ring%d" % i) for i in range(4)]
    ds_w1 = [S.dsem("ds_w1_%d" % i) for i in range(2)]
    ds_x = [S.dsem("ds_x%d" % i) for i in range(4)]
    ds_p = [S.dsem("ds_p%d" % i) for i in range(2)]
    ds_misc = S.dsem("ds_misc")
    ds_ln = S.dsem("ds_ln")
    ds_ple = S.dsem("ds_ple")
    ds_out = [S.dsem("ds_out%d" % i) for i in range(4)]
    ds_dbg = S.dsem("ds_dbg")

    pieces = []
    for j in range(4):
        pieces.append([(ewin_d[:, 4608 + j * 128:4608 + (j + 1) * 128], 0),
                       (ewin_d[:, 5120 + j * 128:5120 + (j + 1) * 128], 128)])
    for hp in range(2):
        for g in range(3):
            c0 = g * 512 + hp * 256
            for base in (0, 1536, 3072):
                pieces.append([(ewin_d[:, base + c0:base + c0 + 256], 0)])
    for l in range(2):
        if l == 1:
            for jj in range(4):
                for base in (0, 1024, 2048):
                    pieces.append([(owin_d[:, base + jj * 256:base + (jj + 1) * 256], 0)])
        for f in range(NF):
            pieces.append([(fwin_d[l][:, f * 128:(f + 1) * 128], 0),
                           (fwin_d[l][:, FF + f * 128:FF + (f + 1) * 128], 128)])
    wq = {"k": 0, "issued": 0, "done": 0}

    def _wissue():
        while wq["issued"] < len(pieces) and wq["issued"] - 4 < wq["done"]:
            m = wq["issued"]
            i = m % 4
            fns = []
            for (src, co) in pieces[m]:
                n = src.shape[1]
                fns.append(lambda e, src=src, co=co, n=n, i=i: e.dma_start(
                    out=ring[i][:, :, co:co + n], in_=src.rearrange("(kc p) n -> p kc n", p=128)))
            S.dma("pool", fns, ds_ring[i], writes=[r_ring[i]])
            wq["issued"] += 1

    def wnext():
        k = wq["k"]
        wq["k"] += 1
        _wissue()
        assert wq["issued"] > k
        return ring[k % 4], r_ring[k % 4]

    def wdone():
        wq["done"] += 1
        _wissue()

    S.op("pool", lambda e: e.memset(identf[:], 1.0), writes=[r_identf])
    S.op("pool", lambda e: e.affine_select(out=identf[:], in_=identf[:], pattern=[[-1, 128]],
                                           compare_op=ALU.is_equal, fill=0.0, base=0, channel_multiplier=1),
         rw=[r_identf])
    S.op("pool", lambda e: e.memset(maskf[:], 1.0), writes=[r_maskf])
    S.op("pool", lambda e: e.affine_select(out=maskf[:, 0, :], in_=maskf[:, 0, :], pattern=[[1, 128]],
                                           compare_op=ALU.is_ge, fill=0.0, base=0, channel_multiplier=-1),
         rw=[r_maskf])
    S.op("pool", lambda e: e.affine_select(out=maskf[:, 1, :], in_=maskf[:, 1, :], pattern=[[-1, 128]],
                                           compare_op=ALU.is_ge, fill=0.0, base=0, channel_multiplier=1),
         rw=[r_maskf])
    S.op("dve", lambda e: e.tensor_copy(out=ident[:], in_=identf[:]), reads=[r_identf], writes=[r_ident])
    S.op("dve", lambda e: e.tensor_copy(out=masks[:], in_=maskf[:]), reads=[r_maskf], writes=[r_masks])
    S.op("dve", lambda e: e.memset(ones[:], 1.0), writes=[r_ones])
    S.op("dve", lambda e: e.memset(hTc[:, :, 0:32], 0.0), writes=[r_hTc])
    S.dma("sp", [
        lambda e: e.dma_start(out=ccw[:], in_=ccw_d),
        lambda e: e.dma_start(out=ccb[:], in_=ccb_d),
        lambda e: e.dma_start(out=clg[:], in_=clg_d),
        lambda e: e.dma_start(out=clb[:], in_=clb_d),
        lambda e: e.dma_start(out=ocw[:], in_=ocw_d),
    ], ds_misc, writes=r_prm)
    S.dma("sp", [lambda e: e.dma_start(out=ropeC[:], in_=ropec_d),
                 lambda e: e.dma_start(out=ropeS[:], in_=ropes_d)], ds_misc, writes=[r_ropeC, r_ropeS])

    def transpose_to(src_bf, r_src, dst_ap_fn, r_dst, nchunk, evac_eng="dve"):
        pt, rpt = ps_t()
        fns = [lambda e, c=c, pt=pt: e.transpose(out=pt[:, c * 128:(c + 1) * 128], in_=src_bf[:, c * 128:(c + 1) * 128],
                                                 identity=ident[:]) for c in range(nchunk)]
        S.mm(fns, reads=[r_src, r_ident], writes=rpt)
        src = pt[:, 0:nchunk * 128].rearrange("p (c n) -> p c n", c=nchunk)
        if evac_eng == "dve":
            return S.op("dve", lambda e: e.tensor_copy(out=dst_ap_fn(), in_=src), reads=rpt, writes=[r_dst])
        else:
            return S.op("act", lambda e: e.activation(out=dst_ap_fn(), in_=src, func=AF.Copy), reads=rpt, writes=[r_dst])

    def dbg_out(name, tensor_ap, shape, reads):
        d = nc.dram_tensor("dbg_" + name, list(shape), tensor_ap.dtype, kind="ExternalOutput").ap()
        dbg_d[name] = d
        S.dma("sp", [lambda e: e.dma_start(out=d, in_=tensor_ap)], ds_dbg, reads=reads)

    def finish():
        toks = []
        for ds in ds_out + [ds_dbg]:
            if ds.n:
                toks.append((ds.sem, 16 * ds.n, id(ds)))
        S.wait("sp", toks)
        S.emit()
        return nc

    def ln_stats(t, k2):
        st = stt[k2]
        S.op("dve", lambda e: e.bn_stats(out=st[:, 0, :], in_=x_tok[:, t, 0:512]), reads=[r_xtok[t]], writes=[r_st[k2]])
        S.op("dve", lambda e: e.bn_stats(out=st[:, 1, :], in_=x_tok[:, t, 512:1024]), reads=[r_xtok[t]], writes=[r_st[k2]])
        S.op("dve", lambda e: e.bn_aggr(out=mvall[:, t, 0:2], in_=st[:]), reads=[r_st[k2]], writes=[r_mvall[t]])
        S.op("dve", lambda e: e.tensor_scalar_mul(out=mvall[:, t, 3:4], in0=mvall[:, t, 0:1], scalar1=-1.0), rw=[r_mvall[t]])

    def ln_rstd_tile(t):
        S.op("act", lambda e: e.activation(out=mvall[:, t, 2:3], in_=mvall[:, t, 1:2], func=AF.Ln, bias=EPS), rw=[r_mvall[t]])
        S.op("act", lambda e: e.activation(out=mvall[:, t, 2:3], in_=mvall[:, t, 2:3], func=AF.Exp, scale=-0.5), rw=[r_mvall[t]])
        S.op("act", lambda e: e.activation(out=mvall[:, t, 3:4], in_=mvall[:, t, 3:4], func=AF.Copy, scale=mvall[:, t, 2:3]),
             rw=[r_mvall[t]])

    def ln_rstd_all():
        S.op("act", lambda e: e.activation(out=mvall[:, :, 2], in_=mvall[:, :, 1], func=AF.Sqrt, bias=EPS), rw=r_mvall)
        S.op("dve", lambda e: e.reciprocal(out=mvall[:, :, 2], in_=mvall[:, :, 2]), rw=r_mvall)
        S.op("dve", lambda e: e.tensor_tensor(out=mvall[:, :, 3], in0=mvall[:, :, 3], in1=mvall[:, :, 2], op=ALU.mult), rw=r_mvall)

    def ln_apply(t):
        xt = x_tok[:, t, :]
        S.op("act", lambda e: e.activation(out=xt, in_=xt, func=AF.Identity, scale=mvall[:, t, 2:3], bias=mvall[:, t, 3:4]),
             reads=[r_mvall[t]], rw=[r_xtok[t]])

    def ln_affine(t, lnp, r_lnp, eg="dve", eb="pool"):
        xt = x_tok[:, t, :]
        S.op(eg, lambda e: e.tensor_tensor(out=xt, in0=xt, in1=lnp[:, 0, :], op=ALU.mult),
             reads=[r_lnp], rw=[r_xtok[t]])
        S.op(eb, lambda e: e.tensor_tensor(out=xt, in0=xt, in1=lnp[:, 1, :], op=ALU.add),
             reads=[r_lnp], rw=[r_xtok[t]])

    def pipeline(stages, n, pre_step=None):
        ns = len(stages)
        for s_ in range(n + ns - 1):
            if pre_step is not None:
                pre_step(s_)
            for k_, f_ in enumerate(stages):
                t_ = s_ - k_
                if 0 <= t_ < n:
                    f_(t_)

    def cast_and_transpose(t, dst_fn, r_dst, k2, evac_eng="dve"):
        cb = castt[k2]
        S.op("act", lambda e: e.activation(out=cb[:], in_=x_tok[:, t, :], func=AF.Copy),
             reads=[r_xtok[t]], writes=[r_castt[k2]])
        transpose_to(cb, r_castt[k2], dst_fn, r_dst, 8, evac_eng)

    def load_lnp(lnp, r_lnp, g_d, b_d, l):
        S.dma("sp", [lambda e: e.dma_start(out=lnp[:, 0, :], in_=g_d[l:l + 1, :].broadcast_to([128, D])),
                     lambda e: e.dma_start(out=lnp[:, 1, :], in_=b_d[l:l + 1, :].broadcast_to([128, D]))],
              ds_ln, writes=[r_lnp])

    for t in range(NT):
        S.dma("sp", [lambda e, t=t: e.dma_start(out=x_tok[:, t, :], in_=x_d[t * 128:(t + 1) * 128, :])],
              ds_x[2 + (t % 2)], writes=[r_xtok[t]])
    stg = [(castt[0], r_castt[0]), (castt[1], r_castt[1]), (stg_a[0], r_stg_a[0]), (stg_a[1], r_stg_a[1])]

    def p0_cast(t):
        cb, r_cb = stg[t % 4]
        if t % 2 == 0:
            S.op("act", lambda e: e.activation(out=cb[:], in_=x_tok[:, t, :], func=AF.Copy),
                 reads=[r_xtok[t]], writes=[r_cb])
        else:
            S.op("dve", lambda e: e.tensor_copy(out=cb[:], in_=x_tok[:, t, :]), reads=[r_xtok[t]], writes=[r_cb])

    p0_cast(0)
    p0_cast(1)
    for t in range(NT):
        if t + 2 < NT:
            p0_cast(t + 2)
        cb, r_cb = stg[t % 4]
        transpose_to(cb, r_cb, lambda t=t: xT[:, :, t * 128:(t + 1) * 128], r_xT[t // 4], 8,
                     "dve" if t % 2 == 0 else "act")

    for j in range(4):
        slot, rs = wnext()
        for tc in range(4):
            k2 = (j * 4 + tc) % 2
            ba, ra = ps_half()
            bg, rg = ps_half()
            fa = [lambda e, d=d, ba=ba, tc=tc, slot=slot: e.matmul(ba, lhsT=slot[:, d, 0:128], rhs=xT[:, d, tc * 512:(tc + 1) * 512],
                                                                   start=(d == 0), stop=(d == 7)) for d in range(8)]
            S.mm(fa, reads=[rs, r_xT[tc]], writes=ra)
            fg = [lambda e, d=d, bg=bg, tc=tc, slot=slot: e.matmul(bg, lhsT=slot[:, d, 128:256], rhs=xT[:, d, tc * 512:(tc + 1) * 512],
                                                                   start=(d == 0), stop=(d == 7)) for d in range(8)]
            S.mm(fg, reads=[rs, r_xT[tc]], writes=rg)
            S.op("act", lambda e, bg=bg, k2=k2: e.activation(out=sgt[k2][:], in_=bg, func=AF.Sigmoid),
                 reads=rg, writes=[r_sgt[k2]])
            S.op("dve", lambda e, ba=ba, k2=k2, j=j, tc=tc: e.tensor_tensor(
                out=hTc[:, j, 32 + tc * 512:32 + (tc + 1) * 512], in0=ba, in1=sgt[k2][:], op=ALU.mult),
                reads=ra + [r_sgt[k2]], writes=[r_hTc])
        wdone()

    if stop_after == "glu":
        dbg_out("hTc", hTc[:], [128, 4, HP], [r_hTc])
        dbg_out("xT", xT[:], [128, 8, T], r_xT)
        return finish(), dbg_d

    conv_groups = [(j, tc) for j in range(4) for tc in range(4)]
    cg = {"i": 0, "tap": 0, "n": 0}
    conv_banks = [(PD[2][:, 0:512], [pd_res[2][0]]), (PD[2][:, 512:1024], [pd_res[2][1]])]

    def build_dg(j):
        S.op("pool", lambda e, j=j: e.tensor_tensor(
            out=dgb[j % 2][:], in0=ident[:].unsqueeze(1).broadcast_to([128, 31, 128]),
            in1=ccw[:, j, :].unsqueeze(2).broadcast_to([128, 31, 128]), op=ALU.mult),
            reads=[r_ident, r_ccw], writes=[r_dgb[j % 2]])

    def conv_emit(n):
        while n > 0 and cg["i"] < len(conv_groups):
            j, tc = conv_groups[cg["i"]]
            bk, rb = conv_banks[cg["i"] % 2]
            tap = cg["tap"]
            fn = lambda e, tap=tap, bk=bk, j=j, tc=tc: e.matmul(
                bk, lhsT=dgb[j % 2][:, tap, :], rhs=hTc[:, j, tc * 512 + 2 + tap:tc * 512 + 2 + tap + 512],
                start=(tap == 0), stop=(tap == 30))
            if tap < 30:
                S.mm_nosig([fn], reads=[r_dgb[j % 2], r_hTc], writes=(rb if tap == 0 else ()))
                cg["tap"] += 1
            else:
                S.mm([fn], reads=[r_dgb[j % 2], r_hTc], writes=rb)
                S.op("act", lambda e, bk=bk, j=j, tc=tc: e.activation(out=convT[:, j, tc * 512:(tc + 1) * 512], in_=bk,
                                                                    func=AF.Identity, bias=ccb[:, j:j + 1]),
                     reads=rb + [r_ccb], writes=[r_convT4[tc]])
                cg["tap"] = 0
                cg["i"] += 1
                if tc == 3 and j + 2 < 4:
                    build_dg(j + 2)
            n -= 1

    build_dg(0)
    build_dg(1)
    psn[0] = 4
    ri = [0]
    for hp in range(2):
        for g in range(3):
            dil = DILS[g]
            L = T // dil
            nb = L // 128
            wq_, rwq = wnext()
            wk_, rwk = wnext()
            wv, rwv = wnext()
            J = 512 // dil
            for (w_, rw_, dstT, r_dst) in ((wq_, rwq, qT, r_qT), (wk_, rwk, kT, r_kT)):
                for h in range(2):
                    for tc in range(4):
                        k2 = ri[0] % 2
                        ri[0] += 1
                        bk, rb = ps_half()
                        fns = [lambda e, d=d, bk=bk, w_=w_, h=h, tc=tc: e.matmul(
                            bk, lhsT=w_[:, d, h * 128:(h + 1) * 128], rhs=xT[:, d, tc * 512:(tc + 1) * 512],
                            start=(d == 0), stop=(d == 7)) for d in range(8)]
                        S.mm(fns, reads=[rw_, r_xT[tc]], writes=rb)
                        t1, t2 = rtmp1[k2], rtmp2[k2]
                        S.op("dve", lambda e, bk=bk, t1=t1, tc=tc: e.tensor_tensor(
                            out=t1[:], in0=bk, in1=ropeC[:, tc * 512:(tc + 1) * 512], op=ALU.mult),
                            reads=rb + [r_ropeC], writes=[r_rtmp1[k2]])
                        S.op("dve", lambda e, bk=bk, t2=t2, tc=tc: e.tensor_tensor(
                            out=t2[0:64, :], in0=bk[64:128, :], in1=ropeS[64:128, tc * 512:(tc + 1) * 512], op=ALU.mult),
                            reads=rb + [r_ropeS], writes=[r_rtmp2[k2]])
                        S.op("dve", lambda e, bk=bk, t2=t2, tc=tc: e.tensor_tensor(
                            out=t2[64:128, :], in0=bk[0:64, :], in1=ropeS[0:64, tc * 512:(tc + 1) * 512], op=ALU.mult),
                            reads=rb + [r_ropeS], writes=[r_rtmp2[k2]])
                        j0 = tc * 512 // dil
                        S.op("pool", lambda e, t1=t1, t2=t2, dstT=dstT, h=h, dil=dil, j0=j0, J=J: e.tensor_tensor(
                            out=dstT[:, h, :].rearrange("p (r j) -> p r j", r=dil)[:, :, j0:j0 + J],
                            in0=t1[:].rearrange("p (j r) -> p r j", r=dil),
                            in1=t2[:].rearrange("p (j r) -> p r j", r=dil), op=ALU.add),
                            reads=[r_rtmp1[k2], r_rtmp2[k2]], writes=[r_dst])
                wdone()
            for blk in range(16):
                r_, n_ = divmod(blk, nb)
                st0 = n_ * 128 * dil + r_
                bk, rb = ps_half()
                fns = [lambda e, d=d, bk=bk, st0=st0, dil=dil, wv=wv: e.matmul(
                    bk[:, 0:256], lhsT=xT[:, d, st0:st0 + 127 * dil + 1:dil], rhs=wv[:, d, :],
                    start=(d == 0), stop=(d == 7)) for d in range(8)]
                S.mm(fns, reads=[rwv] + r_xT, writes=rb)
                S.op("act", lambda e, bk=bk, blk=blk: e.activation(out=Vg[:, blk, :], in_=bk[:, 0:256], func=AF.Copy),
                     reads=rb, writes=[r_Vg])
            wdone()
            for blk in range(16):
                r_, n_ = divmod(blk, nb)
                hasp = n_ > 0
                cc = blk * 128
                cp = (blk - 1) * 128
                k2 = blk % 2
                bS, rS = ps_half()
                fns = []
                for h in range(2):
                    fns.append(lambda e, h=h, bS=bS, cc=cc: e.matmul(
                        bS[:, h * 128:(h + 1) * 128], lhsT=kT[:, h, cc:cc + 128], rhs=qT[:, h, cc:cc + 128],
                        start=True, stop=True))
                if hasp:
                    for h in range(2):
                        fns.append(lambda e, h=h, bS=bS, cc=cc, cp=cp: e.matmul(
                            bS[:, 256 + h * 128:256 + (h + 1) * 128], lhsT=kT[:, h, cp:cp + 128], rhs=qT[:, h, cc:cc + 128],
                            start=True, stop=True))
                S.mm(fns, reads=[r_qT, r_kT], writes=rS)
                conv_emit(6)
                ncp = 2 if hasp else 1
                ncol = 256 * ncp
                S.op("act", lambda e, bS=bS, k2=k2, ncol=ncol: e.activation(
                    out=PT[k2][:, 0:ncol], in_=bS[:, 0:ncol], func=AF.Exp, scale=SCALE),
                    reads=rS, writes=[r_PT[k2]])
                S.op("dve", lambda e, k2=k2, ncol=ncol, ncp=ncp: e.tensor_tensor(
                    out=PT[k2][:, 0:ncol].rearrange("p (c h q) -> p c h q", c=ncp, h=2),
                    in0=PT[k2][:, 0:ncol].rearrange("p (c h q) -> p c h q", c=ncp, h=2),
                    in1=masks[:, 0:ncp, :].unsqueeze(2).broadcast_to([128, ncp, 2, 128]), op=ALU.mult),
                    reads=[r_masks], rw=[r_PT[k2]])
                bU, rU = ps_half()
                fns = []
                for h in range(2):
                    fns.append(lambda e, h=h, bU=bU, blk=blk, k2=k2, hasp=hasp: e.matmul(
                        bU[:, h * 128:(h + 1) * 128], lhsT=Vg[:, blk, h * 128:(h + 1) * 128],
                        rhs=PT[k2][:, h * 128:(h + 1) * 128], start=True, stop=(not hasp)))
                    if hasp:
                        fns.append(lambda e, h=h, bU=bU, blk=blk, k2=k2: e.matmul(
                            bU[:, h * 128:(h + 1) * 128], lhsT=Vg[:, blk - 1, h * 128:(h + 1) * 128],
                            rhs=PT[k2][:, 256 + h * 128:256 + (h + 1) * 128], start=False, stop=True))
                fns.append(lambda e, bU=bU, k2=k2, hasp=hasp: e.matmul(
                    bU[:, 256:512], lhsT=ones[:], rhs=PT[k2][:, 0:256], start=True, stop=(not hasp)))
                if hasp:
                    fns.append(lambda e, bU=bU, k2=k2: e.matmul(
                        bU[:, 256:512], lhsT=ones[:], rhs=PT[k2][:, 256:512], start=False, stop=True))
                S.mm(fns, reads=[r_Vg, r_PT[k2], r_ones], writes=rU)
                st0 = n_ * 128 * dil + r_
                dst = UD[:, :, :, st0:st0 + 127 * dil + 1:dil]
                src = bU.rearrange("p (u h q) -> p u h q", u=2, h=2)
                if g == 0:
                    S.op("act", lambda e, dst=dst, src=src: e.activation(out=dst, in_=src, func=AF.Copy), reads=rU, writes=[r_UD])
                else:
                    S.op("dve", lambda e, dst=dst, src=src: e.tensor_tensor(out=dst, in0=src, in1=dst, op=ALU.add),
                         reads=rU, rw=[r_UD])
        S.op("act", lambda e: e.activation(out=UD[:, 1, :, :], in_=UD[:, 1, :, :], func=AF.Ln), rw=[r_UD])
        S.op("act", lambda e: e.activation(out=UD[:, 1, :, :], in_=UD[:, 1, :, :], func=AF.Exp, scale=-1.0), rw=[r_UD])
        S.op("dve", lambda e, hp=hp: e.tensor_tensor(out=attnT[:, 2 * hp:2 * hp + 2, :], in0=UD[:, 0, :, :],
                                                     in1=UD[:, 1, :, :], op=ALU.mult),
             reads=[r_UD], writes=[r_attnT])

    if stop_after == "attn":
        dbg_out("attnT", attnT[:], [128, 4, T], [r_attnT])
        dbg_out("hTc", hTc[:], [128, 4, HP], [r_hTc])
        return finish(), dbg_d

    conv_emit(10 ** 6)
    psn[0] = 6

    def conv_ln(tc):
        cs = slice(tc * 512, (tc + 1) * 512)
        for j in range(4):
            S.op("dve", lambda e, j=j, cs=cs: e.tensor_tensor(out=ysq[:, j, :], in0=convT[:, j, cs], in1=convT[:, j, cs], op=ALU.mult),
                 reads=[r_convT4[tc]], writes=[r_ysq])
        bm, rm = ps_half()
        S.mm([lambda e, j=j, bm=bm, cs=cs: e.matmul(bm, lhsT=ones[:], rhs=convT[:, j, cs], start=(j == 0), stop=(j == 3))
              for j in range(4)], reads=[r_convT4[tc], r_ones], writes=rm)
        bq, rq = ps_half()
        S.mm([lambda e, j=j, bq=bq: e.matmul(bq, lhsT=ones[:], rhs=ysq[:, j, :], start=(j == 0), stop=(j == 3))
              for j in range(4)], reads=[r_ysq, r_ones], writes=rq)
        S.op("dve", lambda e, bm=bm: e.tensor_scalar_mul(out=mt[:], in0=bm, scalar1=1.0 / 512), reads=rm, writes=[r_mt])
        S.op("dve", lambda e: e.tensor_tensor(out=vt[:], in0=mt[:], in1=mt[:], op=ALU.mult), reads=[r_mt], writes=[r_vt])
        S.op("dve", lambda e, bq=bq: e.scalar_tensor_tensor(out=vt[:], in0=bq, scalar=1.0 / 512, in1=vt[:],
                                                            op0=ALU.mult, op1=ALU.subtract), reads=rq, rw=[r_vt])
        S.op("act", lambda e: e.activation(out=vt[:], in_=vt[:], func=AF.Ln, bias=EPS), rw=[r_vt])
        S.op("act", lambda e: e.activation(out=vt[:], in_=vt[:], func=AF.Exp, scale=-0.5), rw=[r_vt])
        for j in range(4):
            k2 = j % 2
            S.op("dve", lambda e, j=j, k2=k2, cs=cs: e.tensor_tensor(out=zt[k2][:], in0=convT[:, j, cs], in1=mt[:], op=ALU.subtract),
                 reads=[r_convT4[tc], r_mt], writes=[r_zt[k2]])
            S.op("dve", lambda e, k2=k2: e.tensor_tensor(out=zt[k2][:], in0=zt[k2][:], in1=vt[:], op=ALU.mult),
                 reads=[r_vt], rw=[r_zt[k2]])
            S.op("act", lambda e, j=j, k2=k2, cs=cs: e.activation(
                out=convT[:, j, cs], in_=zt[k2][:], func=AF.Silu, scale=clg[:, j:j + 1], bias=clb[:, j:j + 1]),
                reads=[r_zt[k2], r_clg, r_clb], rw=[r_convT4[tc]])

    if stop_after == "conv":
        for tc in range(4):
            conv_ln(tc)
        dbg_out("convT", convT[:], [128, 4, T], r_convT4)
        dbg_out("attnT", attnT[:], [128, 4, T], [r_attnT])
        return finish(), dbg_d

    def mixer_out_and_ln1(l, cat_fn, r_cat_fn, w_d, lnp1, r_lnp1, x_from_hbm, pre_step=None):
        def xload(t):
            S.dma("sp", [lambda e, t=t: e.dma_start(out=x_tok[:, t, :], in_=x_d[t * 128:(t + 1) * 128, :])],
                  ds_x[2 + (t % 2)], writes=[r_xtok[t]])
        if x_from_hbm:
            for t in range(4):
                xload(t)
        S.dma("pool", [lambda e: e.dma_start(out=Wmix[:], in_=w_d.rearrange("(kc p) n -> p kc n", p=128))],
              ds_w1[0], writes=[r_Wmix])
        load_lnp(lnp1, r_lnp1, lmg_d, lmb_d, l)
        if x_from_hbm:
            for t in range(4, NT):
                xload(t)
        def A1(t):
            k2 = t % 2
            bD, rD = ps_full()
            fns = []
            for half in range(2):
                for f in range(8):
                    fns.append(lambda e, half=half, f=f, bD=bD, t=t: e.matmul(
                        bD[:, half * 512:(half + 1) * 512], lhsT=cat_fn(f, t), rhs=Wmix[:, f, half * 512:(half + 1) * 512],
                        start=(f == 0), stop=(f == 7)))
            S.mm(fns, reads=r_cat_fn(t) + [r_Wmix], writes=rD)
            S.op("dve", lambda e, bD=bD, t=t: e.scalar_tensor_tensor(
                out=x_tok[:, t, :], in0=x_tok[:, t, :], scalar=ALPHA, in1=bD[:], op0=ALU.mult, op1=ALU.add),
                reads=rD, rw=[r_xtok[t]])
            ln_stats(t, k2)

        def A2(t):
            ln_rstd_tile(t)
            ln_apply(t)

        pipeline([A1, A2,
                  lambda t: ln_affine(t, lnp1, r_lnp1, "pool", "pool"),
                  lambda t: cast_and_transpose(t, lambda t=t: xT[:, :, t * 128:(t + 1) * 128], r_xT[t // 4], t % 2, "act")],
                 NT, pre_step)

    for tc_ in range(4):
        conv_ln(tc_)

    def l0_pre(s_):
        pass

    mixer_out_and_ln1(0, lambda f, t: (attnT[:, f, t * 128:(t + 1) * 128] if f < 4 else convT[:, f - 4, t * 128:(t + 1) * 128]),
                      lambda t: [r_attnT, r_convT4[t // 4]], ewout_d, lnp1_L0, r_lnp1_L0, True, l0_pre)

    if stop_after == "ln1":
        for t in range(NT):
            dbg_out("x1_%d" % t, x_tok[:, t, :], [128, D], [r_xtok[t]])
        return finish(), dbg_d

    FG = [(0, 4), (4, 8), (8, 12), (12, 16), (16, 19), (19, 22)]

    def ffn_and_ple(l, last_layer):
        fw_out = fwout_d[l]
        for pi, (f0, f1) in enumerate(FG):
            nf = f1 - f0
            wb = pi % 2
            S.dma("pool", [lambda e, wb=wb, f0=f0, f1=f1, nf=nf: e.dma_start(
                out=Wfo[wb][:, 0:nf, :], in_=fw_out[f0 * 128:f1 * 128, :].rearrange("(kc p) n -> p kc n", p=128))],
                ds_w1[wb], writes=[r_Wfo[wb]])
            def inproj(f, tc, slot, rs, f0=f0):
                k2 = tc % 2
                bg, rg = ps_half()
                bu, ru = ps_half()
                S.mm([lambda e, d=d, bg=bg, tc=tc, slot=slot: e.matmul(
                    bg, lhsT=slot[:, d, 0:128], rhs=xT[:, d, tc * 512:(tc + 1) * 512], start=(d == 0), stop=(d == 7))
                    for d in range(8)], reads=[rs, r_xT[tc]], writes=rg)
                S.mm([lambda e, d=d, bu=bu, tc=tc, slot=slot: e.matmul(
                    bu, lhsT=slot[:, d, 128:256], rhs=xT[:, d, tc * 512:(tc + 1) * 512], start=(d == 0), stop=(d == 7))
                    for d in range(8)], reads=[rs, r_xT[tc]], writes=ru)
                S.op("act", lambda e, bg=bg, k2=k2: e.activation(out=sgt[k2][:], in_=bg, func=AF.Silu),
                     reads=rg, writes=[r_sgt[k2]])
                S.op("dve", lambda e, bu=bu, k2=k2, f=f, f0=f0, tc=tc: e.tensor_tensor(
                    out=hTf[:, f - f0, tc * 512:(tc + 1) * 512], in0=bu, in1=sgt[k2][:], op=ALU.mult),
                    reads=ru + [r_sgt[k2]], writes=[r_hTf4[tc]])

            is_last = (pi == len(FG) - 1)
            if not is_last:
                for f in range(f0, f1):
                    slot, rs = wnext()
                    for tc in range(4):
                        inproj(f, tc, slot, rs)
                    wdone()
            if pi == 0:
                load_lnp(lnp2, r_lnp2, lfg_d, lfb_d, l)
                S.dma("pool", [lambda e: e.dma_start(out=Wpg[:], in_=pwg_d[l].rearrange("(kc p) n -> p kc n", p=128)),
                               lambda e: e.dma_start(out=Wpp[:], in_=pwp_d[l].rearrange("(kc p) n -> p kc n", p=128))],
                      ds_ple, writes=[r_Wpg, r_Wpp])
            last = (pi == len(FG) - 1)

            def A1(t, nf=nf, wb=wb, pi=pi, last=last):
                k2 = t % 2
                bD, rD = ps_full()
                fns = []
                for half in range(2):
                    for ff in range(nf):
                        fns.append(lambda e, half=half, ff=ff, bD=bD, t=t, wb=wb, nf=nf: e.matmul(
                            bD[:, half * 512:(half + 1) * 512], lhsT=hTf[:, ff, t * 128:(t + 1) * 128],
                            rhs=Wfo[wb][:, ff, half * 512:(half + 1) * 512], start=(ff == 0), stop=(ff == nf - 1)))
                S.mm(fns, reads=[r_hTf4[t // 4], r_Wfo[wb]], writes=rD)
                if pi == 0:
                    S.op("dve", lambda e, bD=bD, t=t: e.scalar_tensor_tensor(
                        out=x_tok[:, t, :], in0=x_tok[:, t, :], scalar=ALPHA, in1=bD[:], op0=ALU.mult, op1=ALU.add),
                        reads=rD, rw=[r_xtok[t]])
                else:
                    S.op("dve", lambda e, bD=bD, t=t: e.tensor_tensor(
                        out=x_tok[:, t, :], in0=x_tok[:, t, :], in1=bD[:], op=ALU.add), reads=rD, rw=[r_xtok[t]])
                if last:
                    ln_stats(t, k2)

            if not last:
                for t in range(NT):
                    A1(t)
                continue
            slots = [wnext() for _ in range(f0, f1)]
            for tc in range(4):
                for fi, f in enumerate(range(f0, f1)):
                    inproj(f, tc, slots[fi][0], slots[fi][1])
                if tc >= 1:
                    for t in range(4 * (tc - 1), 4 * tc):
                        A1(t)
            for _ in range(f0, f1):
                wdone()
            for t in range(12, 16):
                A1(t)
            ln_rstd_all()

            def cast_only(t, cb, r_cb):
                S.op("act", lambda e: e.activation(out=cb[:], in_=x_tok[:, t, :], func=AF.Copy),
                     reads=[r_xtok[t]], writes=[r_cb])

            def ok(t):
                return 0 <= t < NT

            for s_ in range(NT + 6):
                t = s_ - 2
                if ok(t):
                    k2 = t % 2
                    transpose_to(castt[k2], r_castt[k2], lambda k2=k2: x2T[k2][:], r_x2T[k2], 8, "dve")
                t = s_ - 3
                if ok(t):
                    k2 = t % 2
                    transpose_to(pbt[k2], r_pbt[k2], lambda k2=k2: pTt[k2][:], r_pTt[k2], 2, "act")
                t = s_ - 6
                if ok(t) and not last_layer:
                    k2 = t % 2
                    transpose_to(castd[k2], r_castd[k2], lambda t=t: xT[:, :, t * 128:(t + 1) * 128], r_xT[t // 4], 8, "dve")
                t = s_
                if ok(t):
                    ln_apply(t)
                    ln_affine(t, lnp2, r_lnp2, "dve", "dve")
                t = s_ - 1
                if ok(t):
                    k2 = t % 2
                    cast_only(t, castt[k2], r_castt[k2])
                    S.dma("pool", [lambda e, t=t, k2=k2: e.dma_start(out=pbt[k2][:], in_=p_d[l, t * 128:(t + 1) * 128, :])],
                          ds_p[k2], writes=[r_pbt[k2]])
                t = s_ - 5
                if ok(t):
                    if last_layer:
                        S.dma("sp", [lambda e, t=t: e.dma_start(out=out_d[t * 128:(t + 1) * 128, :], in_=x_tok[:, t, :])],
                              ds_out[t % 4], reads=[r_xtok[t]])
                    else:
                        cast_only(t, castd[t % 2], r_castd[t % 2])
                tg = s_ - 3
                bG = rG = None
                if ok(tg):
                    k2 = tg % 2
                    bG, rG = ps_full()
                    fns = []
                    for half in range(2):
                        for c in range(8):
                            fns.append(lambda e, half=half, c=c, bG=bG, k2=k2: e.matmul(
                                bG[:, half * 512:(half + 1) * 512], lhsT=x2T[k2][:, c, :], rhs=Wpg[:, c, half * 512:(half + 1) * 512],
                                start=(c == 0), stop=(c == 7)))
                    S.mm(fns, reads=[r_x2T[k2], r_Wpg], writes=rG)
                t = s_ - 4
                if ok(t):
                    k2 = t % 2
                    sg_ = sigt[k2]
                    bP, rP = ps_full()
                    fns = []
                    for half in range(2):
                        for c in range(2):
                            fns.append(lambda e, half=half, c=c, bP=bP, k2=k2: e.matmul(
                                bP[:, half * 512:(half + 1) * 512], lhsT=pTt[k2][:, c, :], rhs=Wpp[:, c, half * 512:(half + 1) * 512],
                                start=(c == 0), stop=(c == 1)))
                    S.mm(fns, reads=[r_pTt[k2], r_Wpp], writes=rP)
                    S.op("dve", lambda e, bP=bP, sg_=sg_: e.tensor_tensor(out=sg_[:], in0=bP[:], in1=sg_[:], op=ALU.mult),
                         reads=rP, rw=[r_sigt[k2]])
                    S.op("pool", lambda e, t=t, sg_=sg_: e.tensor_tensor(out=x_tok[:, t, :], in0=x_tok[:, t, :], in1=sg_[:], op=ALU.add),
                         reads=[r_sigt[k2]], rw=[r_xtok[t]])
                if ok(tg):
                    k2 = tg % 2
                    sg_ = sigt[k2]
                    S.op("act", lambda e, bG=bG, sg_=sg_: e.activation(out=sg_[:], in_=bG[:], func=AF.Sigmoid),
                         reads=rG, writes=[r_sigt[k2]])

    ffn_and_ple(0, False)

    if stop_after == "l0":
        for t in range(NT):
            dbg_out("x3_%d" % t, x_tok[:, t, :], [128, D], [r_xtok[t]])
        return finish(), dbg_d

    for i in range(2):
        S.op("dve", lambda e, i=i: e.memset(ub[i][:, 0:2], 0.0), writes=[r_ub[i]])
    ui = [0]
    for jj in range(4):
        wb_, rwb = wnext()
        wc_, rwc = wnext()
        wh_, rwh = wnext()
        for j2 in range(2):
            j = 2 * jj + j2
            ubi = ui[0] % 2
            ui[0] += 1
            u = ub[ubi]
            for tc in range(4):
                k2 = tc % 2
                bb, rbb = ps_half()
                bc, rbc = ps_half()
                bh, rbh = ps_half()
                for (bk, rb, w_, rw_) in ((bb, rbb, wb_, rwb), (bc, rbc, wc_, rwc), (bh, rbh, wh_, rwh)):
                    S.mm([lambda e, d=d, bk=bk, w_=w_, j2=j2, tc=tc: e.matmul(
                        bk, lhsT=w_[:, d, j2 * 128:(j2 + 1) * 128], rhs=xT[:, d, tc * 512:(tc + 1) * 512],
                        start=(d == 0), stop=(d == 7)) for d in range(8)], reads=[rw_, r_xT[tc]], writes=rb)
                S.op("act", lambda e, bc=bc, k2=k2: e.activation(out=sgt[k2][:], in_=bc, func=AF.Copy),
                     reads=rbc, writes=[r_sgt[k2]])
                S.op("dve", lambda e, bh=bh, k2=k2, u=u, tc=tc: e.tensor_tensor(
                    out=u[:, 2 + tc * 512:2 + (tc + 1) * 512], in0=bh, in1=sgt[k2][:], op=ALU.mult),
                    reads=rbh + [r_sgt[k2]], rw=[r_ub[ubi]])
                ct = ctmp[k2]
                S.op("dve", lambda e, u=u, ct=ct, tc=tc, j=j: e.tensor_scalar_mul(
                    out=ct[:], in0=u[:, 2 + tc * 512:2 + (tc + 1) * 512], scalar1=ocw[:, j, 2:3]),
                    reads=[r_ub[ubi], r_ocw], writes=[r_ctmp[k2]])
                S.op("dve", lambda e, u=u, ct=ct, tc=tc, j=j: e.scalar_tensor_tensor(
                    out=ct[:], in0=u[:, 1 + tc * 512:1 + (tc + 1) * 512], scalar=ocw[:, j, 1:2], in1=ct[:],
                    op0=ALU.mult, op1=ALU.add), reads=[r_ub[ubi]], rw=[r_ctmp[k2]])
                S.op("dve", lambda e, u=u, ct=ct, tc=tc, j=j: e.scalar_tensor_tensor(
                    out=ct[:], in0=u[:, tc * 512:(tc + 1) * 512], scalar=ocw[:, j, 0:1], in1=ct[:],
                    op0=ALU.mult, op1=ALU.add), reads=[r_ub[ubi]], rw=[r_ctmp[k2]])
                S.op("dve", lambda e, bb=bb, ct=ct, j=j, tc=tc: e.tensor_tensor(
                    out=yT[:, j, tc * 512:(tc + 1) * 512], in0=bb, in1=ct[:], op=ALU.mult),
                    reads=rbb + [r_ctmp[k2]], writes=[r_yT])
        wdone(); wdone(); wdone()

    mixer_out_and_ln1(1, lambda f, t: yT[:, f, t * 128:(t + 1) * 128], lambda t: [r_yT], owout_d, lnp1_L1, r_lnp1_L1, False)
    ffn_and_ple(1, True)
    return finish(), dbg_d


_CACHE = {}


def _rope_tables():
    half = 64
    inv = 10000.0 ** (-np.arange(half, dtype=np.float64) / half)
    pos = np.arange(T, dtype=np.float64)
    ang = pos[:, None] * inv[None, :]
    c = np.cos(ang).T.astype(np.float32)
    s = np.sin(ang).T.astype(np.float32)
    C = np.concatenate([c, c], axis=0)
    SW = np.concatenate([s, -s], axis=0)
    return np.ascontiguousarray(C), np.ascontiguousarray(SW)


def make_in_maps(inputs):
    f = lambda a: np.ascontiguousarray(np.asarray(a, dtype=np.float32))
    C, SW = _rope_tables()
    shared = {
        "even_w_in": f(inputs["even_w_in"])[0],
        "even_w_out": f(inputs["even_w_out"])[0],
        "conf_conv_w": np.ascontiguousarray(f(inputs["conf_conv_w"])[0].reshape(31, 4, 128).transpose(2, 1, 0)),
        "conf_conv_b": np.ascontiguousarray(f(inputs["conf_conv_b"]).reshape(4, 128).T),
        "conf_ln_g": np.ascontiguousarray(f(inputs["conf_ln_g"]).reshape(4, 128).T),
        "conf_ln_b": np.ascontiguousarray(f(inputs["conf_ln_b"]).reshape(4, 128).T),
        "odd_w_in": f(inputs["odd_w_in"])[0],
        "odd_conv_w": np.ascontiguousarray(f(inputs["odd_conv_w"])[0].reshape(3, 8, 128).transpose(2, 1, 0)),
        "odd_w_out": f(inputs["odd_w_out"])[0],
        "ln_mix_g": f(inputs["ln_mix_g"]),
        "ln_mix_b": f(inputs["ln_mix_b"]),
        "ln_ffn_g": f(inputs["ln_ffn_g"]),
        "ln_ffn_b": f(inputs["ln_ffn_b"]),
        "ffn_w_in": f(inputs["ffn_w_in"]),
        "ffn_w_out": f(inputs["ffn_w_out"]),
        "ple_w_proj": f(inputs["ple_w_proj"]),
        "ple_w_gate": f(inputs["ple_w_gate"]),
        "rope_c": C,
        "rope_s": SW,
    }
    x = f(inputs["x"])
    p = f(inputs["p"])
    maps = []
    for b in range(8):
        m = dict(shared)
        m["x"] = x[b]
        m["p"] = np.ascontiguousarray(p[:, b])
        maps.append(m)
    return maps


def kernel(**inputs):
    if "nc" not in _CACHE:
        _CACHE["nc"] = build()[0]
    nc = _CACHE["nc"]
    in_maps = make_in_maps(inputs)
    res = run_bass_kernel_spmd(nc, in_maps, core_ids=list(range(8)))
    out = np.stack([np.asarray(res.results[b]["out"], dtype=np.float32) for b in range(8)], axis=0)
    return out
```

```python
import os
import numpy as np
import concourse.bass as bass
import concourse.mybir as mybir
from concourse.bass_utils import run_bass_kernel_spmd

F32 = mybir.dt.float32
BF16 = mybir.dt.bfloat16
AF = mybir.ActivationFunctionType
ALU = mybir.AluOpType

T = 2048
D = 1024
NT = 16
FF = 2816
NF = 22
ALPHA = float(4 ** 0.25)
EPS = 1e-5
SCALE = float(128 ** -0.5)
DILS = (1, 4, 16)
K = 1024

BASE = 16512
R_A = BASE
R_B = R_A + 64 * K
R_C = R_B + 32 * K
R_C_SZ = 50 * K
R_W1 = R_C + R_C_SZ
RING = R_W1 + 16 * K
MISC = RING + 16 * K


class Res:
    ALL = []

    def __init__(self, name="", ranges=None):
        self.ws = {}
        self.rs = {}
        self.name = name
        self.ranges = ranges
        self.ov = [self]
        if ranges is not None:
            for o in Res.ALL:
                if o.ranges is None:
                    continue
                if any(a0 < b1 and b0 < a1 for (a0, a1) in ranges for (b0, b1) in o.ranges):
                    self.ov.append(o)
                    o.ov.append(self)
            Res.ALL.append(self)


def _merge(d, tok):
    if tok is None:
        return
    sem, val, key = tok
    if key not in d or d[key][1] < val:
        d[key] = tok


class DmaSem:
    def __init__(self, sem):
        self.sem = sem
        self.n = 0


class Sched:
    ENG = ("pe", "act", "dve", "pool", "sp")

    def __init__(self, nc):
        self.nc = nc
        self.q = {e: [] for e in self.ENG}
        self.sem = {e: nc.alloc_semaphore("sem_" + e) for e in self.ENG}
        self.cnt = {e: 0 for e in self.ENG}
        self.waited = {e: {} for e in self.ENG}
        self.nsem = 0
        self.ninst = {e: 0 for e in self.ENG}

    def dsem(self, name=None):
        self.nsem += 1
        return DmaSem(self.nc.alloc_semaphore(name or ("ds%d" % self.nsem)))

    def _wait(self, eng, toks):
        w = self.waited[eng]
        for d in toks:
            if d is None:
                continue
            sem, val, key = d
            if key == eng:
                pass
            if w.get(key, 0) >= val:
                continue
            w[key] = val
            self.q[eng].append(lambda e, sem=sem, val=val: e.wait_ge(sem, val))
            self.ninst[eng] += 1

    def _deps(self, reads, writes, rw, deps):
        toks = list(deps)
        for r in reads:
            for o in r.ov:
                toks.extend(o.ws.values())
        for r in writes:
            for o in r.ov:
                toks.extend(o.rs.values())
                if o is not r:
                    toks.extend(o.ws.values())
        for r in rw:
            for o in r.ov:
                toks.extend(o.ws.values())
                toks.extend(o.rs.values())
        return toks

    def _post(self, tok, reads, writes, rw):
        for r in reads:
            _merge(r.rs, tok)
        for r in writes:
            if r.rs:
                r.ws = {}
                r.rs = {}
            _merge(r.ws, tok)
        for r in rw:
            r.ws = {}
            r.rs = {}
            _merge(r.ws, tok)

    def op(self, eng, fn, reads=(), writes=(), rw=(), deps=()):
        self._wait(eng, self._deps(reads, writes, rw, deps))
        self.cnt[eng] += 1
        sem = self.sem[eng]
        self.q[eng].append(lambda e, fn=fn, sem=sem: fn(e).then_inc(sem, 1))
        self.ninst[eng] += 1
        tok = (sem, self.cnt[eng], eng)
        self._post(tok, reads, writes, rw)
        return tok

    def mm_nosig(self, fns, reads=(), writes=(), deps=()):
        eng = "pe"
        self._wait(eng, self._deps(reads, writes, (), deps))
        for fn in fns:
            self.q[eng].append(lambda e, fn=fn: fn(e))
        self.ninst[eng] += len(fns)

    def mm(self, fns, reads=(), writes=(), deps=()):
        eng = "pe"
        self._wait(eng, self._deps(reads, writes, (), deps))
        for fn in fns[:-1]:
            self.q[eng].append(lambda e, fn=fn: fn(e))
        self.ninst[eng] += len(fns)
        self.cnt[eng] += 1
        sem = self.sem[eng]
        fn = fns[-1]
        self.q[eng].append(lambda e, fn=fn, sem=sem: fn(e).then_inc(sem, 1))
        tok = (sem, self.cnt[eng], eng)
        self._post(tok, reads, writes, ())
        return tok

    def dma(self, eng, fns, ds, reads=(), writes=(), deps=()):
        toks = self._deps(reads, writes, (), deps)
        if ds.n > 0:
            toks.append((ds.sem, 16 * ds.n, id(ds)))
        self._wait(eng, toks)
        for fn in fns:
            self.q[eng].append(lambda e, fn=fn, sem=ds.sem: fn(e).then_inc(sem, 16))
        self.ninst[eng] += len(fns)
        ds.n += len(fns)
        tok = (ds.sem, 16 * ds.n, id(ds))
        self._post(tok, reads, writes, ())
        return tok

    def wait(self, eng, toks):
        self._wait(eng, toks)

    def emit(self):
        nc = self.nc
        q = self.q
        with nc.Block() as block:
            @block.tensor
            def _(e):
                for f in q["pe"]:
                    f(e)

            @block.scalar
            def _(e):
                for f in q["act"]:
                    f(e)

            @block.vector
            def _(e):
                for f in q["dve"]:
                    f(e)

            @block.gpsimd
            def _(e):
                for f in q["pool"]:
                    f(e)

            @block.sync
            def _(e):
                for f in q["sp"]:
                    f(e)


def build(stop_after=None, dbg=False):
    nc = bass.Bass("TRN2", target_bir_lowering=False)
    S = Sched(nc)
    Res.ALL = []

    def din(name, shape):
        return nc.dram_tensor(name, list(shape), F32, kind="ExternalInput").ap()

    x_d = din("x", [T, D])
    p_d = din("p", [2, T, 256])
    ewin_d = din("even_w_in", [D, 5632])
    ewout_d = din("even_w_out", [D, D])
    ccw_d = din("conf_conv_w", [128, 4, 31])
    ccb_d = din("conf_conv_b", [128, 4])
    clg_d = din("conf_ln_g", [128, 4])
    clb_d = din("conf_ln_b", [128, 4])
    owin_d = din("odd_w_in", [D, 3072])
    ocw_d = din("odd_conv_w", [128, 8, 3])
    owout_d = din("odd_w_out", [D, D])
    lmg_d = din("ln_mix_g", [2, D])
    lmb_d = din("ln_mix_b", [2, D])
    lfg_d = din("ln_ffn_g", [2, D])
    lfb_d = din("ln_ffn_b", [2, D])
    fwin_d = din("ffn_w_in", [2, D, 2 * FF])
    fwout_d = din("ffn_w_out", [2, FF, D])
    pwp_d = din("ple_w_proj", [2, 256, D])
    pwg_d = din("ple_w_gate", [2, D, D])
    ropec_d = din("rope_c", [128, T])
    ropes_d = din("rope_s", [128, T])
    out_d = nc.dram_tensor("out", [T, D], F32, kind="ExternalOutput").ap()
    dbg_d = {}

    def sb(name, shape, dt, off):
        n = int(np.prod(shape[1:])) * (4 if dt == F32 else 2)
        t_ = nc.alloc_sbuf_tensor_at(name, list(shape), dt, offset=off)
        return t_, Res(name, [(off, off + n)])

    x_tok = nc.alloc_sbuf_tensor_at("x_tok", [128, NT, D], F32, offset=R_A)
    r_xtok = [Res("xtok%d" % t, [(R_A + t * 4096, R_A + (t + 1) * 4096)]) for t in range(NT)]
    UD, r_UD = sb("UD", [128, 2, 2, T], F32, R_A)
    qT, r_qT = sb("qT", [128, 2, T], BF16, R_A + 32 * K)
    kT, r_kT = sb("kT", [128, 2, T], BF16, R_A + 40 * K)
    Vg, r_Vg = sb("Vg", [128, 16, 256], BF16, R_A + 48 * K)
    rtmp1, r_rtmp1, rtmp2, r_rtmp2 = [], [], [], []
    for i in range(2):
        a_, b_ = sb("rtmp1_%d" % i, [128, 512], F32, R_A + 56 * K + i * 2 * K)
        rtmp1.append(a_); r_rtmp1.append(b_)
        a_, b_ = sb("rtmp2_%d" % i, [128, 512], F32, R_A + 60 * K + i * 2 * K)
        rtmp2.append(a_); r_rtmp2.append(b_)
    xT = nc.alloc_sbuf_tensor_at("xT", [128, 8, T], BF16, offset=R_B)
    r_xT = [Res("xT%d" % c, [(R_B + d * 4096 + c * 1024, R_B + d * 4096 + (c + 1) * 1024) for d in range(8)])
            for c in range(4)]
    HP = 2080
    hTc, r_hTc = sb("hTc", [128, 4, HP], BF16, R_C)
    attnT, r_attnT = sb("attnT", [128, 4, T], BF16, R_C + 16640)
    convT, r_convT = sb("convT", [128, 4, T], BF16, R_C + 16640 + 16 * K)
    ropeC, r_ropeC = sb("ropeC", [128, T], F32, R_W1)
    ropeS, r_ropeS = sb("ropeS", [128, T], F32, R_W1 + 8 * K)
    r_convT4 = [Res("convT_tc%d" % c, [(R_C + 16640 + 16 * K + j * 4096 + c * 1024,
                                        R_C + 16640 + 16 * K + j * 4096 + (c + 1) * 1024) for j in range(4)])
                for c in range(4)]
    mt, r_mt = sb("mt", [128, 512], F32, R_C + 8 * K)
    vt, r_vt = sb("vt", [128, 512], F32, R_C + 10 * K)
    zt, r_zt = [], []
    for i in range(2):
        a_, b_ = sb("zt%d" % i, [128, 512], F32, R_C + 12 * K + i * 2 * K)
        zt.append(a_); r_zt.append(b_)
    hTf, r_hTf = sb("hTf", [128, 4, T], BF16, R_C)
    r_hTf4 = [Res("hTf_tc%d" % c, [(R_C + ff * 4096 + c * 1024, R_C + ff * 4096 + (c + 1) * 1024) for ff in range(4)])
              for c in range(4)]
    Wpg, r_Wpg = sb("Wpg", [128, 8, D], BF16, R_C + 16 * K)
    Wpp, r_Wpp = sb("Wpp", [128, 2, D], BF16, R_C + 32 * K)
    lnp2, r_lnp2 = sb("lnp2", [128, 2, D], F32, R_C + 36 * K)
    lnp1_L0, r_lnp1_L0 = sb("lnp1a", [128, 2, D], F32, R_C)
    sigt1, r_sigt1 = sb("sigt1", [128, D], F32, R_C + 44 * K)
    yT, r_yT = sb("yT", [128, 8, T], BF16, R_C)
    ub, r_ub = [], []
    for i in range(2):
        a_, b_ = sb("ub%d" % i, [128, 2 + T], F32, R_C + 32 * K + i * 8224)
        ub.append(a_); r_ub.append(b_)
    lnp1_L1, r_lnp1_L1 = sb("lnp1b", [128, 2, D], F32, R_C + 32 * K)
    assert 32 * K + 2 * 8224 <= R_C_SZ
    Wmix, r_Wmix = sb("Wmix", [128, 8, D], BF16, R_W1)
    Wfo, r_Wfo = [], []
    for i in range(2):
        a_, b_ = sb("Wfo%d" % i, [128, 4, D], BF16, R_W1 + i * 8 * K)
        Wfo.append(a_); r_Wfo.append(b_)
    castd, r_castd = [], []
    for i in range(2):
        a_, b_ = sb("castd%d" % i, [128, D], BF16, R_W1 + i * 2 * K)
        castd.append(a_); r_castd.append(b_)
    ring, r_ring = [], []
    for i in range(4):
        a_, b_ = sb("ring%d" % i, [128, 8, 256], BF16, RING + i * 4 * K)
        ring.append(a_); r_ring.append(b_)
    mo = [MISC]

    def misc(name, shape, dt):
        n = int(np.prod(shape[1:])) * (4 if dt == F32 else 2)
        n = (n + 31) // 32 * 32
        t_, r_ = sb(name, shape, dt, mo[0])
        mo[0] += n
        return t_, r_

    def misc2(name, shape, dt):
        ts, rs = [], []
        for i in range(2):
            a_, b_ = misc("%s%d" % (name, i), shape, dt)
            ts.append(a_); rs.append(b_)
        return ts, rs

    ident, r_ident = misc("ident", [128, 128], BF16)
    identf, r_identf = misc("identf", [128, 128], F32)
    ones, r_ones = misc("ones", [128, 128], BF16)
    masks, r_masks = misc("masks", [128, 2, 128], BF16)
    maskf, r_maskf = misc("maskf", [128, 2, 128], F32)
    ccw, r_ccw = misc("ccw", [128, 4, 31], F32)
    ccb, r_ccb = misc("ccb", [128, 4], F32)
    clg, r_clg = misc("clg", [128, 4], F32)
    clb, r_clb = misc("clb", [128, 4], F32)
    ocw, r_ocw = misc("ocw", [128, 8, 3], F32)
    r_prm = [r_ccw, r_ccb, r_clg, r_clb, r_ocw]
    off_dg0 = mo[0]
    sgt, r_sgt = misc2("sgt", [128, 512], F32)
    ctmp, r_ctmp = misc2("ctmp", [128, 512], F32)
    off_dg1 = mo[0]
    castt, r_castt = misc2("castt", [128, D], BF16)
    x2T, r_x2T = misc2("x2T", [128, 8, 128], BF16)
    assert off_dg1 - off_dg0 == 8 * K and mo[0] - off_dg1 == 8 * K
    stg_a, r_stg_a = [], []
    for i in range(2):
        a_, b_ = sb("stg%d" % i, [128, D], BF16, off_dg1 + 4 * K + i * 2 * K)
        stg_a.append(a_); r_stg_a.append(b_)
    ysq, r_ysq = sb("ysq", [128, 4, 512], BF16, off_dg0)
    dgb, r_dgb = [], []
    for i, o_ in enumerate((off_dg0, off_dg1)):
        a_, b_ = sb("dgb%d" % i, [128, 31, 128], BF16, o_)
        dgb.append(a_); r_dgb.append(b_)
    pbt, r_pbt = misc2("pbt", [128, 256], BF16)
    pTt, r_pTt = misc2("pTt", [128, 2, 128], BF16)
    sigt0, r_sigt0 = misc("sigt", [128, D], F32)
    PT, r_PT = misc2("PT", [128, 512], BF16)
    sigt = [sigt0, sigt1]
    r_sigt = [r_sigt0, r_sigt1]
    stt, r_st = misc2("stt", [128, 2, 6], F32)
    mvall, _r = misc("mvall", [128, NT, 4], F32)
    r_mvall = [Res("mvall%d" % t) for t in range(NT)]
    assert mo[0] <= 229344, mo[0]

    PD = [nc.alloc_psum_tensor("pd%d" % i, [128, 1024], F32) for i in range(3)]
    PTB = [nc.alloc_psum_tensor("ptb%d" % i, [128, 1024], BF16) for i in range(2)]
    pd_res = [[Res("pd%d_%d" % (i, h)) for h in range(2)] for i in range(3)]
    ptb_res = [Res("ptb%d" % i) for i in range(2)]
    psc = [0, 0]
    psn = [6]

    def ps_half():
        i = psc[0] % psn[0]
        psc[0] += 1
        return PD[i // 2][:, (i % 2) * 512:(i % 2) * 512 + 512], [pd_res[i // 2][i % 2]]

    def ps_full():
        if psc[0] % 2 == 1:
            psc[0] += 1
        i = psc[0] % psn[0]
        psc[0] += 2
        return PD[i // 2], pd_res[i // 2]

    def ps_t():
        i = psc[1] % 2
        psc[1] += 1
        return PTB[i], [ptb_res[i]]

    ds_ring = [S.dsem("ds_ring%d" % i) for i in range(4)]
    ds_w1 = [S.dsem("ds_w1_%d" % i) for i in range(2)]
    ds_x = [S.dsem("ds_x%d" % i) for i in range(4)]
    ds_p = [S.dsem("ds_p%d" % i) for i in range(2)]
    ds_misc = S.dsem("ds_misc")
    ds_ln = S.dsem("ds_ln")
    ds_ple = S.dsem("ds_ple")
    ds_out = [S.dsem("ds_out%d" % i) for i in range(4)]
    ds_dbg = S.dsem("ds_dbg")

    pieces = []
    for j in range(4):
        pieces.append([(ewin_d[:, 4608 + j * 128:4608 + (j + 1) * 128], 0),
                       (ewin_d[:, 5120 + j * 128:5120 + (j + 1) * 128], 128)])
    for hp in range(2):
        for g in range(3):
            c0 = g * 512 + hp * 256
            for base in (0, 1536, 3072):
                pieces.append([(ewin_d[:, base + c0:base + c0 + 256], 0)])
    for l in range(2):
        if l == 1:
            for jj in range(4):
                for base in (0, 1024, 2048):
                    pieces.append([(owin_d[:, base + jj * 256:base + (jj + 1) * 256], 0)])
        for f in range(NF):
            pieces.append([(fwin_d[l][:, f * 128:(f + 1) * 128], 0),
                           (fwin_d[l][:, FF + f * 128:FF + (f + 1) * 128], 128)])
    wq = {"k": 0, "issued": 0, "done": 0}

    def _wissue():
        while wq["issued"] < len(pieces) and wq["issued"] - 4 < wq["done"]:
            m = wq["issued"]
            i = m % 4
            fns = []
            for (src, co) in pieces[m]:
                n = src.shape[1]
                fns.append(lambda e, src=src, co=co, n=n, i=i: e.dma_start(
                    out=ring[i][:, :, co:co + n], in_=src.rearrange("(kc p) n -> p kc n", p=128)))
            S.dma("pool", fns, ds_ring[i], writes=[r_ring[i]])
            wq["issued"] += 1

    def wnext():
        k = wq["k"]
        wq["k"] += 1
        _wissue()
        assert wq["issued"] > k
        return ring[k % 4], r_ring[k % 4]

    def wdone():
        wq["done"] += 1
        _wissue()

    S.op("pool", lambda e: e.memset(identf[:], 1.0), writes=[r_identf])
    S.op("pool", lambda e: e.affine_select(out=identf[:], in_=identf[:], pattern=[[-1, 128]],
                                           compare_op=ALU.is_equal, fill=0.0, base=0, channel_multiplier=1),
         rw=[r_identf])
    S.op("pool", lambda e: e.memset(maskf[:], 1.0), writes=[r_maskf])
    S.op("pool", lambda e: e.affine_select(out=maskf[:, 0, :], in_=maskf[:, 0, :], pattern=[[1, 128]],
                                           compare_op=ALU.is_ge, fill=0.0, base=0, channel_multiplier=-1),
         rw=[r_maskf])
    S.op("pool", lambda e: e.affine_select(out=maskf[:, 1, :], in_=maskf[:, 1, :], pattern=[[-1, 128]],
                                           compare_op=ALU.is_ge, fill=0.0, base=0, channel_multiplier=1),
         rw=[r_maskf])
    S.op("dve", lambda e: e.tensor_copy(out=ident[:], in_=identf[:]), reads=[r_identf], writes=[r_ident])
    S.op("dve", lambda e: e.tensor_copy(out=masks[:], in_=maskf[:]), reads=[r_maskf], writes=[r_masks])
    S.op("dve", lambda e: e.memset(ones[:], 1.0), writes=[r_ones])
    S.op("dve", lambda e: e.memset(hTc[:, :, 0:32], 0.0), writes=[r_hTc])
    S.dma("sp", [
        lambda e: e.dma_start(out=ccw[:], in_=ccw_d),
        lambda e: e.dma_start(out=ccb[:], in_=ccb_d),
        lambda e: e.dma_start(out=clg[:], in_=clg_d),
        lambda e: e.dma_start(out=clb[:], in_=clb_d),
        lambda e: e.dma_start(out=ocw[:], in_=ocw_d),
    ], ds_misc, writes=r_prm)
    S.dma("sp", [lambda e: e.dma_start(out=ropeC[:], in_=ropec_d),
                 lambda e: e.dma_start(out=ropeS[:], in_=ropes_d)], ds_misc, writes=[r_ropeC, r_ropeS])

    def transpose_to(src_bf, r_src, dst_ap_fn, r_dst, nchunk, evac_eng="dve"):
        pt, rpt = ps_t()
        fns = [lambda e, c=c, pt=pt: e.transpose(out=pt[:, c * 128:(c + 1) * 128], in_=src_bf[:, c * 128:(c + 1) * 128],
                                                 identity=ident[:]) for c in range(nchunk)]
        S.mm(fns, reads=[r_src, r_ident], writes=rpt)
        src = pt[:, 0:nchunk * 128].rearrange("p (c n) -> p c n", c=nchunk)
        if evac_eng == "dve":
            return S.op("dve", lambda e: e.tensor_copy(out=dst_ap_fn(), in_=src), reads=rpt, writes=[r_dst])
        else:
            return S.op("act", lambda e: e.activation(out=dst_ap_fn(), in_=src, func=AF.Copy), reads=rpt, writes=[r_dst])

    def dbg_out(name, tensor_ap, shape, reads):
        d = nc.dram_tensor("dbg_" + name, list(shape), tensor_ap.dtype, kind="ExternalOutput").ap()
        dbg_d[name] = d
        S.dma("sp", [lambda e: e.dma_start(out=d, in_=tensor_ap)], ds_dbg, reads=reads)

    def finish():
        toks = []
        for ds in ds_out + [ds_dbg]:
            if ds.n:
                toks.append((ds.sem, 16 * ds.n, id(ds)))
        S.wait("sp", toks)
        S.emit()
        return nc

    def ln_stats(t, k2):
        st = stt[k2]
        S.op("dve", lambda e: e.bn_stats(out=st[:, 0, :], in_=x_tok[:, t, 0:512]), reads=[r_xtok[t]], writes=[r_st[k2]])
        S.op("dve", lambda e: e.bn_stats(out=st[:, 1, :], in_=x_tok[:, t, 512:1024]), reads=[r_xtok[t]], writes=[r_st[k2]])
        S.op("dve", lambda e: e.bn_aggr(out=mvall[:, t, 0:2], in_=st[:]), reads=[r_st[k2]], writes=[r_mvall[t]])
        S.op("dve", lambda e: e.tensor_scalar_mul(out=mvall[:, t, 3:4], in0=mvall[:, t, 0:1], scalar1=-1.0), rw=[r_mvall[t]])

    def ln_rstd_tile(t):
        S.op("act", lambda e: e.activation(out=mvall[:, t, 2:3], in_=mvall[:, t, 1:2], func=AF.Ln, bias=EPS), rw=[r_mvall[t]])
        S.op("act", lambda e: e.activation(out=mvall[:, t, 2:3], in_=mvall[:, t, 2:3], func=AF.Exp, scale=-0.5), rw=[r_mvall[t]])
        S.op("act", lambda e: e.activation(out=mvall[:, t, 3:4], in_=mvall[:, t, 3:4], func=AF.Copy, scale=mvall[:, t, 2:3]),
             rw=[r_mvall[t]])

    def ln_rstd_all():
        S.op("act", lambda e: e.activation(out=mvall[:, :, 2], in_=mvall[:, :, 1], func=AF.Sqrt, bias=EPS), rw=r_mvall)
        S.op("dve", lambda e: e.reciprocal(out=mvall[:, :, 2], in_=mvall[:, :, 2]), rw=r_mvall)
        S.op("dve", lambda e: e.tensor_tensor(out=mvall[:, :, 3], in0=mvall[:, :, 3], in1=mvall[:, :, 2], op=ALU.mult), rw=r_mvall)

    def ln_apply(t):
        xt = x_tok[:, t, :]
        S.op("act", lambda e: e.activation(out=xt, in_=xt, func=AF.Identity, scale=mvall[:, t, 2:3], bias=mvall[:, t, 3:4]),
             reads=[r_mvall[t]], rw=[r_xtok[t]])

    def ln_affine(t, lnp, r_lnp, eg="dve", eb="pool"):
        xt = x_tok[:, t, :]
        S.op(eg, lambda e: e.tensor_tensor(out=xt, in0=xt, in1=lnp[:, 0, :], op=ALU.mult),
             reads=[r_lnp], rw=[r_xtok[t]])
        S.op(eb, lambda e: e.tensor_tensor(out=xt, in0=xt, in1=lnp[:, 1, :], op=ALU.add),
             reads=[r_lnp], rw=[r_xtok[t]])

    def pipeline(stages, n, pre_step=None):
        ns = len(stages)
        for s_ in range(n + ns - 1):
            if pre_step is not None:
                pre_step(s_)
            for k_, f_ in enumerate(stages):
                t_ = s_ - k_
                if 0 <= t_ < n:
                    f_(t_)

    def cast_and_transpose(t, dst_fn, r_dst, k2, evac_eng="dve"):
        cb = castt[k2]
        S.op("act", lambda e: e.activation(out=cb[:], in_=x_tok[:, t, :], func=AF.Copy),
             reads=[r_xtok[t]], writes=[r_castt[k2]])
        transpose_to(cb, r_castt[k2], dst_fn, r_dst, 8, evac_eng)

    def load_lnp(lnp, r_lnp, g_d, b_d, l):
        S.dma("sp", [lambda e: e.dma_start(out=lnp[:, 0, :], in_=g_d[l:l + 1, :].broadcast_to([128, D])),
                     lambda e: e.dma_start(out=lnp[:, 1, :], in_=b_d[l:l + 1, :].broadcast_to([128, D]))],
              ds_ln, writes=[r_lnp])

    for t in range(NT):
        S.dma("sp", [lambda e, t=t: e.dma_start(out=x_tok[:, t, :], in_=x_d[t * 128:(t + 1) * 128, :])],
              ds_x[2 + (t % 2)], writes=[r_xtok[t]])
    stg = [(castt[0], r_castt[0]), (castt[1], r_castt[1]), (stg_a[0], r_stg_a[0]), (stg_a[1], r_stg_a[1])]

    def p0_cast(t):
        cb, r_cb = stg[t % 4]
        if t % 2 == 0:
            S.op("act", lambda e: e.activation(out=cb[:], in_=x_tok[:, t, :], func=AF.Copy),
                 reads=[r_xtok[t]], writes=[r_cb])
        else:
            S.op("dve", lambda e: e.tensor_copy(out=cb[:], in_=x_tok[:, t, :]), reads=[r_xtok[t]], writes=[r_cb])

    p0_cast(0)
    p0_cast(1)
    glu_slots = [wnext() for _ in range(4)]
    gk = [0]
    for t in range(NT):
        if t + 2 < NT:
            p0_cast(t + 2)
        cb, r_cb = stg[t % 4]
        transpose_to(cb, r_cb, lambda t=t: xT[:, :, t * 128:(t + 1) * 128], r_xT[t // 4], 8,
                     "dve" if t % 2 == 0 else "act")
        if t % 4 != 3:
            continue
        tc = t // 4
        for j in range(4):
            slot, rs = glu_slots[j]
            k2 = gk[0] % 2
            gk[0] += 1
            ba, ra = ps_half()
            bg, rg = ps_half()
            fa = [lambda e, d=d, ba=ba, tc=tc, slot=slot: e.matmul(ba, lhsT=slot[:, d, 0:128], rhs=xT[:, d, tc * 512:(tc + 1) * 512],
                                                                   start=(d == 0), stop=(d == 7)) for d in range(8)]
            S.mm(fa, reads=[rs, r_xT[tc]], writes=ra)
            fg = [lambda e, d=d, bg=bg, tc=tc, slot=slot: e.matmul(bg, lhsT=slot[:, d, 128:256], rhs=xT[:, d, tc * 512:(tc + 1) * 512],
                                                                   start=(d == 0), stop=(d == 7)) for d in range(8)]
            S.mm(fg, reads=[rs, r_xT[tc]], writes=rg)
            S.op("act", lambda e, bg=bg, k2=k2: e.activation(out=sgt[k2][:], in_=bg, func=AF.Sigmoid),
                 reads=rg, writes=[r_sgt[k2]])
            S.op("dve", lambda e, ba=ba, k2=k2, j=j, tc=tc: e.tensor_tensor(
                out=hTc[:, j, 32 + tc * 512:32 + (tc + 1) * 512], in0=ba, in1=sgt[k2][:], op=ALU.mult),
                reads=ra + [r_sgt[k2]], writes=[r_hTc])
    for _ in range(4):
        wdone()

    if stop_after == "glu":
        dbg_out("hTc", hTc[:], [128, 4, HP], [r_hTc])
        dbg_out("xT", xT[:], [128, 8, T], r_xT)
        return finish(), dbg_d

    conv_groups = [(j, tc) for j in range(4) for tc in range(4)]
    cg = {"i": 0, "tap": 0, "n": 0}
    conv_banks = [(PD[2][:, 0:512], [pd_res[2][0]]), (PD[2][:, 512:1024], [pd_res[2][1]])]

    def build_dg(j):
        S.op("pool", lambda e, j=j: e.tensor_tensor(
            out=dgb[j % 2][:], in0=ident[:].unsqueeze(1).broadcast_to([128, 31, 128]),
            in1=ccw[:, j, :].unsqueeze(2).broadcast_to([128, 31, 128]), op=ALU.mult),
            reads=[r_ident, r_ccw], writes=[r_dgb[j % 2]])

    def conv_emit(n):
        while n > 0 and cg["i"] < len(conv_groups):
            j, tc = conv_groups[cg["i"]]
            bk, rb = conv_banks[cg["i"] % 2]
            tap = cg["tap"]
            fn = lambda e, tap=tap, bk=bk, j=j, tc=tc: e.matmul(
                bk, lhsT=dgb[j % 2][:, tap, :], rhs=hTc[:, j, tc * 512 + 2 + tap:tc * 512 + 2 + tap + 512],
                start=(tap == 0), stop=(tap == 30))
            if tap < 30:
                S.mm_nosig([fn], reads=[r_dgb[j % 2], r_hTc], writes=(rb if tap == 0 else ()))
                cg["tap"] += 1
            else:
                S.mm([fn], reads=[r_dgb[j % 2], r_hTc], writes=rb)
                S.op("act", lambda e, bk=bk, j=j, tc=tc: e.activation(out=convT[:, j, tc * 512:(tc + 1) * 512], in_=bk,
                                                                    func=AF.Identity, bias=ccb[:, j:j + 1]),
                     reads=rb + [r_ccb], writes=[r_convT4[tc]])
                cg["tap"] = 0
                cg["i"] += 1
                if tc == 3 and j + 2 < 4:
                    build_dg(j + 2)
            n -= 1

    build_dg(0)
    build_dg(1)
    psn[0] = 4
    ri = [0]
    for hp in range(2):
        for g in range(3):
            dil = DILS[g]
            L = T // dil
            nb = L // 128
            wq_, rwq = wnext()
            wk_, rwk = wnext()
            wv, rwv = wnext()
            J = 512 // dil
            for (w_, rw_, dstT, r_dst) in ((wq_, rwq, qT, r_qT), (wk_, rwk, kT, r_kT)):
                for h in range(2):
                    for tc in range(4):
                        k2 = ri[0] % 2
                        ri[0] += 1
                        bk, rb = ps_half()
                        fns = [lambda e, d=d, bk=bk, w_=w_, h=h, tc=tc: e.matmul(
                            bk, lhsT=w_[:, d, h * 128:(h + 1) * 128], rhs=xT[:, d, tc * 512:(tc + 1) * 512],
                            start=(d == 0), stop=(d == 7)) for d in range(8)]
                        S.mm(fns, reads=[rw_, r_xT[tc]], writes=rb)
                        t1, t2 = rtmp1[k2], rtmp2[k2]
                        S.op("dve", lambda e, bk=bk, t1=t1, tc=tc: e.tensor_tensor(
                            out=t1[:], in0=bk, in1=ropeC[:, tc * 512:(tc + 1) * 512], op=ALU.mult),
                            reads=rb + [r_ropeC], writes=[r_rtmp1[k2]])
                        S.op("dve", lambda e, bk=bk, t2=t2, tc=tc: e.tensor_tensor(
                            out=t2[0:64, :], in0=bk[64:128, :], in1=ropeS[64:128, tc * 512:(tc + 1) * 512], op=ALU.mult),
                            reads=rb + [r_ropeS], writes=[r_rtmp2[k2]])
                        S.op("dve", lambda e, bk=bk, t2=t2, tc=tc: e.tensor_tensor(
                            out=t2[64:128, :], in0=bk[0:64, :], in1=ropeS[0:64, tc * 512:(tc + 1) * 512], op=ALU.mult),
                            reads=rb + [r_ropeS], writes=[r_rtmp2[k2]])
                        j0 = tc * 512 // dil
                        S.op("pool", lambda e, t1=t1, t2=t2, dstT=dstT, h=h, dil=dil, j0=j0, J=J: e.tensor_tensor(
                            out=dstT[:, h, :].rearrange("p (r j) -> p r j", r=dil)[:, :, j0:j0 + J],
                            in0=t1[:].rearrange("p (j r) -> p r j", r=dil),
                            in1=t2[:].rearrange("p (j r) -> p r j", r=dil), op=ALU.add),
                            reads=[r_rtmp1[k2], r_rtmp2[k2]], writes=[r_dst])
                wdone()
            for blk in range(16):
                r_, n_ = divmod(blk, nb)
                st0 = n_ * 128 * dil + r_
                bk, rb = ps_half()
                fns = [lambda e, d=d, bk=bk, st0=st0, dil=dil, wv=wv: e.matmul(
                    bk[:, 0:256], lhsT=xT[:, d, st0:st0 + 127 * dil + 1:dil], rhs=wv[:, d, :],
                    start=(d == 0), stop=(d == 7)) for d in range(8)]
                S.mm(fns, reads=[rwv] + r_xT, writes=rb)
                S.op("act", lambda e, bk=bk, blk=blk: e.activation(out=Vg[:, blk, :], in_=bk[:, 0:256], func=AF.Copy),
                     reads=rb, writes=[r_Vg])
            wdone()
            for blk in range(16):
                r_, n_ = divmod(blk, nb)
                hasp = n_ > 0
                cc = blk * 128
                cp = (blk - 1) * 128
                k2 = blk % 2
                bS, rS = ps_half()
                fns = []
                for h in range(2):
                    fns.append(lambda e, h=h, bS=bS, cc=cc: e.matmul(
                        bS[:, h * 128:(h + 1) * 128], lhsT=kT[:, h, cc:cc + 128], rhs=qT[:, h, cc:cc + 128],
                        start=True, stop=True))
                if hasp:
                    for h in range(2):
                        fns.append(lambda e, h=h, bS=bS, cc=cc, cp=cp: e.matmul(
                            bS[:, 256 + h * 128:256 + (h + 1) * 128], lhsT=kT[:, h, cp:cp + 128], rhs=qT[:, h, cc:cc + 128],
                            start=True, stop=True))
                S.mm(fns, reads=[r_qT, r_kT], writes=rS)
                conv_emit(6)
                ncp = 2 if hasp else 1
                ncol = 256 * ncp
                S.op("act", lambda e, bS=bS, k2=k2, ncol=ncol: e.activation(
                    out=PT[k2][:, 0:ncol], in_=bS[:, 0:ncol], func=AF.Exp, scale=SCALE),
                    reads=rS, writes=[r_PT[k2]])
                S.op("dve", lambda e, k2=k2, ncol=ncol, ncp=ncp: e.tensor_tensor(
                    out=PT[k2][:, 0:ncol].rearrange("p (c h q) -> p c h q", c=ncp, h=2),
                    in0=PT[k2][:, 0:ncol].rearrange("p (c h q) -> p c h q", c=ncp, h=2),
                    in1=masks[:, 0:ncp, :].unsqueeze(2).broadcast_to([128, ncp, 2, 128]), op=ALU.mult),
                    reads=[r_masks], rw=[r_PT[k2]])
                bU, rU = ps_half()
                fns = []
                for h in range(2):
                    fns.append(lambda e, h=h, bU=bU, blk=blk, k2=k2, hasp=hasp: e.matmul(
                        bU[:, h * 128:(h + 1) * 128], lhsT=Vg[:, blk, h * 128:(h + 1) * 128],
                        rhs=PT[k2][:, h * 128:(h + 1) * 128], start=True, stop=(not hasp)))
                    if hasp:
                        fns.append(lambda e, h=h, bU=bU, blk=blk, k2=k2: e.matmul(
                            bU[:, h * 128:(h + 1) * 128], lhsT=Vg[:, blk - 1, h * 128:(h + 1) * 128],
                            rhs=PT[k2][:, 256 + h * 128:256 + (h + 1) * 128], start=False, stop=True))
                fns.append(lambda e, bU=bU, k2=k2, hasp=hasp: e.matmul(
                    bU[:, 256:512], lhsT=ones[:], rhs=PT[k2][:, 0:256], start=True, stop=(not hasp)))
                if hasp:
                    fns.append(lambda e, bU=bU, k2=k2: e.matmul(
                        bU[:, 256:512], lhsT=ones[:], rhs=PT[k2][:, 256:512], start=False, stop=True))
                S.mm(fns, reads=[r_Vg, r_PT[k2], r_ones], writes=rU)
                st0 = n_ * 128 * dil + r_
                dst = UD[:, :, :, st0:st0 + 127 * dil + 1:dil]
                src = bU.rearrange("p (u h q) -> p u h q", u=2, h=2)
                if g == 0:
                    S.op("act", lambda e, dst=dst, src=src: e.activation(out=dst, in_=src, func=AF.Copy), reads=rU, writes=[r_UD])
                else:
                    S.op("dve", lambda e, dst=dst, src=src: e.tensor_tensor(out=dst, in0=src, in1=dst, op=ALU.add),
                         reads=rU, rw=[r_UD])
        S.op("act", lambda e: e.activation(out=UD[:, 1, :, :], in_=UD[:, 1, :, :], func=AF.Ln), rw=[r_UD])
        S.op("act", lambda e: e.activation(out=UD[:, 1, :, :], in_=UD[:, 1, :, :], func=AF.Exp, scale=-1.0), rw=[r_UD])
        S.op("dve", lambda e, hp=hp: e.tensor_tensor(out=attnT[:, 2 * hp:2 * hp + 2, :], in0=UD[:, 0, :, :],
                                                     in1=UD[:, 1, :, :], op=ALU.mult),
             reads=[r_UD], writes=[r_attnT])

    if stop_after == "attn":
        dbg_out("attnT", attnT[:], [128, 4, T], [r_attnT])
        dbg_out("hTc", hTc[:], [128, 4, HP], [r_hTc])
        return finish(), dbg_d

    conv_emit(10 ** 6)
    psn[0] = 6

    def conv_ln(tc):
        cs = slice(tc * 512, (tc + 1) * 512)
        for j in range(4):
            S.op("dve", lambda e, j=j, cs=cs: e.tensor_tensor(out=ysq[:, j, :], in0=convT[:, j, cs], in1=convT[:, j, cs], op=ALU.mult),
                 reads=[r_convT4[tc]], writes=[r_ysq])
        bm, rm = ps_half()
        S.mm([lambda e, j=j, bm=bm, cs=cs: e.matmul(bm, lhsT=ones[:], rhs=convT[:, j, cs], start=(j == 0), stop=(j == 3))
              for j in range(4)], reads=[r_convT4[tc], r_ones], writes=rm)
        bq, rq = ps_half()
        S.mm([lambda e, j=j, bq=bq: e.matmul(bq, lhsT=ones[:], rhs=ysq[:, j, :], start=(j == 0), stop=(j == 3))
              for j in range(4)], reads=[r_ysq, r_ones], writes=rq)
        S.op("dve", lambda e, bm=bm: e.tensor_scalar_mul(out=mt[:], in0=bm, scalar1=1.0 / 512), reads=rm, writes=[r_mt])
        S.op("dve", lambda e: e.tensor_tensor(out=vt[:], in0=mt[:], in1=mt[:], op=ALU.mult), reads=[r_mt], writes=[r_vt])
        S.op("dve", lambda e, bq=bq: e.scalar_tensor_tensor(out=vt[:], in0=bq, scalar=1.0 / 512, in1=vt[:],
                                                            op0=ALU.mult, op1=ALU.subtract), reads=rq, rw=[r_vt])
        S.op("act", lambda e: e.activation(out=vt[:], in_=vt[:], func=AF.Ln, bias=EPS), rw=[r_vt])
        S.op("act", lambda e: e.activation(out=vt[:], in_=vt[:], func=AF.Exp, scale=-0.5), rw=[r_vt])
        for j in range(4):
            k2 = j % 2
            S.op("dve", lambda e, j=j, k2=k2, cs=cs: e.tensor_tensor(out=zt[k2][:], in0=convT[:, j, cs], in1=mt[:], op=ALU.subtract),
                 reads=[r_convT4[tc], r_mt], writes=[r_zt[k2]])
            S.op("dve", lambda e, k2=k2: e.tensor_tensor(out=zt[k2][:], in0=zt[k2][:], in1=vt[:], op=ALU.mult),
                 reads=[r_vt], rw=[r_zt[k2]])
            S.op("act", lambda e, j=j, k2=k2, cs=cs: e.activation(
                out=convT[:, j, cs], in_=zt[k2][:], func=AF.Silu, scale=clg[:, j:j + 1], bias=clb[:, j:j + 1]),
                reads=[r_zt[k2], r_clg, r_clb], rw=[r_convT4[tc]])

    if stop_after == "conv":
        for tc in range(4):
            conv_ln(tc)
        dbg_out("convT", convT[:], [128, 4, T], r_convT4)
        dbg_out("attnT", attnT[:], [128, 4, T], [r_attnT])
        return finish(), dbg_d

    def mixer_out_and_ln1(l, cat_fn, r_cat_fn, w_d, lnp1, r_lnp1, x_from_hbm, pre_step=None):
        def xload(t):
            S.dma("sp", [lambda e, t=t: e.dma_start(out=x_tok[:, t, :], in_=x_d[t * 128:(t + 1) * 128, :])],
                  ds_x[2 + (t % 2)], writes=[r_xtok[t]])
        if x_from_hbm:
            for t in range(4):
                xload(t)
        S.dma("pool", [lambda e: e.dma_start(out=Wmix[:], in_=w_d.rearrange("(kc p) n -> p kc n", p=128))],
              ds_w1[0], writes=[r_Wmix])
        load_lnp(lnp1, r_lnp1, lmg_d, lmb_d, l)
        if x_from_hbm:
            for t in range(4, NT):
                xload(t)
        def A1(t):
            k2 = t % 2
            bD, rD = ps_full()
            fns = []
            for half in range(2):
                for f in range(8):
                    fns.append(lambda e, half=half, f=f, bD=bD, t=t: e.matmul(
                        bD[:, half * 512:(half + 1) * 512], lhsT=cat_fn(f, t), rhs=Wmix[:, f, half * 512:(half + 1) * 512],
                        start=(f == 0), stop=(f == 7)))
            S.mm(fns, reads=r_cat_fn(t) + [r_Wmix], writes=rD)
            S.op("dve", lambda e, bD=bD, t=t: e.scalar_tensor_tensor(
                out=x_tok[:, t, :], in0=x_tok[:, t, :], scalar=ALPHA, in1=bD[:], op0=ALU.mult, op1=ALU.add),
                reads=rD, rw=[r_xtok[t]])
            ln_stats(t, k2)

        def A2(t):
            ln_rstd_tile(t)
            ln_apply(t)

        pipeline([A1, A2,
                  lambda t: ln_affine(t, lnp1, r_lnp1, "pool", "pool"),
                  lambda t: cast_and_transpose(t, lambda t=t: xT[:, :, t * 128:(t + 1) * 128], r_xT[t // 4], t % 2, "act")],
                 NT, pre_step)

    for tc_ in range(4):
        conv_ln(tc_)

    def l0_pre(s_):
        pass

    mixer_out_and_ln1(0, lambda f, t: (attnT[:, f, t * 128:(t + 1) * 128] if f < 4 else convT[:, f - 4, t * 128:(t + 1) * 128]),
                      lambda t: [r_attnT, r_convT4[t // 4]], ewout_d, lnp1_L0, r_lnp1_L0, True, l0_pre)

    if stop_after == "ln1":
        for t in range(NT):
            dbg_out("x1_%d" % t, x_tok[:, t, :], [128, D], [r_xtok[t]])
        return finish(), dbg_d

    FG = [(0, 4), (4, 8), (8, 12), (12, 16), (16, 19), (19, 22)]

    def ffn_and_ple(l, last_layer):
        fw_out = fwout_d[l]
        for pi, (f0, f1) in enumerate(FG):
            nf = f1 - f0
            wb = pi % 2
            S.dma("pool", [lambda e, wb=wb, f0=f0, f1=f1, nf=nf: e.dma_start(
                out=Wfo[wb][:, 0:nf, :], in_=fw_out[f0 * 128:f1 * 128, :].rearrange("(kc p) n -> p kc n", p=128))],
                ds_w1[wb], writes=[r_Wfo[wb]])
            def inproj(f, tc, slot, rs, f0=f0):
                k2 = tc % 2
                bg, rg = ps_half()
                bu, ru = ps_half()
                S.mm([lambda e, d=d, bg=bg, tc=tc, slot=slot: e.matmul(
                    bg, lhsT=slot[:, d, 0:128], rhs=xT[:, d, tc * 512:(tc + 1) * 512], start=(d == 0), stop=(d == 7))
                    for d in range(8)], reads=[rs, r_xT[tc]], writes=rg)
                S.mm([lambda e, d=d, bu=bu, tc=tc, slot=slot: e.matmul(
                    bu, lhsT=slot[:, d, 128:256], rhs=xT[:, d, tc * 512:(tc + 1) * 512], start=(d == 0), stop=(d == 7))
                    for d in range(8)], reads=[rs, r_xT[tc]], writes=ru)
                S.op("act", lambda e, bg=bg, k2=k2: e.activation(out=sgt[k2][:], in_=bg, func=AF.Silu),
                     reads=rg, writes=[r_sgt[k2]])
                S.op("dve", lambda e, bu=bu, k2=k2, f=f, f0=f0, tc=tc: e.tensor_tensor(
                    out=hTf[:, f - f0, tc * 512:(tc + 1) * 512], in0=bu, in1=sgt[k2][:], op=ALU.mult),
                    reads=ru + [r_sgt[k2]], writes=[r_hTf4[tc]])

            is_last = (pi == len(FG) - 1)
            if not is_last:
                for f in range(f0, f1):
                    slot, rs = wnext()
                    for tc in range(4):
                        inproj(f, tc, slot, rs)
                    wdone()
            if pi == 0:
                load_lnp(lnp2, r_lnp2, lfg_d, lfb_d, l)
                S.dma("pool", [lambda e: e.dma_start(out=Wpg[:], in_=pwg_d[l].rearrange("(kc p) n -> p kc n", p=128)),
                               lambda e: e.dma_start(out=Wpp[:], in_=pwp_d[l].rearrange("(kc p) n -> p kc n", p=128))],
                      ds_ple, writes=[r_Wpg, r_Wpp])
            last = (pi == len(FG) - 1)

            def A1(t, nf=nf, wb=wb, pi=pi, last=last):
                k2 = t % 2
                bD, rD = ps_full()
                fns = []
                for half in range(2):
                    for ff in range(nf):
                        fns.append(lambda e, half=half, ff=ff, bD=bD, t=t, wb=wb, nf=nf: e.matmul(
                            bD[:, half * 512:(half + 1) * 512], lhsT=hTf[:, ff, t * 128:(t + 1) * 128],
                            rhs=Wfo[wb][:, ff, half * 512:(half + 1) * 512], start=(ff == 0), stop=(ff == nf - 1)))
                S.mm(fns, reads=[r_hTf4[t // 4], r_Wfo[wb]], writes=rD)
                if pi == 0:
                    S.op("dve", lambda e, bD=bD, t=t: e.scalar_tensor_tensor(
                        out=x_tok[:, t, :], in0=x_tok[:, t, :], scalar=ALPHA, in1=bD[:], op0=ALU.mult, op1=ALU.add),
                        reads=rD, rw=[r_xtok[t]])
                else:
                    S.op("dve", lambda e, bD=bD, t=t: e.tensor_tensor(
                        out=x_tok[:, t, :], in0=x_tok[:, t, :], in1=bD[:], op=ALU.add), reads=rD, rw=[r_xtok[t]])
                if last:
                    ln_stats(t, k2)

            if not last:
                for t in range(NT):
                    A1(t)
                continue
            slots = [wnext() for _ in range(f0, f1)]
            for tc in range(4):
                for fi, f in enumerate(range(f0, f1)):
                    inproj(f, tc, slots[fi][0], slots[fi][1])
                if tc >= 1:
                    for t in range(4 * (tc - 1), 4 * tc):
                        A1(t)
            for _ in range(f0, f1):
                wdone()
            for t in range(12, 16):
                A1(t)
            ln_rstd_all()

            def cast_only(t, cb, r_cb):
                S.op("act", lambda e: e.activation(out=cb[:], in_=x_tok[:, t, :], func=AF.Copy),
                     reads=[r_xtok[t]], writes=[r_cb])

            def ok(t):
                return 0 <= t < NT

            for s_ in range(NT + 6):
                t = s_ - 2
                if ok(t):
                    k2 = t % 2
                    transpose_to(castt[k2], r_castt[k2], lambda k2=k2: x2T[k2][:], r_x2T[k2], 8, "dve")
                t = s_ - 3
                if ok(t):
                    k2 = t % 2
                    transpose_to(pbt[k2], r_pbt[k2], lambda k2=k2: pTt[k2][:], r_pTt[k2], 2, "act")
                t = s_ - 6
                if ok(t) and not last_layer:
                    k2 = t % 2
                    transpose_to(castd[k2], r_castd[k2], lambda t=t: xT[:, :, t * 128:(t + 1) * 128], r_xT[t // 4], 8, "dve")
                t = s_
                if ok(t):
                    ln_apply(t)
                    ln_affine(t, lnp2, r_lnp2, "dve", "dve")
                t = s_ - 1
                if ok(t):
                    k2 = t % 2
                    cast_only(t, castt[k2], r_castt[k2])
                    S.dma("pool", [lambda e, t=t, k2=k2: e.dma_start(out=pbt[k2][:], in_=p_d[l, t * 128:(t + 1) * 128, :])],
                          ds_p[k2], writes=[r_pbt[k2]])
                t = s_ - 5
                if ok(t):
                    if last_layer:
                        S.dma("sp", [lambda e, t=t: e.dma_start(out=out_d[t * 128:(t + 1) * 128, :], in_=x_tok[:, t, :])],
                              ds_out[t % 4], reads=[r_xtok[t]])
                    else:
                        cast_only(t, castd[t % 2], r_castd[t % 2])
                tg = s_ - 3
                bG = rG = None
                if ok(tg):
                    k2 = tg % 2
                    bG, rG = ps_full()
                    fns = []
                    for half in range(2):
                        for c in range(8):
                            fns.append(lambda e, half=half, c=c, bG=bG, k2=k2: e.matmul(
                                bG[:, half * 512:(half + 1) * 512], lhsT=x2T[k2][:, c, :], rhs=Wpg[:, c, half * 512:(half + 1) * 512],
                                start=(c == 0), stop=(c == 7)))
                    S.mm(fns, reads=[r_x2T[k2], r_Wpg], writes=rG)
                t = s_ - 4
                if ok(t):
                    k2 = t % 2
                    sg_ = sigt[k2]
                    bP, rP = ps_full()
                    fns = []
                    for half in range(2):
                        for c in range(2):
                            fns.append(lambda e, half=half, c=c, bP=bP, k2=k2: e.matmul(
                                bP[:, half * 512:(half + 1) * 512], lhsT=pTt[k2][:, c, :], rhs=Wpp[:, c, half * 512:(half + 1) * 512],
                                start=(c == 0), stop=(c == 1)))
                    S.mm(fns, reads=[r_pTt[k2], r_Wpp], writes=rP)
                    S.op("dve", lambda e, bP=bP, sg_=sg_: e.tensor_tensor(out=sg_[:], in0=bP[:], in1=sg_[:], op=ALU.mult),
                         reads=rP, rw=[r_sigt[k2]])
                    S.op("pool", lambda e, t=t, sg_=sg_: e.tensor_tensor(out=x_tok[:, t, :], in0=x_tok[:, t, :], in1=sg_[:], op=ALU.add),
                         reads=[r_sigt[k2]], rw=[r_xtok[t]])
                if ok(tg):
                    k2 = tg % 2
                    sg_ = sigt[k2]
                    S.op("act", lambda e, bG=bG, sg_=sg_: e.activation(out=sg_[:], in_=bG[:], func=AF.Sigmoid),
                         reads=rG, writes=[r_sigt[k2]])

    ffn_and_ple(0, False)

    if stop_after == "l0":
        for t in range(NT):
            dbg_out("x3_%d" % t, x_tok[:, t, :], [128, D], [r_xtok[t]])
        return finish(), dbg_d

    for i in range(2):
        S.op("dve", lambda e, i=i: e.memset(ub[i][:, 0:2], 0.0), writes=[r_ub[i]])
    ui = [0]
    for jj in range(4):
        wb_, rwb = wnext()
        wc_, rwc = wnext()
        wh_, rwh = wnext()
        for j2 in range(2):
            j = 2 * jj + j2
            ubi = ui[0] % 2
            ui[0] += 1
            u = ub[ubi]
            for tc in range(4):
                k2 = tc % 2
                bb, rbb = ps_half()
                bc, rbc = ps_half()
                bh, rbh = ps_half()
                for (bk, rb, w_, rw_) in ((bb, rbb, wb_, rwb), (bc, rbc, wc_, rwc), (bh, rbh, wh_, rwh)):
                    S.mm([lambda e, d=d, bk=bk, w_=w_, j2=j2, tc=tc: e.matmul(
                        bk, lhsT=w_[:, d, j2 * 128:(j2 + 1) * 128], rhs=xT[:, d, tc * 512:(tc + 1) * 512],
                        start=(d == 0), stop=(d == 7)) for d in range(8)], reads=[rw_, r_xT[tc]], writes=rb)
                S.op("act", lambda e, bc=bc, k2=k2: e.activation(out=sgt[k2][:], in_=bc, func=AF.Copy),
                     reads=rbc, writes=[r_sgt[k2]])
                S.op("dve", lambda e, bh=bh, k2=k2, u=u, tc=tc: e.tensor_tensor(
                    out=u[:, 2 + tc * 512:2 + (tc + 1) * 512], in0=bh, in1=sgt[k2][:], op=ALU.mult),
                    reads=rbh + [r_sgt[k2]], rw=[r_ub[ubi]])
                ct = ctmp[k2]
                S.op("dve", lambda e, u=u, ct=ct, tc=tc, j=j: e.tensor_scalar_mul(
                    out=ct[:], in0=u[:, 2 + tc * 512:2 + (tc + 1) * 512], scalar1=ocw[:, j, 2:3]),
                    reads=[r_ub[ubi], r_ocw], writes=[r_ctmp[k2]])
                S.op("dve", lambda e, u=u, ct=ct, tc=tc, j=j: e.scalar_tensor_tensor(
                    out=ct[:], in0=u[:, 1 + tc * 512:1 + (tc + 1) * 512], scalar=ocw[:, j, 1:2], in1=ct[:],
                    op0=ALU.mult, op1=ALU.add), reads=[r_ub[ubi]], rw=[r_ctmp[k2]])
                S.op("dve", lambda e, u=u, ct=ct, tc=tc, j=j: e.scalar_tensor_tensor(
                    out=ct[:], in0=u[:, tc * 512:(tc + 1) * 512], scalar=ocw[:, j, 0:1], in1=ct[:],
                    op0=ALU.mult, op1=ALU.add), reads=[r_ub[ubi]], rw=[r_ctmp[k2]])
                S.op("dve", lambda e, bb=bb, ct=ct, j=j, tc=tc: e.tensor_tensor(
                    out=yT[:, j, tc * 512:(tc + 1) * 512], in0=bb, in1=ct[:], op=ALU.mult),
                    reads=rbb + [r_ctmp[k2]], writes=[r_yT])
        wdone(); wdone(); wdone()

    mixer_out_and_ln1(1, lambda f, t: yT[:, f, t * 128:(t + 1) * 128], lambda t: [r_yT], owout_d, lnp1_L1, r_lnp1_L1, False)
    ffn_and_ple(1, True)
    return finish(), dbg_d


_CACHE = {}


def _rope_tables():
    half = 64
    inv = 10000.0 ** (-np.arange(half, dtype=np.float64) / half)
    pos = np.arange(T, dtype=np.float64)
    ang = pos[:, None] * inv[None, :]
    c = np.cos(ang).T.astype(np.float32)
    s = np.sin(ang).T.astype(np.float32)
    C = np.concatenate([c, c], axis=0)
    SW = np.concatenate([s, -s], axis=0)
    return np.ascontiguousarray(C), np.ascontiguousarray(SW)


def make_in_maps(inputs):
    f = lambda a: np.ascontiguousarray(np.asarray(a, dtype=np.float32))
    C, SW = _rope_tables()
    shared = {
        "even_w_in": f(inputs["even_w_in"])[0],
        "even_w_out": f(inputs["even_w_out"])[0],
        "conf_conv_w": np.ascontiguousarray(f(inputs["conf_conv_w"])[0].reshape(31, 4, 128).transpose(2, 1, 0)),
        "conf_conv_b": np.ascontiguousarray(f(inputs["conf_conv_b"]).reshape(4, 128).T),
        "conf_ln_g": np.ascontiguousarray(f(inputs["conf_ln_g"]).reshape(4, 128).T),
        "conf_ln_b": np.ascontiguousarray(f(inputs["conf_ln_b"]).reshape(4, 128).T),
        "odd_w_in": f(inputs["odd_w_in"])[0],
        "odd_conv_w": np.ascontiguousarray(f(inputs["odd_conv_w"])[0].reshape(3, 8, 128).transpose(2, 1, 0)),
        "odd_w_out": f(inputs["odd_w_out"])[0],
        "ln_mix_g": f(inputs["ln_mix_g"]),
        "ln_mix_b": f(inputs["ln_mix_b"]),
        "ln_ffn_g": f(inputs["ln_ffn_g"]),
        "ln_ffn_b": f(inputs["ln_ffn_b"]),
        "ffn_w_in": f(inputs["ffn_w_in"]),
        "ffn_w_out": f(inputs["ffn_w_out"]),
        "ple_w_proj": f(inputs["ple_w_proj"]),
        "ple_w_gate": f(inputs["ple_w_gate"]),
        "rope_c": C,
        "rope_s": SW,
    }
    x = f(inputs["x"])
    p = f(inputs["p"])
    maps = []
    for b in range(8):
        m = dict(shared)
        m["x"] = x[b]
        m["p"] = np.ascontiguousarray(p[:, b])
        maps.append(m)
    return maps


def kernel(**inputs):
    if "nc" not in _CACHE:
        _CACHE["nc"] = build()[0]
    nc = _CACHE["nc"]
    in_maps = make_in_maps(inputs)
    res = run_bass_kernel_spmd(nc, in_maps, core_ids=list(range(8)))
    out = np.stack([np.asarray(res.results[b]["out"], dtype=np.float32) for b in range(8)], axis=0)
    return out
```
